# Optimizing a Trainium2 kernel written in Bass

```python
import jax, jax.numpy as jnp
from jax import lax
import numpy as np

D_MODEL = 1024
BATCH = 8
SEQ = 2048
DEPTH = 4

GRID_W = 64
CTX_LEN = 256
N_MOD = 6
MLA_H = 4
MLA_NOPE = 64
MLA_ROPE = 32
MLA_V = 64
MLA_Q_LORA = 256
MLA_KV_LORA = 128
Q_BLOCK = 128
ROPE_THETA = 10000.0
NA_H = 4
NA_D = 64
NA_W = NA_H * NA_D
NA_ROWS = 8
NA_COLS = 16
DN_H = 4
DN_D = 128
DN_W = DN_H * DN_D
DN_CONV = 5
DN_CHUNK = 64
DN_PROJ = 4 * DN_W + 4 * DN_H
IN_W = MLA_Q_LORA + MLA_KV_LORA + MLA_ROPE + 3 * NA_W + DN_PROJ
MIX_W = MLA_H * MLA_V + NA_W + DN_W
FFN_HIDDEN = -(-8 * D_MODEL // (3 * 256)) * 256
EPS = 1e-6

kernel_name = "hybrid_mla_natten_gdn_diffusion_block"


def rms_norm(x, g):
    xf = x.astype(jnp.float32)
    y = xf * lax.rsqrt(jnp.mean(xf * xf, axis=-1, keepdims=True) + EPS)
    return (y * g.astype(jnp.float32)).astype(x.dtype)


def l2_norm(x):
    xf = x.astype(jnp.float32)
    return xf * lax.rsqrt(jnp.sum(xf * xf, axis=-1, keepdims=True) + EPS)


def modulate(h, shift, scale):
    return h * (1 + scale) + shift


def split_in(p):
    sizes = [MLA_Q_LORA, MLA_KV_LORA, MLA_ROPE, NA_W, NA_W, NA_W, DN_PROJ]
    offs = [int(o) for o in np.cumsum(sizes)[:-1]]
    return jnp.split(p, offs, axis=-1)


def axial_rope_tables(n_tok):
    t = jnp.arange(n_tok)
    pos = jnp.stack([t // GRID_W, t % GRID_W], axis=-1).astype(jnp.float32)
    n_freq = MLA_ROPE // 4
    inv = jnp.power(ROPE_THETA, -jnp.arange(n_freq, dtype=jnp.float32) / n_freq)
    ang = pos[:, :, None] * inv
    return jnp.cos(ang), jnp.sin(ang)


def apply_axial_rope(x, cos, sin):
    B, T, H, R = x.shape
    xa = x.astype(jnp.float32).reshape(B, T, H, 2, 2, R // 4)
    x1, x2 = xa[..., 0, :], xa[..., 1, :]
    c, s = cos[None, :, None], sin[None, :, None]
    y = jnp.stack([x1 * c - x2 * s, x1 * s + x2 * c], axis=-2)
    return y.reshape(B, T, H, R).astype(x.dtype)


def softmax_attention(q, k, v, scale):
    s = jnp.einsum('bqhd,bkhd->bhqk', q, k).astype(jnp.float32) * scale
    p = jax.nn.softmax(s, axis=-1).astype(v.dtype)
    return jnp.einsum('bhqk,bkhd->bqhd', p, v)


def blocked_attention(q, k, v, scale):
    B, T, H, d = q.shape
    nb = T // Q_BLOCK
    qb = q.reshape(B, nb, Q_BLOCK, H, d).transpose(1, 0, 2, 3, 4)
    out = lax.map(lambda blk: softmax_attention(blk, k, v, scale), qb)
    return out.transpose(1, 0, 2, 3, 4).reshape(B, T, H, v.shape[-1])


def mla_mixer(q_c, kv_c, k_pe, cq_c, ckv_c, ck_pe, g_q, g_kv, w_q_up, w_kv_up, cos, sin, need_ctx):
    B, T, _ = q_c.shape
    scale = (MLA_NOPE + MLA_ROPE) ** -0.5

    def q_heads(qc):
        return (rms_norm(qc, g_q) @ w_q_up).reshape(qc.shape[0], qc.shape[1], MLA_H, MLA_NOPE + MLA_ROPE)

    def kv_heads(kvc, pe):
        kv = (rms_norm(kvc, g_kv) @ w_kv_up).reshape(kvc.shape[0], kvc.shape[1], MLA_H, MLA_NOPE + MLA_V)
        pe = jnp.broadcast_to(pe, kv.shape[:3] + (MLA_ROPE,))
        return jnp.concatenate([kv[..., :MLA_NOPE], pe], axis=-1), kv[..., MLA_NOPE:]

    q = q_heads(q_c)
    q = jnp.concatenate([q[..., :MLA_NOPE], apply_axial_rope(q[..., MLA_NOPE:], cos, sin)], axis=-1)
    k, v = kv_heads(kv_c, apply_axial_rope(k_pe[:, :, None, :], cos, sin))
    ck, cv = kv_heads(ckv_c, ck_pe[:, :, None, :])
    keys = jnp.concatenate([ck, k], axis=1)
    vals = jnp.concatenate([cv, v], axis=1)
    out = blocked_attention(q, keys, vals, scale).reshape(B, T, MLA_H * MLA_V)
    if not need_ctx:
        return out, None
    cq = q_heads(cq_c)
    c_out = softmax_attention(cq, ck, cv, scale).reshape(cq.shape[0], cq.shape[1], MLA_H * MLA_V)
    return out, c_out


def na_mixer(q, k, v, cq, ck, cv, rel_bias, need_ctx):
    B, T, _ = q.shape
    rows = T // GRID_W
    kh = min(NA_ROWS, rows)
    heads = lambda t: t.reshape(t.shape[0], t.shape[1], NA_H, NA_D)
    q, k, v, cq, ck, cv = heads(q), heads(k), heads(v), heads(cq), heads(ck), heads(cv)
    grid = lambda t: t.reshape(B, rows, GRID_W, NA_H, NA_D)
    qg, kg, vg = grid(q), grid(k), grid(v)
    r = jnp.arange(rows)
    row_idx = jnp.clip(r - kh // 2, 0, rows - kh)[:, None] + jnp.arange(kh)[None, :]
    kb, vb = kg[:, row_idx], vg[:, row_idx]
    col = jnp.arange(GRID_W)
    col_start = jnp.clip(col - NA_COLS // 2, 0, GRID_W - NA_COLS)
    col_ok = (col[None, :] >= col_start[:, None]) & (col[None, :] < col_start[:, None] + NA_COLS)
    dr_i = row_idx - r[:, None] + NA_ROWS - 1
    dc_i = jnp.clip(col[None, :] - col[:, None] + NA_COLS - 1, 0, 2 * NA_COLS - 2)
    bias = rel_bias[:, dr_i[:, None, :, None], dc_i[None, :, None, :]].astype(jnp.float32)
    scale = NA_D ** -0.5
    s_loc = jnp.einsum('brqhd,brjkhd->bhrqjk', qg, kb).astype(jnp.float32) * scale + bias
    s_loc = jnp.where(col_ok[:, None, :], s_loc, -jnp.inf).reshape(B, NA_H, rows, GRID_W, kh * GRID_W)
    s_ctx = jnp.einsum('brqhd,bchd->bhrqc', qg, ck).astype(jnp.float32) * scale
    p = jax.nn.softmax(jnp.concatenate([s_loc, s_ctx], axis=-1), axis=-1).astype(v.dtype)
    p_loc = p[..., :kh * GRID_W].reshape(B, NA_H, rows, GRID_W, kh, GRID_W)
    p_ctx = p[..., kh * GRID_W:]
    o = (jnp.einsum('bhrqjk,brjkhd->brqhd', p_loc, vb)
         + jnp.einsum('bhrqc,bchd->brqhd', p_ctx, cv)).reshape(B, T, NA_W)
    if not need_ctx:
        return o, None
    c_out = softmax_attention(cq, ck, cv, scale).reshape(cq.shape[0], cq.shape[1], NA_W)
    return o, c_out


def centred_depthwise_conv(x, w):
    K = w.shape[0]
    return lax.conv_general_dilated(x, w[:, None, :], window_strides=(1,), padding=[(K // 2, K // 2)],
                                    dimension_numbers=('NWC', 'WIO', 'NWC'), feature_group_count=x.shape[-1])


def chunk_gated_delta_rule(q, k, v, g, beta, state, with_out):
    f32 = jnp.float32
    B, T, H, dk = q.shape
    dv = v.shape[-1]
    N = T // DN_CHUNK

    def blocks(t):
        t = t.astype(f32).reshape((B, N, DN_CHUNK, H) + t.shape[3:])
        return jnp.moveaxis(t, 3, 1)

    q = blocks(q) * dk ** -0.5
    k, v, g, beta = blocks(k), blocks(v), blocks(g), blocks(beta)
    gc = jnp.cumsum(g, axis=-1)
    idx = jnp.arange(DN_CHUNK)
    lower = idx[:, None] >= idx[None, :]
    strict = idx[:, None] > idx[None, :]
    decay = jnp.exp(jnp.where(lower, gc[..., :, None] - gc[..., None, :], -jnp.inf))
    kb = k * beta[..., None]
    L = jnp.where(strict, jnp.einsum('bhnid,bhnjd->bhnij', kb, k) * decay, 0.0)
    eye = jnp.eye(DN_CHUNK, dtype=f32)
    tinv = lax.linalg.triangular_solve(L + eye, jnp.broadcast_to(eye, L.shape), left_side=True,
                                       lower=True, unit_diagonal=True)
    u = tinv @ (v * beta[..., None])
    w = tinv @ (kb * jnp.exp(gc)[..., None])
    k_tail = k * jnp.exp(gc[..., -1:] - gc)[..., None]
    g_last = jnp.exp(gc[..., -1])
    mv = lambda t: jnp.moveaxis(t, 2, 0)
    xs = (mv(u), mv(w), mv(k_tail), mv(g_last))
    if with_out:
        q_dec = q * jnp.exp(gc)[..., None]
        a_intra = jnp.einsum('bhnid,bhnjd->bhnij', q, k) * decay
        xs = xs + (mv(q_dec), mv(a_intra))

    def step(S, xn):
        v_new = xn[0] - xn[1] @ S
        S_new = S * xn[3][..., None, None] + jnp.swapaxes(xn[2], -1, -2) @ v_new
        if with_out:
            return S_new, xn[4] @ S + xn[5] @ v_new
        return S_new, None

    S, o = lax.scan(step, state.astype(f32), xs)
    if with_out:
        o = o.transpose(1, 0, 3, 2, 4).reshape(B, T, H, dv)
    return o, S


def gated_deltanet_mixer(p, pc, conv_w, a_log, dt_bias, g_out, need_ctx):
    def prep(t):
        B, T, _ = t.shape
        qkv = jax.nn.silu(centred_depthwise_conv(t[..., :3 * DN_W], conv_w))
        q, k, v = [s.reshape(B, T, DN_H, DN_D) for s in jnp.split(qkv, 3, axis=-1)]
        a = t[..., 4 * DN_W:4 * DN_W + 2 * DN_H].reshape(B, T, 2, DN_H).astype(jnp.float32)
        b = t[..., 4 * DN_W + 2 * DN_H:].reshape(B, T, 2, DN_H).astype(jnp.float32)
        g = -jnp.exp(a_log.astype(jnp.float32)) * jax.nn.softplus(a + dt_bias.astype(jnp.float32))
        return l2_norm(q), l2_norm(k), v, g, jax.nn.sigmoid(b)

    def direction(seq, d):
        q, k, v, g, beta = seq
        f = (lambda t: jnp.flip(t, axis=1)) if d == 1 else (lambda t: t)
        return f(q), f(k), f(v), f(g[:, :, d]), f(beta[:, :, d])

    lat, cseq = prep(p), prep(pc)
    B = p.shape[0]
    o_lat, o_ctx = 0.0, 0.0
    for d in range(2):
        s0 = jnp.zeros((B, DN_H, DN_D, DN_D), jnp.float32)
        oc, s_ctx = chunk_gated_delta_rule(*direction(cseq, d), s0, need_ctx)
        ol, _ = chunk_gated_delta_rule(*direction(lat, d), s_ctx, True)
        o_lat = o_lat + (jnp.flip(ol, axis=1) if d == 1 else ol)
        if need_ctx:
            o_ctx = o_ctx + (jnp.flip(oc, axis=1) if d == 1 else oc)

    def out_gate(o, t):
        z = t[..., 3 * DN_W:4 * DN_W].reshape(o.shape).astype(jnp.float32)
        return (rms_norm(o, g_out) * jax.nn.silu(z)).reshape(o.shape[0], o.shape[1], DN_W).astype(t.dtype)

    return out_gate(o_lat, p), (out_gate(o_ctx, pc) if need_ctx else None)


def swiglu(h, w_gate, w_up, w_down):
    return (jax.nn.silu(h @ w_gate) * (h @ w_up)) @ w_down


def setup_inputs(seed: int = 0) -> dict:
    key = jax.random.key(seed)
    ks = jax.random.split(key, 24)
    nrm = lambda k, shape, s: jax.random.normal(k, shape, jnp.float32) * s
    gain = lambda k, shape: 1.0 + 0.1 * jax.random.normal(k, shape, jnp.float32)
    dt = jnp.exp(jax.random.uniform(ks[15], (DEPTH, 2, DN_H), jnp.float32, np.log(1e-3), np.log(1e-1)))
    return {
        "x": nrm(ks[0], (BATCH, SEQ, D_MODEL), 1.0),
        "c": nrm(ks[1], (BATCH, D_MODEL), 1.0),
        "ctx": nrm(ks[2], (BATCH, CTX_LEN, D_MODEL), 1.0),
        "c_ctx": nrm(ks[3], (D_MODEL,), 1.0),
        "w_ada": nrm(ks[4], (DEPTH, D_MODEL, N_MOD * D_MODEL), 0.5 * D_MODEL ** -0.5),
        "b_ada": nrm(ks[5], (DEPTH, N_MOD * D_MODEL), 0.01),
        "g_mix": gain(ks[6], (DEPTH, D_MODEL)),
        "w_in": nrm(ks[7], (DEPTH, D_MODEL, IN_W), D_MODEL ** -0.5),
        "mla_g_q": gain(ks[8], (DEPTH, MLA_Q_LORA)),
        "mla_g_kv": gain(ks[9], (DEPTH, MLA_KV_LORA)),
        "mla_w_q_up": nrm(ks[10], (DEPTH, MLA_Q_LORA, MLA_H * (MLA_NOPE + MLA_ROPE)), MLA_Q_LORA ** -0.5),
        "mla_w_kv_up": nrm(ks[11], (DEPTH, MLA_KV_LORA, MLA_H * (MLA_NOPE + MLA_V)), MLA_KV_LORA ** -0.5),
        "na_rel_bias": nrm(ks[12], (DEPTH, NA_H, 2 * NA_ROWS - 1, 2 * NA_COLS - 1), 0.1),
        "dn_conv_w": nrm(ks[13], (DEPTH, DN_CONV, 3 * DN_W), DN_CONV ** -0.5),
        "dn_a_log": jnp.log(jax.random.uniform(ks[14], (DEPTH, 2, DN_H), jnp.float32, 1.0, 16.0)),
        "dn_dt_bias": jnp.log(jnp.expm1(dt)),
        "dn_g_out": gain(ks[16], (DEPTH, DN_D)),
        "w_out": nrm(ks[17], (DEPTH, MIX_W, D_MODEL), MIX_W ** -0.5),
        "g_ffn": gain(ks[18], (DEPTH, D_MODEL)),
        "w_gate": nrm(ks[19], (DEPTH, D_MODEL, FFN_HIDDEN), D_MODEL ** -0.5),
        "w_up": nrm(ks[20], (DEPTH, D_MODEL, FFN_HIDDEN), D_MODEL ** -0.5),
        "w_down": nrm(ks[21], (DEPTH, FFN_HIDDEN, D_MODEL), FFN_HIDDEN ** -0.5),
        "g_final": gain(ks[22], (D_MODEL,)),
    }


def reference(x, c, ctx, c_ctx, w_ada, b_ada, g_mix, w_in, mla_g_q, mla_g_kv, mla_w_q_up, mla_w_kv_up,
              na_rel_bias, dn_conv_w, dn_a_log, dn_dt_bias, dn_g_out, w_out, g_ffn, w_gate, w_up, w_down,
              g_final):
    T = x.shape[1]
    cos, sin = axial_rope_tables(T)
    sc = jax.nn.silu(c)
    scc = jax.nn.silu(c_ctx)[None]
    for l in range(DEPTH):
        need_ctx = l < DEPTH - 1
        mx = (sc @ w_ada[l] + b_ada[l]).reshape(-1, N_MOD, 1, D_MODEL)
        mc = (scc @ w_ada[l] + b_ada[l]).reshape(1, N_MOD, 1, D_MODEL)
        h = modulate(rms_norm(x, g_mix[l]), mx[:, 0], mx[:, 1])
        hc = modulate(rms_norm(ctx, g_mix[l]), mc[:, 0], mc[:, 1])
        mq, mkv, mpe, nq, nk, nv, dn = split_in(h @ w_in[l])
        cmq, cmkv, cmpe, cnq, cnk, cnv, cdn = split_in(hc @ w_in[l])
        mla_o, mla_c = mla_mixer(mq, mkv, mpe, cmq, cmkv, cmpe, mla_g_q[l], mla_g_kv[l], mla_w_q_up[l],
                                 mla_w_kv_up[l], cos, sin, need_ctx)
        na_o, na_c = na_mixer(nq, nk, nv, cnq, cnk, cnv, na_rel_bias[l], need_ctx)
        dn_o, dn_c = gated_deltanet_mixer(dn, cdn, dn_conv_w[l], dn_a_log[l], dn_dt_bias[l], dn_g_out[l], need_ctx)
        x = x + mx[:, 2] * (jnp.concatenate([mla_o, na_o, dn_o], axis=-1) @ w_out[l])
        h = modulate(rms_norm(x, g_ffn[l]), mx[:, 3], mx[:, 4])
        x = x + mx[:, 5] * swiglu(h, w_gate[l], w_up[l], w_down[l])
        if need_ctx:
            ctx = ctx + mc[:, 2] * (jnp.concatenate([mla_c, na_c, dn_c], axis=-1) @ w_out[l])
            hc = modulate(rms_norm(ctx, g_ffn[l]), mc[:, 3], mc[:, 4])
            ctx = ctx + mc[:, 5] * swiglu(hc, w_gate[l], w_up[l], w_down[l])
    return rms_norm(x, g_final)
```

```python
import numpy as np
from contextlib import ExitStack
import concourse.bass as bass
import concourse.mybir as mybir
from concourse.bass_utils import run_bass_kernel_spmd

F32 = mybir.dt.float32
BF16 = mybir.dt.bfloat16
AF = mybir.ActivationFunctionType
ALU = mybir.AluOpType
AX = mybir.AxisListType

D = 1024
SEQ = 2048
CTXL = 256
T = SEQ + CTXL
DEPTH = 4
KC = 8
FFN = 2816
NH = FFN // 128
IN_W = 3248
EPS = 1e-6
TB = [(0, 512), (512, 512), (1024, 512), (1536, 512), (2048, 256)]
ENGS = ("pe", "act", "dve", "pool", "sp")
import os as _os
NOPFIX = int(_os.environ.get("NOPFIX", "0"))


class Buf:
    __slots__ = ("name", "w_eng", "w_dma", "r_eng", "r_dma", "dslot", "excl")

    def __init__(self, name):
        self.name = name
        self.excl = False
        self.w_eng = {}
        self.w_dma = []
        self.r_eng = {}
        self.r_dma = []
        self.dslot = None


class Rec:
    __slots__ = ("eng", "fn", "deps", "is_dma", "sembuf", "dval", "dslot", "need_inc", "ival", "semi")

    def __init__(self):
        self.need_inc = False
        self.ival = 0
        self.semi = 0
        self.dval = 0


class V:
    __slots__ = ("ap", "bufs")

    def __init__(self, ap, bufs):
        self.ap = ap
        self.bufs = bufs

    def __getitem__(self, k):
        return V(self.ap[k], self.bufs)

    def bc(self, shape):
        return V(self.ap.to_broadcast(shape), self.bufs)


def _bufs(vs):
    out = []
    for v in vs:
        if v is None or isinstance(v, (int, float)):
            continue
        out.extend(v.bufs)
    return out


def _ap(v):
    return v.ap if isinstance(v, V) else v


class Prog:
    SEM_EPOCH = 20000

    def __init__(self):
        self.recs = {e: [] for e in ENGS}
        self.dma_all = []
        self.slots = []
        self.free = []
        self.live = []

    def op(self, eng, fn, reads=(), writes=(), pwrites=(), dma_buf=None):
        r = Rec()
        r.eng = eng
        r.fn = fn
        r.is_dma = dma_buf is not None
        r.sembuf = dma_buf
        deps = {}

        def add(d, kind):
            if d is r:
                return
            if (not r.is_dma) and (not d.is_dma) and d.eng == eng:
                if eng == "pe" or kind != "raw":
                    return
            deps[id(d)] = d

        for b in reads:
            for w in b.w_eng.values():
                add(w, "raw")
            for w in b.w_dma:
                add(w, "raw")
            if b.excl:
                for x in b.r_eng.values():
                    if x.eng != eng:
                        add(x, "raw")
        for b in list(writes) + list(pwrites):
            for w in b.w_eng.values():
                add(w, "waw")
            for w in b.w_dma:
                add(w, "waw")
            for x in b.r_eng.values():
                add(x, "war")
            for x in b.r_dma:
                add(x, "war")
        r.deps = list(deps.values())
        for b in reads:
            if r.is_dma:
                b.r_dma.append(r)
            else:
                b.r_eng[eng] = r
        for b in writes:
            b.w_eng = {}
            b.w_dma = []
            b.r_eng = {}
            b.r_dma = []
        for b in list(writes) + list(pwrites):
            if r.is_dma:
                b.w_dma.append(r)
            else:
                b.w_eng[eng] = r
            b.r_eng = {}
            b.r_dma = []
        if r.is_dma:
            if dma_buf.dslot is None:
                if self.free:
                    dma_buf.dslot = self.free.pop()
                else:
                    self.slots.append(0)
                    dma_buf.dslot = len(self.slots) - 1
                self.live.append(dma_buf)
            self.slots[dma_buf.dslot] += 16
            r.dslot = dma_buf.dslot
            r.dval = self.slots[dma_buf.dslot]
            self.dma_all.append(r)
        self.recs[eng].append(r)
        return r

    def barrier(self):
        last = []
        for e in ENGS:
            for r in reversed(self.recs[e]):
                if not r.is_dma and r.fn is not None:
                    last.append(r)
                    break
        pend = list(self.dma_all)
        self.dma_all = []
        for b in self.live:
            if self.slots[b.dslot] < 40000:
                self.free.append(b.dslot)
            b.dslot = None
        self.live = []
        for e in ENGS:
            r = Rec()
            r.eng = e
            r.fn = None
            r.is_dma = False
            r.sembuf = None
            r.deps = [d for d in last if d.eng != e] + pend
            self.recs[e].append(r)

    def emit(self, nc, es, final_waits):
        for e in ENGS:
            for r in self.recs[e]:
                for d in r.deps:
                    if not d.is_dma:
                        d.need_inc = True
        nsem = {}
        for e in ENGS:
            c = 0
            si = 0
            for r in self.recs[e]:
                if r.need_inc:
                    c += 1
                    if c > self.SEM_EPOCH:
                        si += 1
                        c = 1
                    r.ival = c
                    r.semi = si
            nsem[e] = si + 1
        sems = {e: [es.enter_context(nc.semaphore("s_%s%d" % (e, i))) for i in range(nsem[e])] for e in ENGS}
        dsems = [es.enter_context(nc.semaphore("d_%d" % i)) for i in range(len(self.slots))]
        recs = self.recs

        def run(e, eng):
            seen = {}
            for r in recs[e]:
                need = {}
                for d in r.deps:
                    if d.is_dma:
                        key = ("d", d.dslot)
                        sem = dsems[d.dslot]
                        val = d.dval
                    else:
                        key = (d.eng, d.semi)
                        sem = sems[d.eng][d.semi]
                        val = d.ival
                    if seen.get(key, 0) >= val:
                        continue
                    if key not in need or need[key][1] < val:
                        need[key] = (sem, val)
                for key, (sem, val) in need.items():
                    eng.wait_ge(sem, val)
                    seen[key] = val
                if NOPFIX and e == "pe" and len(need) >= 2:
                    eng.nop()
                if r.fn is None:
                    continue
                ins = r.fn(eng)
                if r.is_dma:
                    ins.then_inc(dsems[r.dslot], 16)
                elif r.need_inc:
                    ins.then_inc(sems[e][r.semi], 1)
            if e == "sp":
                for d in final_waits:
                    eng.wait_ge(dsems[d.dslot], d.dval)

        block = es.enter_context(nc.Block())

        @block.sync
        def _(eng):
            run("sp", eng)

        @block.tensor
        def _(eng):
            run("pe", eng)

        @block.scalar
        def _(eng):
            run("act", eng)

        @block.vector
        def _(eng):
            run("dve", eng)

        @block.gpsimd
        def _(eng):
            run("pool", eng)


class Builder:
    def __init__(self, nc, es, n_layers=DEPTH, mixers=True, dbg=None):
        self.nc = nc
        self.es = es
        self.P = Prog()
        self.L = n_layers
        self.mixers = mixers
        self.dbg = dbg
        self.nbuf = 0

    def buf(self, name="b"):
        self.nbuf += 1
        return Buf("%s%d" % (name, self.nbuf))

    def sb(self, name, shape, dt):
        t = self.es.enter_context(self.nc.sbuf_tensor("sb_" + name, list(shape), dt))
        return t

    def sbv(self, name, shape, dt):
        t = self.sb(name, shape, dt)
        return V(t[:], [self.buf(name)])

    def dram_in(self, name, shape, dt=F32):
        return self.nc.dram_tensor(name, list(shape), dt, kind="ExternalInput").ap()

    def mm(self, out, lhsT, rhs, start=True, stop=True, extra_reads=()):
        o, l, r = out.ap, lhsT.ap, rhs.ap
        self.P.op("pe", lambda e: e.matmul(o, lhsT=l, rhs=r, start=start, stop=stop),
                  reads=_bufs([lhsT, rhs]) + list(extra_reads), writes=() if not start else (), pwrites=out.bufs)

    def tr(self, out, in_, ident):
        o, i, d = out.ap, in_.ap, ident.ap
        self.P.op("pe", lambda e: e.transpose(o, i, d), reads=_bufs([in_, ident]), pwrites=out.bufs)

    def act(self, out, in_, func, bias=None, scale=None, accum_out=None, eng="act"):
        o, i = out.ap, in_.ap
        kw = {}
        if bias is not None:
            kw["bias"] = _ap(bias)
        if scale is not None:
            kw["scale"] = _ap(scale)
        if accum_out is not None:
            kw["accum_out"] = accum_out.ap
        self.P.op("act", lambda e: e.activation(o, i, func, **kw),
                  reads=_bufs([in_, bias, scale]), pwrites=_bufs([out, accum_out]))

    def tt(self, out, in0, in1, op, eng="dve"):
        o, a, b = out.ap, in0.ap, in1.ap
        self.P.op(eng, lambda e: e.tensor_tensor(o, a, b, op), reads=_bufs([in0, in1]), pwrites=out.bufs)

    def ts(self, out, in0, s1, op0, s2=None, op1=None, eng="dve"):
        o, a = out.ap, in0.ap
        s1a, s2a = _ap(s1), _ap(s2)
        if op1 is None:
            fn = lambda e: e.tensor_scalar(o, a, s1a, None, op0)
        else:
            fn = lambda e: e.tensor_scalar(o, a, s1a, s2a, op0, op1)
        self.P.op(eng, fn, reads=_bufs([in0, s1, s2]), pwrites=out.bufs)

    def stt(self, out, in0, scalar, in1, op0, op1, eng="dve"):
        o, a, b = out.ap, in0.ap, in1.ap
        s = _ap(scalar)
        self.P.op(eng, lambda e: e.scalar_tensor_tensor(o, a, s, b, op0, op1),
                  reads=_bufs([in0, scalar, in1]), pwrites=out.bufs)

    def copy(self, out, in_, eng="dve"):
        o, i = out.ap, in_.ap
        if eng == "act":
            self.P.op("act", lambda e: e.copy(o, i), reads=in_.bufs, pwrites=out.bufs)
        else:
            self.P.op(eng, lambda e: e.tensor_copy(o, i), reads=in_.bufs, pwrites=out.bufs)

    def recip(self, out, in_):
        o, i = out.ap, in_.ap
        self.P.op("dve", lambda e: e.reciprocal(o, i), reads=in_.bufs, pwrites=out.bufs)

    def memset(self, out, val, eng="dve"):
        o = out.ap
        self.P.op(eng, lambda e: e.memset(o, val), reads=(), pwrites=out.bufs)

    def dma(self, out, in_, eng="sp", sembuf=None):
        o, i = _ap(out), _ap(in_)
        rb = in_.bufs if isinstance(in_, V) else []
        wb = out.bufs if isinstance(out, V) else []
        if sembuf is None:
            sembuf = (wb or rb)[0]
        return self.P.op(eng, lambda e: e.dma_start(out=o, in_=i), reads=rb, pwrites=wb, dma_buf=sembuf)

    def build(self):
        nc, P, L = self.nc, self.P, self.L
        x_d = self.dram_in("x", [SEQ, D])
        ctx_d = self.dram_in("ctx", [CTXL, D])
        cc_d = self.dram_in("cc", [128, KC, 2])
        wada_d = self.dram_in("w_ada", [DEPTH, D, 6 * D])
        bada_d = self.dram_in("b_adaT", [128, DEPTH, 48])
        gv_d = self.dram_in("gvecs", [128, DEPTH, 2, KC])
        gfin_d = self.dram_in("g_finalT", [128, KC])
        wg_d = self.dram_in("w_gate", [DEPTH, D, FFN])
        wu_d = self.dram_in("w_up", [DEPTH, D, FFN])
        wd_d = self.dram_in("w_down", [DEPTH, FFN, D])
        ident_d = self.dram_in("ident", [128, 128])
        if self.mixers:
            self.win_d = self.dram_in("w_in", [DEPTH, D, IN_W])
            self.wpe_d = self.dram_in("w_pe2", [DEPTH, D, 2, 96])
            self.wq2_d = self.dram_in("w_q2", [DEPTH, 256, 2, 384])
            self.wkv_d = self.dram_in("w_kv_nv", [DEPTH, 128, 512])
            mlag_d = self.dram_in("mla_g", [128, DEPTH, 3])
            self.rope_d = self.dram_in("ropeCS", [128, 2, SEQ])
            sel_d = self.dram_in("sel65", [128, 64])
            self.nab_d = self.dram_in("nab", [DEPTH, 128, 4, 19, 64])
            self.wout_d = self.dram_in("w_out", [DEPTH, D, D])
            gdnc_d = self.dram_in("gdnc", [128, 832])
            gabc_d = self.dram_in("gdn_ab", [128, DEPTH, 16])
            gcw_d = self.dram_in("gdn_cw", [128, DEPTH, 3, 4, 5])
            ggo_d = self.dram_in("gdn_go", [128, DEPTH])
        out_d = nc.dram_tensor("out", [SEQ, D], F32, kind="ExternalOutput").ap()
        self.out_d = out_d
        if self.dbg:
            self.dbg_d = nc.dram_tensor("dbg", list(self.dbg), F32, kind="ExternalOutput").ap()

        POOLW = 40960
        XW = KC * T
        pool_t = self.sb("pool", [128, POOLW], F32)
        self.pool_t, self.POOLW, self.XW = pool_t, POOLW, XW
        xT_t = pool_t[:, 0:XW].rearrange("p (k t) -> p k t", k=KC)
        hT_t = self.sb("hT", [128, KC, T], BF16)
        self.xT_t, self.hT_t = xT_t, hT_t
        xb = [[self.buf("x") for _ in TB] for _ in range(KC)]
        hb = [[self.buf("h") for _ in TB] for _ in range(KC)]

        def xT(kc, tb):
            o, n = TB[tb]
            return V(xT_t[:, kc, o:o + n], [xb[kc][tb]])

        def hT(kc, tb):
            o, n = TB[tb]
            return V(hT_t[:, kc, o:o + n], [hb[kc][tb]])

        self.xT, self.hT = xT, hT
        modT = self.sbv("modT", [128, DEPTH, 6, KC, 2], F32)
        gsc = self.sbv("gsc", [128, DEPTH, 2, KC, 2], F32)
        bT = self.sbv("bT", [128, DEPTH, 48], F32)
        gv = self.sbv("gv", [128, DEPTH, 2, KC], F32)
        gfin = self.sbv("gfin", [128, KC], F32)
        ident = self.sbv("ident", [128, 128], F32)
        ones_bf = self.sbv("ones_bf", [128, 128], BF16)
        cc = self.sbv("cc", [128, KC, 2], F32)
        scc = self.sbv("scc", [128, KC, 2], F32)
        self.modT, self.gsc, self.ident, self.ones_bf = modT, gsc, ident, ones_bf
        self.xb, self.hb = xb, hb
        if self.mixers:
            self.mlag = self.sbv("mlag", [128, DEPTH, 3], F32)
            self.sel65 = self.sbv("sel65", [128, 64], F32)
            self.dma(self.mlag, mlag_d)
            self.gdnc = self.sbv("gdnc", [128, 832], F32)
            self.gabc = self.sbv("gabc", [128, DEPTH, 16], F32)
            self.gcw = self.sbv("gcw", [128, DEPTH, 3, 4, 5], F32)
            self.ggo = self.sbv("ggo", [128, DEPTH], F32)
            self.one_t = self.sbv("one_t", [128, 1], F32)
            self.ident_bf = self.sbv("ident_bf", [128, 128], BF16)
            self.dma(self.gdnc, gdnc_d)
            self.dma(self.gabc, gabc_d)
            self.dma(self.gcw, gcw_d)
            self.dma(self.ggo, ggo_d)
            self.memset(self.one_t, 1.0)
            self.dma(self.sel65, sel_d)
        self.scr_base = XW
        self.scr_lim = POOLW
        self.ps = [self.es.enter_context(nc.psum_tensor("ps%d" % i, [128, 512], F32)) for i in range(6)]
        self.psD_t = self.es.enter_context(nc.psum_tensor("psD", [128, 1024], F32))
        self.psb = [self.buf("ps") for _ in range(8)]
        for b_ in self.psb:
            b_.excl = True

        def PS(i, n=512, p=128):
            if i >= 6:
                return V(self.psD_t[0:p, (i - 6) * 512:(i - 6) * 512 + n], [self.psb[i]])
            return V(self.ps[i][0:p, 0:n], [self.psb[i]])

        self.PS = PS

        self.dma(ident, ident_d)
        self.dma(cc, cc_d)
        self.dma(bT, bada_d)
        self.dma(gv, gv_d)
        self.dma(gfin, gfin_d)
        self.memset(ones_bf, 1.0)
        self.act(scc, cc, AF.Silu)
        if self.mixers:
            self.copy(self.ident_bf, ident)

        scr_off = [0]

        def carve(n_f32, dt=F32, shape=None, name="c"):
            a = self.scr_base + scr_off[0]
            scr_off[0] += n_f32
            assert a + n_f32 <= self.scr_lim, (name, a + n_f32)
            ap = pool_t[:, a:a + n_f32]
            if dt != F32:
                ap = ap.bitcast(dt)
            if shape is not None:
                names = " ".join("d%d" % i for i in range(len(shape)))
                kw = {"d%d" % i: s for i, s in enumerate(shape[:-1])}
                ap = ap.rearrange("p (%s) -> p %s" % (names, names), **kw)
            return V(ap, [self.buf(name)])

        self.carve = carve
        self.scr_off = scr_off
        wa = [carve(KC * 512, F32, [KC, 512], "wa") for _ in range(2)]
        n = 0
        for l in range(L):
            pst = PS(l % 2, 96)
            for cb in range(12):
                w = wa[n % 2]
                n += 1
                self.dma(w, wada_d[l, :, cb * 512:(cb + 1) * 512].rearrange("(k p) c -> p k c", p=128),
                         eng="sp" if n % 2 else "act")
                for jj in range(4):
                    j = cb * 4 + jj
                    for k in range(KC):
                        self.mm(pst[:, 2 * j:2 * j + 2], w[:, k, jj * 128:(jj + 1) * 128], scc[:, k, :],
                                start=(k == 0), stop=(k == KC - 1))
            o = modT.ap[:, l].rearrange("p s k w -> p (s k) w")
            i0 = pst.ap.rearrange("p (j w) -> p j w", w=2)
            i1 = bT.ap[:, l, :].unsqueeze(2).to_broadcast([128, 48, 2])
            self.tt(V(o, modT.bufs), V(i0, pst.bufs), V(i1, bT.bufs), ALU.add)
            for which, (si, gi) in enumerate(((1, 0), (4, 1))):
                g_b = gv.ap[:, l, gi, :].unsqueeze(2).to_broadcast([128, KC, 2])
                self.stt(V(gsc.ap[:, l, which], gsc.bufs), V(modT.ap[:, l, si], modT.bufs), 1.0,
                         V(g_b, gv.bufs), ALU.add, ALU.mult)
        P.barrier()
        scr_off[0] = 0

        stg = [carve(4 * D, F32, [4, D], "stg") for _ in range(2)]
        n = 0
        for tb, (o, nt) in enumerate(TB):
            s = stg[tb % 2]
            ntile = nt // 128
            if tb < 4:
                src = x_d[o:o + nt, :]
            else:
                src = ctx_d[:, :]
            self.dma(s[:, 0:ntile, :], src.rearrange("(t p) d -> p t d", p=128), eng="sp")
            for kc in range(KC):
                pb = PS(n % 8, nt)
                n += 1
                for t in range(ntile):
                    self.tr(pb[:, t * 128:(t + 1) * 128], s[:, t, kc * 128:(kc + 1) * 128], ident)
                self.copy(xT(kc, tb), pb, eng="act" if kc % 2 else "dve")
        P.barrier()
        scr_off[0] = 0

        for l in range(L):
            self.norm_mod(l, 0)
            if self.mixers:
                self.mixer_phase(l)
            self.norm_mod(l, 1)
            self.ffn_phase(l, wg_d, wu_d, wd_d)
        self.final_phase(gfin)
        P.emit(nc, self.es, self.final_waits)

    def rstd_block(self, tb, sqs, rs_tmp, rstd):
        o, nt = TB[tb]
        ps = self.PS(tb % 2, nt)
        for kc in range(KC):
            sq = sqs[kc % 2]
            self.act(sq[:, 0:nt], self.xT(kc, tb), AF.Square)
            self.mm(ps, self.ones_bf, sq[:, 0:nt], start=(kc == 0), stop=(kc == KC - 1))
        self.act(rs_tmp[:, 0:nt], ps, AF.Sqrt, bias=self.eps_t, scale=1.0 / D)
        self.recip(rstd[:, 0:nt], rs_tmp[:, 0:nt])

    def norm_mod(self, l, which):
        P = self.P
        self.scr_off[0] = 0
        if not hasattr(self, "eps_t"):
            self.eps_t = self.sbv("eps_t", [128, 1], F32)
            self.memset(self.eps_t, EPS)
        sqs = [self.carve(256, BF16, None, "sq") for _ in range(2)]
        rs_tmp = self.carve(512, F32, None, "rs")
        rstds = [self.carve(512, F32, None, "rstd") for _ in range(2)]
        tmps = [self.carve(512, F32, None, "nt") for _ in range(2)]
        shift_i = 0 if which == 0 else 3
        n = 0
        for tb, (o, nt) in enumerate(TB):
            rstd = rstds[tb % 2]
            self.rstd_block(tb, sqs, rs_tmp, rstd)
            w = 0 if tb < 4 else 1
            for kc in range(KC):
                tmp = tmps[n % 2]
                n += 1
                self.stt(tmp[:, 0:nt], self.xT(kc, tb), V(self.gsc.ap[:, l, which, kc, w:w + 1], self.gsc.bufs),
                         rstd[:, 0:nt], ALU.mult, ALU.mult)
                self.act(self.hT(kc, tb), tmp[:, 0:nt], AF.Identity,
                         bias=V(self.modT.ap[:, l, shift_i, kc, w:w + 1], self.modT.bufs), scale=1.0)
        P.barrier()
        self.scr_off[0] = 0

    def ffn_phase(self, l, wg_d, wu_d, wd_d):
        P = self.P
        self.scr_off[0] = 0
        HC = NH // 2
        hid = self.sb_scr_hid()
        wgs = [self.carve(KC * 64, BF16, [KC, 128], "wg") for _ in range(3)]
        wus = [self.carve(KC * 64, BF16, [KC, 128], "wu") for _ in range(3)]
        wds = [self.carve(HC * 64, BF16, [HC, 128], "wd") for _ in range(2)]
        sil = [self.carve(512, F32, None, "sil") for _ in range(2)]
        n = 0
        nd = 0
        npb = 0
        for half in range(2):
            for jj in range(HC):
                j = half * HC + jj
                wg, wu = wgs[n % 3], wus[n % 3]
                n += 1
                self.dma(wg, wg_d[l, :, j * 128:(j + 1) * 128].rearrange("(k p) c -> p k c", p=128), eng="pool")
                self.dma(wu, wu_d[l, :, j * 128:(j + 1) * 128].rearrange("(k p) c -> p k c", p=128), eng="pool")
                for tb, (o, nt) in enumerate(TB):
                    pg = self.PS(npb % 4, nt)
                    pu = self.PS(4 + npb % 4, nt)
                    npb += 1
                    for kc in range(KC):
                        self.mm(pg, wg[:, kc, :], self.hT(kc, tb), start=(kc == 0), stop=(kc == KC - 1))
                    for kc in range(KC):
                        self.mm(pu, wu[:, kc, :], self.hT(kc, tb), start=(kc == 0), stop=(kc == KC - 1))
                    s = sil[npb % 2]
                    self.act(s[:, 0:nt], pg, AF.Silu)
                    self.tt(V(hid.ap[:, jj, o:o + nt], [self.hidb[jj][tb]]), s[:, 0:nt], pu, ALU.mult)
            for m in range(KC):
                wd = wds[nd % 2]
                nd += 1
                self.dma(wd, wd_d[l, half * HC * 128:(half + 1) * HC * 128, m * 128:(m + 1) * 128]
                         .rearrange("(j p) c -> p j c", p=128), eng="pool")
                for tb, (o, nt) in enumerate(TB):
                    po = self.PS(npb % 8, nt)
                    npb += 1
                    for jj in range(HC):
                        self.mm(po, wd[:, jj, :], V(hid.ap[:, jj, o:o + nt], [self.hidb[jj][tb]]),
                                start=(jj == 0), stop=(jj == HC - 1))
                    w = 0 if tb < 4 else 1
                    self.stt(self.xT(m, tb), po, V(self.modT.ap[:, l, 5, m, w:w + 1], self.modT.bufs),
                             self.xT(m, tb), ALU.mult, ALU.add)
        P.barrier()
        self.scr_off[0] = 0

    def sb_scr_hid(self):
        HC = NH // 2
        hid = self.carve(HC * T // 2, BF16, [HC, T], "hid")
        self.hidb = [[self.buf("hid") for _ in TB] for _ in range(HC)]
        return hid

    def final_phase(self, gfin):
        P = self.P
        self.scr_off[0] = 0
        sqs = [self.carve(256, BF16, None, "sq") for _ in range(2)]
        rs_tmp = self.carve(512, F32, None, "rs")
        rstds = [self.carve(512, F32, None, "rstd") for _ in range(2)]
        ys = [self.carve(512, F32, None, "y") for _ in range(3)]
        outs = [self.carve(4 * D, F32, [4, D], "os") for _ in range(2)]
        self.final_waits = []
        n = 0
        for tb in range(4):
            o, nt = TB[tb]
            rstd = rstds[tb % 2]
            self.rstd_block(tb, sqs, rs_tmp, rstd)
            ost = outs[tb % 2]
            for kc in range(KC):
                y = ys[n % 3]
                self.stt(y, self.xT(kc, tb), gfin[:, kc:kc + 1], rstd, ALU.mult, ALU.mult)
                pb = self.PS(2 + n % 6, 512)
                n += 1
                for t in range(4):
                    self.tr(pb[:, t * 128:(t + 1) * 128], y[:, t * 128:(t + 1) * 128], self.ident)
                dst = V(ost.ap[:, :, kc * 128:(kc + 1) * 128], ost.bufs)
                srcv = V(pb.ap.rearrange("p (t c) -> p t c", c=128), pb.bufs)
                self.copy(dst, srcv, eng="act" if kc % 2 else "dve")
            r = self.dma(self.out_d[o:o + nt, :].rearrange("(t p) d -> p t d", p=128), ost, eng="sp")
            self.final_waits.append(r)

    def load_w(self, dst, src_ap, eng="pool"):
        return self.dma(dst, src_ap, eng=eng)

    def mixer_phase(self, l):
        P = self.P
        nc = self.nc
        if not hasattr(self, "xsp_d"):
            self.xsp_d = nc.dram_tensor("xspill", [128, KC * T], F32).ap()
            self.xsp_b = self.buf("xsp")
        xall = V(self.pool_t[:, 0:self.XW], [b for row in self.xb for b in row])
        self.dma(V(self.xsp_d, [self.xsp_b]), xall, eng="sp", sembuf=self.xsp_b)
        P.barrier()
        self.scr_base = 0
        self.scr_lim = self.POOLW
        self.scr_off[0] = 0
        if not hasattr(self, "mix_d"):
            self.mix_d = nc.dram_tensor("mixspill", [128, KC, T], BF16).ap()
        self.mixb = [self.buf("mix") for _ in range(KC)]
        self.mla(l)
        P.barrier()
        self.scr_off[0] = 0
        self.na(l)
        P.barrier()
        self.scr_off[0] = 0
        self.gdn(l)
        P.barrier()
        self.dma(xall, V(self.xsp_d, [self.xsp_b]), eng="sp", sembuf=self.xsp_b)
        self.scr_base = self.XW
        self.scr_off[0] = 0
        mixs = self.carve(KC * T // 2, BF16, [KC, T], "mixs")
        mix_t = mixs.ap
        self.mix_t = mix_t
        for k in range(KC):
            self.dma(V(mix_t[:, k, :], [mixs.bufs[0]]), V(self.mix_d[:, k, :], [self.mixb[k]]), eng="sp" if k % 2 else "act",
                     sembuf=mixs.bufs[0])
        self.mixb = [mixs.bufs[0]] * KC
        if self.dbg and l == 0:
            self.dump_mix()
        wos = [self.carve(KC * 64, BF16, [KC, 128], "wo") for _ in range(2)]
        npb = 0
        for m in range(KC):
            wo = wos[m % 2]
            self.load_w(wo, self.wout_d[l, :, m * 128:(m + 1) * 128].rearrange("(k p) c -> p k c", p=128))
            for tb, (o, nt) in enumerate(TB):
                po = self.PS(npb % 8, nt)
                npb += 1
                for kk in range(KC):
                    self.mm(po, wo[:, kk, :], V(mix_t[:, kk, o:o + nt], [self.mixb[kk]]),
                            start=(kk == 0), stop=(kk == KC - 1))
                w = 0 if tb < 4 else 1
                self.stt(self.xT(m, tb), po, V(self.modT.ap[:, l, 2, m, w:w + 1], self.modT.bufs),
                         self.xT(m, tb), ALU.mult, ALU.add)
        P.barrier()
        self.scr_off[0] = 0
        self.scr_lim = self.POOLW

    def dump_mix(self):
        st = [self.carve(T, F32, None, "dst") for _ in range(2)]
        for k in range(KC):
            s_ = st[k % 2]
            self.copy(s_, V(self.mix_t[:, k, :], [self.mixb[k]]), eng="act")
            self.dma(self.dbg_d[k * 128:(k + 1) * 128, :], s_, eng="sp")

    def finish_attn(self, O_ps, n, pb, chunk, tok0, tmp):
        osb, rden, obuf = tmp
        self.copy(osb[0:65, 0:n], O_ps[0:65, 0:n], eng="act")
        den = self.PS(5, n, 64)
        self.mm(den, self.sel65[0:65, :], osb[0:65, 0:n])
        self.recip(rden[0:64, 0:n], den)
        self.tt(obuf[0:64, 0:n], osb[0:64, 0:n], rden[0:64, 0:n], ALU.mult)
        dst = V(self.mix_d[pb:pb + 64, chunk, tok0:tok0 + n], [self.mixb[chunk]])
        self.dma(dst, obuf[0:64, 0:n], eng="sp", sembuf=obuf.bufs[0])

    def attn_dense(self, QT, KT_fn, V_fn, key_tiles, n, scale, dst, st):
        pts, fin_tmp = st["pts"], st["fin"]
        O_ps = self.PS(3 + st["nO"] % 2, n, 65)
        st["nO"] += 1
        nk = len(key_tiles)
        LA = 2
        pend = []
        for i in range(nk + LA):
            if i < nk:
                t = key_tiles[i]
                S_ps = self.PS(st["nS"] % 3, n)
                pt = pts[st["nS"] % 3]
                st["nS"] += 1
                self.mm(S_ps, KT_fn(t), QT)
                pend.append((i, t, S_ps, pt))
            if i >= LA:
                j, t, S_ps, pt = pend.pop(0)
                self.act(pt[:, 0:n], S_ps, AF.Exp, scale=scale)
                self.mm(O_ps, V_fn(t), pt[:, 0:n], start=(j == 0), stop=(j == nk - 1))
        self.finish_attn(O_ps, n, dst[0], dst[1], dst[2], fin_tmp[st["nO"] % 2])

    def attn_state(self):
        pts = [self.carve(256, BF16, None, "pt") for _ in range(3)]
        fin = [(self.carve(512, F32, None, "osb"), self.carve(512, F32, None, "rden"),
                self.carve(256, BF16, None, "obuf")) for _ in range(2)]
        return {"pts": pts, "fin": fin, "nO": 0, "nS": 0}

    def mla(self, l):
        hT = self.hT
        SC = 96 ** -0.5
        w_in3 = [self.carve(KC * 64, BF16, [KC, 128], "wmi") for _ in range(3)]
        for c in range(3):
            self.load_w(w_in3[c], self.win_d[l, :, c * 128:(c + 1) * 128].rearrange("(k p) c -> p k c", p=128))
        wpe = self.carve(KC * 96, BF16, [KC, 2, 96], "wpe")
        self.load_w(wpe, self.wpe_d[l].rearrange("(k p) a c -> p k a c", p=128))
        wq2 = self.carve(2 * 384, BF16, [2, 2, 384], "wq2")
        self.load_w(wq2, self.wq2_d[l].rearrange("(k p) a c -> p k a c", p=128))
        wkv = self.carve(256, BF16, None, "wkv")
        self.load_w(wkv, self.wkv_d[l])
        rope = self.carve(2 * SEQ, F32, [2, SEQ], "rope")
        self.dma(rope, self.rope_d, eng="sp")
        mqn = self.carve(T, BF16, [2, T], "mqn")
        mkvn = self.carve(T // 2, BF16, None, "mkvn")
        peR = self.carve(T // 2, BF16, None, "peR")
        vaug = self.carve(18 * 4 * 65 // 2, BF16, [18, 4, 65], "vaug")
        self.memset(V(vaug.ap[:, :, :, 64:65], vaug.bufs), 1.0, eng="pool")
        raw = [self.carve(512, F32, None, "raw") for _ in range(3)]
        sqs = [self.carve(256, BF16, None, "sq") for _ in range(3)]
        rt = [self.carve(512, F32, None, "rt") for _ in range(4)]
        r1 = self.carve(512, F32, None, "r1")
        r2 = self.carve(512, F32, None, "r2")
        for tb, (o, nt) in enumerate(TB):
            pr = [self.PS(6, nt), self.PS(7, nt), self.PS(5, nt)]
            for c in range(3):
                for kc in range(KC):
                    self.mm(pr[c], w_in3[c][:, kc, :], hT(kc, tb), start=(kc == 0), stop=(kc == KC - 1))
                self.copy(raw[c][:, 0:nt], pr[c], eng="act" if c % 2 else "dve")
                self.act(sqs[c][:, 0:nt], raw[c][:, 0:nt], AF.Square)
            ps_q = self.PS(3, nt)
            self.mm(ps_q, self.ones_bf, sqs[0][:, 0:nt], start=True, stop=False)
            self.mm(ps_q, self.ones_bf, sqs[1][:, 0:nt], start=False, stop=True)
            ps_k = self.PS(4, nt)
            self.mm(ps_k, self.ones_bf, sqs[2][:, 0:nt])
            self.act(rt[0][:, 0:nt], ps_q, AF.Sqrt, bias=self.eps_t, scale=1.0 / 256)
            self.recip(rt[1][:, 0:nt], rt[0][:, 0:nt])
            self.act(rt[2][:, 0:nt], ps_k, AF.Sqrt, bias=self.eps_t, scale=1.0 / 128)
            self.recip(rt[3][:, 0:nt], rt[2][:, 0:nt])
            for c in range(2):
                self.stt(V(mqn.ap[:, c, o:o + nt], mqn.bufs), raw[c][:, 0:nt],
                         V(self.mlag.ap[:, l, c:c + 1], self.mlag.bufs), rt[1][:, 0:nt], ALU.mult, ALU.mult)
            self.stt(mkvn[:, o:o + nt], raw[2][:, 0:nt], V(self.mlag.ap[:, l, 2:3], self.mlag.bufs),
                     rt[3][:, 0:nt], ALU.mult, ALU.mult)
            pp = [self.PS(0, nt, 96), self.PS(1, nt, 96)]
            for a in range(2):
                for kc in range(KC):
                    self.mm(pp[a], wpe[:, kc, a, :], hT(kc, tb), start=(kc == 0), stop=(kc == KC - 1))
            if tb < 4:
                self.tt(r1[64:96, 0:nt], pp[0][64:96, :], rope[64:96, 0, o:o + nt], ALU.mult)
                self.tt(r2[64:96, 0:nt], pp[1][64:96, :], rope[64:96, 1, o:o + nt], ALU.mult)
                self.tt(peR[64:96, o:o + nt], r1[64:96, 0:nt], r2[64:96, 0:nt], ALU.add)
            else:
                self.copy(peR[64:96, o:o + nt], pp[0][64:96, :], eng="act")
        for t in range(18):
            pv = self.PS(6 + t % 2, 256)
            self.mm(pv, mkvn[:, t * 128:(t + 1) * 128], wkv[:, 256:512])
            self.copy(V(vaug.ap[:, t, :, 0:64], vaug.bufs), V(pv.ap.rearrange("p (h d) -> p h d", h=4), pv.bufs),
                      eng="act" if t % 2 else "dve")
        QTs = [self.carve(T // 2, BF16, None, "QT") for _ in range(2)]
        KTs = [self.carve(T // 2, BF16, None, "KT") for _ in range(2)]
        st = self.attn_state()
        for h in range(4):
            QT, KT = QTs[h % 2], KTs[h % 2]
            for tb, (o, nt) in enumerate(TB):
                pq = [self.PS(6, nt, 96), self.PS(7, nt, 96)]
                na_ = 2 if tb < 4 else 1
                for a in range(na_):
                    for c in range(2):
                        self.mm(pq[a], wq2[:, c, a, h * 96:(h + 1) * 96], V(mqn.ap[:, c, o:o + nt], mqn.bufs),
                                start=(c == 0), stop=(c == 1))
                if tb < 4:
                    self.copy(QT[0:64, o:o + nt], pq[0][0:64, :], eng="act")
                    self.tt(r1[64:96, 0:nt], pq[0][64:96, :], rope[64:96, 0, o:o + nt], ALU.mult)
                    self.tt(r2[64:96, 0:nt], pq[1][64:96, :], rope[64:96, 1, o:o + nt], ALU.mult)
                    self.tt(QT[64:96, o:o + nt], r1[64:96, 0:nt], r2[64:96, 0:nt], ALU.add)
                else:
                    self.copy(QT[0:96, o:o + nt], pq[0][0:96, :], eng="act")
                pk = self.PS(5, nt, 64)
                self.mm(pk, wkv[:, h * 64:(h + 1) * 64], mkvn[:, o:o + nt])
                self.copy(KT[0:64, o:o + nt], pk, eng="dve")
            self.copy(KT[64:96, :], peR[64:96, :], eng="pool")
            KT_fn = lambda t, KT=KT: KT[0:96, t * 128:(t + 1) * 128]
            V_fn = lambda t, h=h: V(vaug.ap[:, t, h, :], vaug.bufs)
            for qb in range(4):
                self.attn_dense(QT[0:96, qb * 512:(qb + 1) * 512], KT_fn, V_fn, list(range(18)), 512, SC,
                                ((h % 2) * 64, h // 2, qb * 512), st)
            self.attn_dense(QT[0:96, SEQ:T], KT_fn, V_fn, [16, 17], 256, SC, ((h % 2) * 64, h // 2, SEQ), st)

    def na(self, l):
        hT = self.hT
        SC = 64 ** -0.5
        nqk = self.carve(2 * T, BF16, [2, 2, T], "nqk")
        vaug = self.carve(18 * 4 * 65 // 2, BF16, [18, 4, 65], "vaugn")
        self.memset(V(vaug.ap[:, :, :, 64:65], vaug.bufs), 1.0, eng="pool")
        ws = [self.carve(KC * 64, BF16, [KC, 128], "wn") for _ in range(2)]
        n = 0
        for qk in range(2):
            for c in range(2):
                w = ws[n % 2]
                n += 1
                col = 416 + qk * 256 + c * 128
                self.load_w(w, self.win_d[l, :, col:col + 128].rearrange("(k p) c -> p k c", p=128))
                for tb, (o, nt) in enumerate(TB):
                    pp = self.PS(6 + tb % 2, nt)
                    for kc in range(KC):
                        self.mm(pp, w[:, kc, :], hT(kc, tb), start=(kc == 0), stop=(kc == KC - 1))
                    self.copy(V(nqk.ap[:, qk, c, o:o + nt], nqk.bufs), pp, eng="act" if tb % 2 else "dve")
        wv = self.carve(KC * 128, BF16, [KC, 256], "wnv")
        self.load_w(wv, self.win_d[l, :, 928:1184].rearrange("(k p) c -> p k c", p=128))
        for t in range(18):
            pv = self.PS(6 + t % 2, 256)
            for kc in range(KC):
                self.mm(pv, V(self.hT_t[:, kc, t * 128:(t + 1) * 128], [self.hb[kc][t // 4]]), wv[:, kc, :],
                        start=(kc == 0), stop=(kc == KC - 1))
            self.copy(V(vaug.ap[:, t, :, 0:64], vaug.bufs), V(pv.ap.rearrange("p (h d) -> p h d", h=4), pv.bufs),
                      eng="act" if t % 2 else "dve")
        nabs = self.carve(4 * 19 * 64, F32, [4, 19, 64], "nabs")
        self.dma(nabs, self.nab_d[l], eng="sp")
        E = self.carve(4 * 19 * 32, BF16, [4, 19, 64], "E")
        self.act(E, nabs, AF.Exp)
        st = self.attn_state()
        tmps = [self.carve(7 * 32, BF16, [7, 64], "ntmp") for _ in range(3)]
        ptl = [self.carve(5 * 32, BF16, [5, 64], "ptl") for _ in range(3)]
        nn = 0
        for h in range(4):
            c, pb = h // 2, (h % 2) * 64
            V_fn = lambda t, h=h: V(vaug.ap[:, t, h, :], vaug.bufs)
            LA = 2
            pend = []
            for rr_ in range(32 + LA):
                if rr_ < 32:
                    r = rr_
                    sr = min(max(r - 4, 0), 24)
                    if sr % 2 == 0:
                        nloc, t0 = 4, sr // 2
                        off = sr - r + 7
                        Esel = V(E.ap[:, h, off:off + 7:2, :], E.bufs)
                    else:
                        nloc, t0 = 5, (sr - 1) // 2
                        Esel = V(E.ap[:, h, 14:19, :], E.bufs)
                    tiles = [t0 + i for i in range(nloc)] + [16, 17]
                    ntl = len(tiles)
                    S_ps = self.PS(nn % 3, ntl * 64)
                    tmp = tmps[nn % 3]
                    pl = ptl[nn % 3]
                    nn += 1
                    q = V(nqk.ap[pb:pb + 64, 0, c, r * 64:(r + 1) * 64], nqk.bufs)
                    for i, t in enumerate(tiles):
                        k = V(nqk.ap[pb:pb + 64, 1, c, t * 128:(t + 1) * 128], nqk.bufs)
                        self.mm(S_ps[:, i * 64:(i + 1) * 64], k, q)
                    pend.append((r, tiles, nloc, Esel, S_ps, tmp, pl))
                if rr_ >= LA:
                    r, tiles, nloc, Esel, S_ps, tmp, pl = pend.pop(0)
                    ntl = len(tiles)
                    if r % 8 == 0:
                        O_ps = self.PS(3 + st["nO"] % 2, 512, 65)
                        st["nO"] += 1
                    self.act(V(tmp.ap[:, 0:ntl, :], tmp.bufs), V(S_ps.ap.rearrange("p (t q) -> p t q", q=64), S_ps.bufs),
                             AF.Exp, scale=SC)
                    self.tt(V(pl.ap[:, 0:nloc, :], pl.bufs), V(tmp.ap[:, 0:nloc, :], tmp.bufs), Esel, ALU.mult)
                    Or = O_ps[0:65, (r % 8) * 64:(r % 8 + 1) * 64]
                    for i, t in enumerate(tiles):
                        if i < nloc:
                            p_ = V(pl.ap[:, i, :], pl.bufs)
                        else:
                            p_ = V(tmp.ap[:, i, :], tmp.bufs)
                        self.mm(Or, V_fn(t), p_, start=(i == 0), stop=(i == ntl - 1))
                    if r % 8 == 7:
                        self.finish_attn(O_ps, 512, pb, 2 + c, (r - 7) * 64, st["fin"][st["nO"] % 2])
            KT_fn = lambda t, c=c, pb=pb: V(nqk.ap[pb:pb + 64, 1, c, t * 128:(t + 1) * 128], nqk.bufs)
            self.attn_dense(V(nqk.ap[pb:pb + 64, 0, c, SEQ:T], nqk.bufs), KT_fn, V_fn, [16, 17], 256, SC,
                            (pb, 2 + c, SEQ), st)

    def gdn(self, l):
        hT = self.hT
        NCH = 36
        DK = 128 ** -0.5
        gc_ = self.gdnc
        U = [gc_[0:64, d * 320:d * 320 + 64] for d in range(2)]
        SU = [gc_[0:64, d * 320 + 64:d * 320 + 128] for d in range(2)]
        MN = [gc_[0:64, d * 320 + 128:d * 320 + 320] for d in range(2)]
        I64 = gc_[0:64, 640:704]
        ONE = gc_[0:64, 704:832]
        ORD = [[32, 33, 34, 35] + list(range(32)), [35, 34, 33, 32] + list(range(31, -1, -1))]
        psD = lambda n0, n1, p=64: V(self.psD_t[0:p, n0:n1], [self.psb[6], self.psb[7]])
        wab = self.carve(KC * 8, BF16, [KC, 16], "wab")
        self.load_w(wab, self.win_d[l, :, 3232:3248].rearrange("(k p) c -> p k c", p=128))
        ab = self.carve(NCH * 16, F32, [NCH, 16], "ab")
        for n in range(NCH):
            for kc in range(KC):
                self.mm(psD(n * 16, n * 16 + 16),
                        V(self.hT_t[:, kc, n * 64:(n + 1) * 64], [self.hb[kc][n // 8]]), wab[:, kc, :],
                        start=(kc == 0), stop=(kc == KC - 1))
        self.copy(ab[0:64], V(psD(0, NCH * 16).ap.rearrange("p (n c) -> p n c", c=16), [self.psb[6], self.psb[7]]))
        abc = self.gabc
        nA = self.carve(8, F32, None, "nA")
        self.act(nA[0:64], V(abc.ap[0:64, l, 0:8], abc.bufs), AF.Exp)
        self.ts(nA[0:64], nA[0:64], -1.0, ALU.mult)
        g = self.carve(NCH * 8, F32, [NCH, 8], "g")
        beta = self.carve(NCH * 8, F32, [NCH, 8], "beta")
        sp = self.carve(NCH * 8, F32, [NCH, 8], "sp")
        dtb = V(abc.ap[0:64, l, 8:16].unsqueeze(1).to_broadcast([64, NCH, 8]), abc.bufs)
        self.tt(sp[0:64], V(ab.ap[0:64, :, 0:8], ab.bufs), dtb, ALU.add)
        self.act(sp[0:64], sp[0:64], AF.Exp)
        self.act(sp[0:64], sp[0:64], AF.Ln, bias=self.one_t[0:64], scale=1.0)
        self.tt(g[0:64], sp[0:64], V(nA.ap[0:64].unsqueeze(1).to_broadcast([64, NCH, 8]), nA.bufs), ALU.mult)
        self.act(beta[0:64], V(ab.ap[0:64, :, 8:16], ab.bufs), AF.Sigmoid)
        gcs = self.carve(NCH * 8, F32, [NCH, 8], "gcs")
        for d in range(2):
            pc = self.PS(d, NCH * 4, 64)
            self.mm(pc, U[d], V(g.ap[0:64, :, d * 4:(d + 1) * 4], g.bufs))
            self.copy(V(gcs.ap[0:64, :, d * 4:(d + 1) * 4], gcs.bufs),
                      V(pc.ap.rearrange("p (n c) -> p n c", c=4), pc.bufs))
        ptot = self.PS(2, NCH * 8, 128)
        self.mm(ptot, ONE, V(g.ap[0:64].rearrange("p n c -> p (n c)"), g.bufs))
        glast = self.carve(NCH * 8, F32, [NCH, 8], "glast")
        self.act(V(glast.ap.rearrange("p n c -> p (n c)"), glast.bufs), ptot, AF.Exp)
        etail = self.carve(NCH * 8, F32, [NCH, 8], "etail")
        self.tt(V(etail.ap[0:64].rearrange("p n c -> p (n c)"), etail.bufs), ptot[0:64, :],
                V(gcs.ap[0:64].rearrange("p n c -> p (n c)"), gcs.bufs), ALU.subtract)
        self.act(etail[0:64], etail[0:64], AF.Exp)
        eg = self.carve(NCH * 8, F32, [NCH, 8], "eg")
        self.act(eg[0:64], gcs[0:64], AF.Exp)
        s_kbg = self.carve(NCH * 8, F32, [NCH, 8], "skbg")
        self.tt(s_kbg[0:64], beta[0:64], eg[0:64], ALU.mult)
        s_q = self.carve(NCH * 8, F32, [NCH, 8], "sq_")
        self.ts(s_q[0:64], eg[0:64], DK, ALU.mult)
        import os
        STOP = float(os.environ.get("GDN_STOP", "99"))
        if STOP <= 0:
            return
        kT = self.carve(T // 2, BF16, None, "kT")
        qT = self.carve(T // 2, BF16, None, "qT")
        szT = self.carve(T // 2, BF16, None, "szT")
        k_tok = self.carve(NCH * 64, BF16, [NCH, 128], "ktok")
        v_tok = self.carve(NCH * 64, BF16, [NCH, 128], "vtok")
        kbT_sh = self.carve(T // 2, BF16, None, "kbT")
        oT = self.carve(T, F32, None, "oT")
        oTb = [self.buf("oT") for _ in range(NCH)]
        off_r = self.scr_base + self.scr_off[0]
        rawp = self.carve(T + 8, F32, None, "rawp")
        acc = self.carve(T, F32, None, "acc")
        XYall = self.pool_t[:, off_r:off_r + 9 * 512].rearrange("p (g a i c) -> p g a i c", g=9, a=2, i=4)
        XYb = [self.buf("XYb") for _ in range(9)]
        Qgb = [self.buf("Qgb") for _ in range(9)]
        perd = []
        for d in range(2):
            perd.append(dict(
                kbT=kbT_sh, qdT=self.carve(T // 2, BF16, None, "qdT"),
                QT=self.carve(NCH * 32, BF16, [NCH, 64], "QTa"), AiT=self.carve(NCH * 32, BF16, [NCH, 64], "AiT"),
                nwT=self.carve(NCH * 32, BF16, [NCH, 64], "nwT"),
                S=self.carve(128, F32, None, "S"), Sb=self.carve(64, BF16, None, "Sb")))
        wq_ = [self.carve(KC * 64, BF16, [KC, 128], "wg_") for _ in range(2)]
        sqb = [self.carve(256, BF16, None, "sqb") for _ in range(2)]
        rr = [self.carve(512, F32, None, "rr") for _ in range(2)]
        diag = [self.carve(512, F32, [8, 64], "diag") for _ in range(2)]
        GU = [self.carve(256, F32, [4, 64], "GU") for _ in range(2)]
        Dall = [self.carve(768, F32, [4, 192], "Dall") for _ in range(2)]
        Qall = self.carve(9 * 256, F32, [9, 4, 64], "Qall").ap
        kbg = [self.carve(256, BF16, [4, 128], "kbg") for _ in range(2)]
        vbt = [self.carve(64, BF16, None, "vb") for _ in range(4)]
        ktt = [self.carve(64, BF16, None, "kt") for _ in range(4)]
        vnw = [self.carve(64, BF16, None, "vn") for _ in range(4)]
        gt = self.carve(512, F32, None, "gt")
        self.memset(rawp[:, 0:2], 0.0)
        self.memset(rawp[:, 2 + SEQ:2 + SEQ + 4], 0.0)
        self.memset(rawp[:, T + 6:T + 8], 0.0)
        nw = 0
        for h in range(4):
            for which in range(4):
                w = wq_[nw % 2]
                nw += 1
                col = 1184 + which * 512 + h * 128
                self.load_w(w, self.win_d[l, :, col:col + 128].rearrange("(k p) c -> p k c", p=128))
                for tb, (o, nt) in enumerate(TB):
                    pp = self.PS(tb % 2, nt)
                    for kc in range(KC):
                        self.mm(pp, w[:, kc, :], hT(kc, tb), start=(kc == 0), stop=(kc == KC - 1))
                    if which == 3:
                        self.act(szT[:, o:o + nt], pp, AF.Silu)
                    else:
                        oo = 2 + o if tb < 4 else 6 + o
                        self.copy(rawp[:, oo:oo + nt], pp, eng="act")
                if which == 3:
                    continue
                cw = lambda j: V(self.gcw.ap[:, l, which, h, j:j + 1], self.gcw.bufs)
                for (o0, n0, p0) in ((0, SEQ, 0), (SEQ, CTXL, SEQ + 4)):
                    a_ = acc[:, o0:o0 + n0]
                    self.ts(a_, rawp[:, p0:p0 + n0], cw(0), ALU.mult)
                    for j in range(1, 5):
                        self.stt(a_, rawp[:, p0 + j:p0 + j + n0], cw(j), a_, ALU.mult, ALU.add)
                self.act(acc, acc, AF.Silu)
                if which == 2:
                    for g4 in range(0, NCH, 4):
                        pt_ = self.PS(4 + (g4 // 4) % 2, 512, 64)
                        for i in range(4):
                            n = g4 + i
                            self.tr(pt_[:, i * 128:(i + 1) * 128], acc[:, n * 64:(n + 1) * 64], self.ident)
                        self.copy(V(v_tok.ap[0:64, g4:g4 + 4, :], v_tok.bufs),
                                  V(pt_.ap.rearrange("p (n c) -> p n c", c=128), pt_.bufs), eng="act")
                    continue
                dst = qT if which == 0 else kT
                for tb, (o, nt) in enumerate(TB):
                    sq = sqb[tb % 2]
                    self.act(sq[:, 0:nt], acc[:, o:o + nt], AF.Square)
                    pn = self.PS(2 + tb % 2, nt)
                    self.mm(pn, self.ones_bf, sq[:, 0:nt])
                    r_ = rr[tb % 2]
                    self.act(r_[:, 0:nt], pn, AF.Sqrt, bias=self.eps_t, scale=1.0)
                    self.recip(r_[:, 0:nt], r_[:, 0:nt])
                    self.tt(dst[:, o:o + nt], acc[:, o:o + nt], r_[:, 0:nt], ALU.mult)
            for (src, dstt) in ((kT, k_tok),):
                for g8 in range(0, NCH, 8):
                    cnt = min(8, NCH - g8)
                    pt_ = V(self.ps[4 + (g8 // 8) % 2][0:64, 0:512].bitcast(BF16), [self.psb[4 + (g8 // 8) % 2]])
                    for i in range(cnt):
                        n = g8 + i
                        self.tr(pt_[:, i * 128:(i + 1) * 128], src[:, n * 64:(n + 1) * 64], self.ident_bf)
                    self.copy(V(dstt.ap[0:64, g8:g8 + cnt, :], dstt.bufs),
                              V(pt_.ap[:, 0:cnt * 128].rearrange("p (n c) -> p n c", c=128), pt_.bufs), eng="act")
            self.P.barrier()
            if STOP <= 1:
                continue
            for d in range(2):
                c = d * 4 + h
                pd = perd[d]
                jobs = []
                for (srcT, scal, dstT) in ((kT, beta, pd["kbT"]), (qT, s_q, pd["qdT"])):
                    for n0 in range(0, NCH, 8):
                        jobs.append((srcT, scal, dstT, n0, min(8, NCH - n0)))
                pendb = []
                for bi in range(len(jobs) + 1):
                    if bi < len(jobs):
                        srcT, scal, dstT, n0, cnt = jobs[bi]
                        dg = diag[bi % 2]
                        self.tt(V(dg.ap[0:64, 0:cnt, :], dg.bufs),
                                V(I64.ap.unsqueeze(1).to_broadcast([64, cnt, 64]), I64.bufs),
                                V(scal.ap[0:64, n0:n0 + cnt, c:c + 1].to_broadcast([64, cnt, 64]), scal.bufs), ALU.mult)
                        pb_ = self.PS(bi % 2, cnt * 64)
                        self.mm(pb_, ONE, V(dg.ap[0:64, 0:cnt, :].rearrange("p n c -> p (n c)"), dg.bufs))
                        pendb.append((srcT, dstT, n0, cnt, pb_))
                    if bi >= 1:
                        srcT, dstT, n0, cnt, pb_ = pendb.pop(0)
                        self.tt(dstT[:, n0 * 64:(n0 + cnt) * 64], srcT[:, n0 * 64:(n0 + cnt) * 64], pb_, ALU.mult)
                if STOP <= 2.1:
                    continue
                NG = NCH // 4
                def setup_front(gi):
                    n0 = gi * 4
                    gu, da = GU[gi % 2], Dall[gi % 2]
                    self.tt(gu[0:64], V(U[d].ap.unsqueeze(1).to_broadcast([64, 4, 64]), U[d].bufs),
                            V(g.ap[0:64, n0:n0 + 4, c:c + 1].to_broadcast([64, 4, 64]), g.bufs), ALU.mult)
                    for i in range(4):
                        b0 = i * 256
                        for rg in range(3):
                            reg = psD(b0 + rg * 64, b0 + rg * 64 + 64)
                            self.mm(reg, I64, MN[d][:, rg * 64:(rg + 1) * 64], start=True, stop=False)
                            if rg == 0:
                                self.mm(reg, V(gu.ap[0:64, i, :], gu.bufs), SU[d], start=False, stop=True)
                            else:
                                self.mm(reg, SU[d], V(gu.ap[0:64, i, :], gu.bufs), start=False, stop=True)
                    pk = self.PS(5 if gi % 2 == 0 else 3, 512, 64)
                    pa = self.PS(4 if gi % 2 == 0 else 2, 256, 64)
                    for i in range(4):
                        n = n0 + i
                        self.mm(pk[:, i * 128:i * 128 + 64], pd["kbT"][:, n * 64:(n + 1) * 64], kT[:, n * 64:(n + 1) * 64])
                        self.mm(pk[:, i * 128 + 64:i * 128 + 128], kT[:, n * 64:(n + 1) * 64], pd["kbT"][:, n * 64:(n + 1) * 64])
                        self.mm(pa[:, i * 64:(i + 1) * 64], kT[:, n * 64:(n + 1) * 64], qT[:, n * 64:(n + 1) * 64])
                    self.act(da[0:64], V(psD(0, 1024).ap.rearrange("p (i c) -> p i c", c=256)[:, :, 0:192],
                                         [self.psb[6], self.psb[7]]), AF.Exp)
                    return pk, pa

                def setup_back(gi, pk, pa):
                    n0 = gi * 4
                    da = Dall[gi % 2]
                    pk3 = V(pk.ap.rearrange("p (i c) -> p i c", c=128), pk.bufs)
                    X = V(XYall[0:64, gi, 0], [XYb[gi]])
                    Y = V(XYall[0:64, gi, 1], [XYb[gi]])
                    Q = V(Qall[0:64, gi], [Qgb[gi]])
                    self.stt(X, pk3[:, :, 0:64], -1.0, V(da.ap[0:64, :, 0:64], da.bufs), ALU.mult, ALU.mult)
                    self.stt(Y, pk3[:, :, 64:128], -1.0, V(da.ap[0:64, :, 64:128], da.bufs), ALU.mult, ALU.mult)
                    self.stt(V(pd["AiT"].ap[0:64, n0:n0 + 4, :], pd["AiT"].bufs),
                             V(pa.ap.rearrange("p (i c) -> p i c", c=64), pa.bufs), DK,
                             V(da.ap[0:64, :, 128:192], da.bufs), ALU.mult, ALU.mult)
                    self.tt(Q, Y, V(I64.ap.unsqueeze(1).to_broadcast([64, 4, 64]), I64.bufs), ALU.add, eng="dve")

                prev = None
                for gi in range(NG + 1):
                    cur = None
                    if gi < NG:
                        cur = (gi,) + setup_front(gi)
                    if prev is not None:
                        setup_back(*prev)
                    prev = cur
                if STOP <= 2.2:
                    continue
                nps = 0
                for m in range(1, 6):
                    for gi in range(NG):
                        pxy = self.PS(nps % 3, 512, 64)
                        nps += 1
                        Xi = lambda i: V(XYall[0:64, gi, 0, i, :], [XYb[gi]])
                        Yi = lambda i: V(XYall[0:64, gi, 1, i, :], [XYb[gi]])
                        for i in range(4):
                            self.mm(pxy[:, i * 64:(i + 1) * 64], Yi(i), Xi(i))
                        nc_ = 256
                        if m < 5:
                            nc_ = 512
                            for i in range(4):
                                self.mm(pxy[:, 256 + i * 64:256 + (i + 1) * 64], Xi(i), Yi(i))
                        self.copy(V(XYall[0:64, gi].rearrange("p a i c -> p (a i c)")[:, 0:nc_], [XYb[gi]]), pxy[:, 0:nc_],
                                  eng="act")
                    for gi in range(NG):
                        pq_ = self.PS(3 + gi % 2, 256, 64)
                        for i in range(4):
                            self.mm(pq_[:, i * 64:(i + 1) * 64], V(XYall[0:64, gi, 0, i, :], [XYb[gi]]),
                                    V(Qall[0:64, gi, i, :], [Qgb[gi]]))
                        Qf_ = V(Qall[0:64, gi].rearrange("p i c -> p (i c)"), [Qgb[gi]])
                        self.tt(Qf_, Qf_, pq_, ALU.add)
                if STOP <= 2.3:
                    continue
                for gi in range(NG + 1):
                    if gi < NG:
                        n0 = gi * 4
                        self.copy(V(pd["QT"].ap[0:64, n0:n0 + 4, :], pd["QT"].bufs), V(Qall[0:64, gi], [Qgb[gi]]), eng="act")
                        kb_ = kbg[gi % 2]
                        self.tt(kb_[0:64], V(k_tok.ap[0:64, n0:n0 + 4, :], k_tok.bufs),
                                V(s_kbg.ap[0:64, n0:n0 + 4, c:c + 1].to_broadcast([64, 4, 128]), s_kbg.bufs), ALU.mult,
                                eng="dve")
                    if gi >= 1:
                        g1 = gi - 1
                        n0 = g1 * 4
                        kb_ = kbg[g1 % 2]
                        pw = self.PS(g1 % 2, 256, 128)
                        for i in range(4):
                            n = n0 + i
                            self.mm(pw[:, i * 64:(i + 1) * 64], V(kb_.ap[0:64, i, :], kb_.bufs),
                                    V(pd["QT"].ap[0:64, n, :], pd["QT"].bufs))
                        self.ts(V(pd["nwT"].ap[:, n0:n0 + 4, :].rearrange("p i c -> p (i c)"), pd["nwT"].bufs), pw, -1.0, ALU.mult)
            self.P.barrier()
            if STOP <= 2:
                continue
            self.memset(oT, 0.0, eng="pool")
            for d in range(2):
                self.memset(perd[d]["S"], 0.0)
                self.memset(perd[d]["Sb"], 0.0)
            for step in range(NCH):
                for d in range(2):
                    n = ORD[d][step]
                    c = d * 4 + h
                    pd = perd[d]
                    sl = (step % 2) * 2 + d
                    vb, kt, vn = vbt[sl], ktt[sl], vnw[sl]
                    self.ts(vb[0:64], V(v_tok.ap[0:64, n, :], v_tok.bufs), V(beta.ap[0:64, n, c:c + 1], beta.bufs),
                            ALU.mult, eng="dve")
                    self.ts(kt[0:64], V(k_tok.ap[0:64, n, :], k_tok.bufs), V(etail.ap[0:64, n, c:c + 1], etail.bufs),
                            ALU.mult, eng="dve")
                    pv_ = self.PS(d * 3, 128, 64)
                    self.mm(pv_, V(pd["QT"].ap[0:64, n, :], pd["QT"].bufs), vb[0:64], start=True, stop=False)
                    self.mm(pv_, V(pd["nwT"].ap[:, n, :], pd["nwT"].bufs), pd["Sb"], start=False, stop=True)
                    self.copy(vn[0:64], pv_, eng="act")
                    po_ = self.PS(d * 3 + 1, 64, 128)
                    self.mm(po_, pd["Sb"], pd["qdT"][:, n * 64:(n + 1) * 64], start=True, stop=False)
                    self.mm(po_, vn[0:64], V(pd["AiT"].ap[0:64, n, :], pd["AiT"].bufs), start=False, stop=True)
                    ps_ = self.PS(d * 3 + 2, 128, 128)
                    self.mm(ps_, kt[0:64], vn[0:64])
                    self.stt(pd["S"], pd["S"], V(glast.ap[:, n, c:c + 1], glast.bufs), ps_, ALU.mult, ALU.add)
                    self.copy(pd["Sb"], pd["S"], eng="act")
                    ov = V(oT.ap[:, n * 64:(n + 1) * 64], [oTb[n]])
                    self.tt(ov, ov, po_, ALU.add)
            if STOP <= 3:
                continue
            for tb, (o, nt) in enumerate(TB):
                ovb = V(oT.ap[:, o:o + nt], [oTb[n] for n in range(o // 64, (o + nt) // 64)] + oT.bufs)
                sq = sqb[tb % 2]
                self.act(sq[:, 0:nt], ovb, AF.Square)
                pn = self.PS(6 + tb % 2, nt)
                self.mm(pn, self.ones_bf, sq[:, 0:nt])
                r_ = rr[tb % 2]
                self.act(r_[:, 0:nt], pn, AF.Sqrt, bias=self.eps_t, scale=1.0 / 128)
                self.recip(r_[:, 0:nt], r_[:, 0:nt])
                self.stt(gt[:, 0:nt], ovb, V(self.ggo.ap[:, l:l + 1], self.ggo.bufs), r_[:, 0:nt], ALU.mult, ALU.mult)
                go = sqb[(tb + 1) % 2]
                self.tt(go[:, 0:nt], gt[:, 0:nt], szT[:, o:o + nt], ALU.mult)
                self.dma(V(self.mix_d[:, 4 + h, o:o + nt], [self.mixb[4 + h]]), go[:, 0:nt], eng="sp", sembuf=go.bufs[0])
            self.P.barrier()

def host_inputs(inputs, b, mixers=True):
    f = lambda a: np.ascontiguousarray(a, dtype=np.float32)
    m = {}
    m["x"] = f(inputs["x"][b])
    m["ctx"] = f(inputs["ctx"][b])
    cc = np.stack([inputs["c"][b], inputs["c_ctx"]], axis=-1)
    m["cc"] = f(cc.reshape(KC, 128, 2).transpose(1, 0, 2))
    m["w_ada"] = f(inputs["w_ada"])
    m["b_adaT"] = f(inputs["b_ada"].reshape(DEPTH, 48, 128).transpose(2, 0, 1))
    gv = np.stack([inputs["g_mix"], inputs["g_ffn"]], axis=1)
    m["gvecs"] = f(gv.reshape(DEPTH, 2, KC, 128).transpose(3, 0, 1, 2))
    m["g_finalT"] = f(inputs["g_final"].reshape(KC, 128).T)
    m["w_gate"] = f(inputs["w_gate"])
    m["w_up"] = f(inputs["w_up"])
    m["w_down"] = f(inputs["w_down"])
    m["ident"] = np.eye(128, dtype=np.float32)
    if not mixers:
        return m
    w_in = inputs["w_in"]
    m["w_in"] = f(w_in)
    idx = np.arange(32)
    a_, hf, fr = idx // 16, (idx // 8) % 2, idx % 8
    partner = a_ * 16 + (1 - hf) * 8 + fr
    wpe = np.zeros((DEPTH, D, 2, 96), np.float32)
    wpe[:, :, 0, 64:96] = w_in[:, :, 384:416]
    wpe[:, :, 1, 64:96] = w_in[:, :, 384 + partner]
    m["w_pe2"] = wpe
    wq = inputs["mla_w_q_up"]
    wq2 = np.stack([wq, wq], axis=2).astype(np.float32)
    for h in range(4):
        wq2[:, :, 1, h * 96 + 64 + idx] = wq[:, :, h * 96 + 64 + partner]
    m["w_q2"] = f(wq2)
    wkv = inputs["mla_w_kv_up"].reshape(DEPTH, 128, 4, 128)
    m["w_kv_nv"] = f(np.concatenate([wkv[..., :64].reshape(DEPTH, 128, 256), wkv[..., 64:].reshape(DEPTH, 128, 256)], -1))
    gq = inputs["mla_g_q"].reshape(DEPTH, 2, 128)
    mg = np.concatenate([gq, inputs["mla_g_kv"].reshape(DEPTH, 1, 128)], axis=1)
    m["mla_g"] = f(mg.transpose(2, 0, 1))
    m["ropeCS"] = _rope_tables()
    sel = np.zeros((128, 64), np.float32)
    sel[64, :] = 1.0
    m["sel65"] = sel
    m["nab"] = _na_tables(inputs["na_rel_bias"])
    m["w_out"] = f(inputs["w_out"])
    m["gdnc"] = _gdn_consts()
    ab = np.concatenate([inputs["dn_a_log"].reshape(DEPTH, 8), inputs["dn_dt_bias"].reshape(DEPTH, 8)], axis=1)
    m["gdn_ab"] = f(np.broadcast_to(ab[None], (128, DEPTH, 16)))
    cw = inputs["dn_conv_w"].reshape(DEPTH, 5, 3, 4, 128)
    m["gdn_cw"] = f(cw.transpose(4, 0, 2, 3, 1))
    m["gdn_go"] = f(inputs["dn_g_out"].T)
    return m


def _gdn_consts():
    if "gdn" in _CONST:
        return _CONST["gdn"]
    NEG = -30000.0
    t = np.arange(64)[:, None]
    i = np.arange(64)[None, :]
    out = np.zeros((128, 832), np.float32)
    for d in range(2):
        U = (t <= i) if d == 0 else (t >= i)
        SU = (t > i) if d == 0 else (t < i)
        Ma = (i < t) if d == 0 else (i > t)
        Mb = (t < i) if d == 0 else (t > i)
        Mc = (t <= i) if d == 0 else (t >= i)
        base = d * 320
        out[0:64, base:base + 64] = U
        out[0:64, base + 64:base + 128] = SU
        out[0:64, base + 128:base + 192] = np.where(Ma, 0.0, NEG)
        out[0:64, base + 192:base + 256] = np.where(Mb, 0.0, NEG)
        out[0:64, base + 256:base + 320] = np.where(Mc, 0.0, NEG)
    out[0:64, 640:704] = np.eye(64)
    out[0:64, 704:832] = 1.0
    _CONST["gdn"] = out
    return out


_CONST = {}


def _rope_tables():
    if "rope" in _CONST:
        return _CONST["rope"]
    t = np.arange(SEQ)
    pos = np.stack([t // 64, t % 64], axis=-1).astype(np.float32)
    inv = np.power(np.float32(10000.0), -np.arange(8, dtype=np.float32) / np.float32(8)).astype(np.float32)
    ang = (pos[:, :, None] * inv).astype(np.float32)
    cos, sin = np.cos(ang).astype(np.float32), np.sin(ang).astype(np.float32)
    tab = np.zeros((128, 2, SEQ), np.float32)
    for i in range(32):
        a_, hf, fr = i // 16, (i // 8) % 2, i % 8
        tab[64 + i, 0, :] = cos[:, a_, fr]
        tab[64 + i, 1, :] = sin[:, a_, fr] * (-1.0 if hf == 0 else 1.0)
    _CONST["rope"] = tab
    return tab


def _na_tables(rel_bias):
    NEG = np.float32(-30000.0)
    kc = np.arange(64)[:, None]
    qc = np.arange(64)[None, :]
    cs = np.clip(qc - 8, 0, 48)
    ok = (kc >= cs) & (kc < cs + 16)
    dc = np.clip(kc - qc + 15, 0, 30)
    tb = rel_bias[:, :, :, dc]
    tb = np.where(ok[None, None, None], tb, NEG).astype(np.float32)
    mask = np.full((DEPTH, 4, 64, 64), NEG, np.float32)
    tiles = []
    for dra in range(14):
        tiles.append(np.concatenate([tb[:, :, dra], tb[:, :, dra + 1]], axis=2))
    tiles.append(np.concatenate([mask, tb[:, :, 3]], axis=2))
    for dra in (4, 6, 8):
        tiles.append(np.concatenate([tb[:, :, dra], tb[:, :, dra + 1]], axis=2))
    tiles.append(np.concatenate([tb[:, :, 10], mask], axis=2))
    nab = np.stack(tiles, axis=2)
    return np.ascontiguousarray(nab.transpose(0, 3, 1, 2, 4), dtype=np.float32)


def build_nc(n_layers=DEPTH, mixers=True, dbg=None):
    nc = bass.Bass("TRN2", target_bir_lowering=False)
    es = ExitStack()
    b = Builder(nc, es, n_layers=n_layers, mixers=mixers, dbg=dbg)
    with es:
        b.build()
    return nc


def kernel(**inputs):
    nc = build_nc()
    in_maps = [host_inputs(inputs, b) for b in range(8)]
    res = run_bass_kernel_spmd(nc, in_maps, core_ids=list(range(8)))
    out = np.stack([np.asarray(r["out"], dtype=np.float32) for r in res.results], axis=0)
    return out
```

```python
import numpy as np
from contextlib import ExitStack
import concourse.bass as bass
import concourse.mybir as mybir
from concourse.bass_utils import run_bass_kernel_spmd

F32 = mybir.dt.float32
BF16 = mybir.dt.bfloat16
AF = mybir.ActivationFunctionType
ALU = mybir.AluOpType
AX = mybir.AxisListType

D = 1024
SEQ = 2048
CTXL = 256
T = SEQ + CTXL
DEPTH = 4
KC = 8
FFN = 2816
NH = FFN // 128
IN_W = 3248
EPS = 1e-6
TB = [(0, 512), (512, 512), (1024, 512), (1536, 512), (2048, 256)]
ENGS = ("pe", "act", "dve", "pool", "sp")
import os as _os
NOPFIX = int(_os.environ.get("NOPFIX", "0"))


class Buf:
    __slots__ = ("name", "w_eng", "w_dma", "r_eng", "r_dma", "dslot", "excl")

    def __init__(self, name):
        self.name = name
        self.excl = False
        self.w_eng = {}
        self.w_dma = []
        self.r_eng = {}
        self.r_dma = []
        self.dslot = None


class Rec:
    __slots__ = ("eng", "fn", "deps", "is_dma", "sembuf", "dval", "dslot", "need_inc", "ival", "semi")

    def __init__(self):
        self.need_inc = False
        self.ival = 0
        self.semi = 0
        self.dval = 0


class V:
    __slots__ = ("ap", "bufs")

    def __init__(self, ap, bufs):
        self.ap = ap
        self.bufs = bufs

    def __getitem__(self, k):
        return V(self.ap[k], self.bufs)

    def bc(self, shape):
        return V(self.ap.to_broadcast(shape), self.bufs)


def _bufs(vs):
    out = []
    for v in vs:
        if v is None or isinstance(v, (int, float)):
            continue
        out.extend(v.bufs)
    return out


def _ap(v):
    return v.ap if isinstance(v, V) else v


class Prog:
    SEM_EPOCH = 20000

    def __init__(self):
        self.recs = {e: [] for e in ENGS}
        self.dma_all = []
        self.slots = []
        self.free = []
        self.live = []

    def op(self, eng, fn, reads=(), writes=(), pwrites=(), dma_buf=None):
        r = Rec()
        r.eng = eng
        r.fn = fn
        r.is_dma = dma_buf is not None
        r.sembuf = dma_buf
        deps = {}

        def add(d, kind):
            if d is r:
                return
            if (not r.is_dma) and (not d.is_dma) and d.eng == eng:
                if eng == "pe" or kind != "raw":
                    return
            deps[id(d)] = d

        for b in reads:
            for w in b.w_eng.values():
                add(w, "raw")
            for w in b.w_dma:
                add(w, "raw")
            if b.excl:
                for x in b.r_eng.values():
                    if x.eng != eng:
                        add(x, "raw")
        for b in list(writes) + list(pwrites):
            for w in b.w_eng.values():
                add(w, "waw")
            for w in b.w_dma:
                add(w, "waw")
            for x in b.r_eng.values():
                add(x, "war")
            for x in b.r_dma:
                add(x, "war")
        r.deps = list(deps.values())
        for b in reads:
            if r.is_dma:
                b.r_dma.append(r)
            else:
                b.r_eng[eng] = r
        for b in writes:
            b.w_eng = {}
            b.w_dma = []
            b.r_eng = {}
            b.r_dma = []
        for b in list(writes) + list(pwrites):
            if r.is_dma:
                b.w_dma.append(r)
            else:
                b.w_eng[eng] = r
            b.r_eng = {}
            b.r_dma = []
        if r.is_dma:
            if dma_buf.dslot is None:
                if self.free:
                    dma_buf.dslot = self.free.pop()
                else:
                    self.slots.append(0)
                    dma_buf.dslot = len(self.slots) - 1
                self.live.append(dma_buf)
            self.slots[dma_buf.dslot] += 16
            r.dslot = dma_buf.dslot
            r.dval = self.slots[dma_buf.dslot]
            self.dma_all.append(r)
        self.recs[eng].append(r)
        return r

    def barrier(self):
        last = []
        for e in ENGS:
            for r in reversed(self.recs[e]):
                if not r.is_dma and r.fn is not None:
                    last.append(r)
                    break
        pend = list(self.dma_all)
        self.dma_all = []
        for b in self.live:
            if self.slots[b.dslot] < 40000:
                self.free.append(b.dslot)
            b.dslot = None
        self.live = []
        for e in ENGS:
            r = Rec()
            r.eng = e
            r.fn = None
            r.is_dma = False
            r.sembuf = None
            r.deps = [d for d in last if d.eng != e] + pend
            self.recs[e].append(r)

    def emit(self, nc, es, final_waits):
        for e in ENGS:
            for r in self.recs[e]:
                for d in r.deps:
                    if not d.is_dma:
                        d.need_inc = True
        nsem = {}
        for e in ENGS:
            c = 0
            si = 0
            for r in self.recs[e]:
                if r.need_inc:
                    c += 1
                    if c > self.SEM_EPOCH:
                        si += 1
                        c = 1
                    r.ival = c
                    r.semi = si
            nsem[e] = si + 1
        sems = {e: [es.enter_context(nc.semaphore("s_%s%d" % (e, i))) for i in range(nsem[e])] for e in ENGS}
        dsems = [es.enter_context(nc.semaphore("d_%d" % i)) for i in range(len(self.slots))]
        recs = self.recs

        def run(e, eng):
            seen = {}
            for r in recs[e]:
                need = {}
                for d in r.deps:
                    if d.is_dma:
                        key = ("d", d.dslot)
                        sem = dsems[d.dslot]
                        val = d.dval
                    else:
                        key = (d.eng, d.semi)
                        sem = sems[d.eng][d.semi]
                        val = d.ival
                    if seen.get(key, 0) >= val:
                        continue
                    if key not in need or need[key][1] < val:
                        need[key] = (sem, val)
                for key, (sem, val) in need.items():
                    eng.wait_ge(sem, val)
                    seen[key] = val
                if NOPFIX and e == "pe" and len(need) >= 2:
                    eng.nop()
                if r.fn is None:
                    continue
                ins = r.fn(eng)
                if r.is_dma:
                    ins.then_inc(dsems[r.dslot], 16)
                elif r.need_inc:
                    ins.then_inc(sems[e][r.semi], 1)
            if e == "sp":
                for d in final_waits:
                    eng.wait_ge(dsems[d.dslot], d.dval)

        block = es.enter_context(nc.Block())

        @block.sync
        def _(eng):
            run("sp", eng)

        @block.tensor
        def _(eng):
            run("pe", eng)

        @block.scalar
        def _(eng):
            run("act", eng)

        @block.vector
        def _(eng):
            run("dve", eng)

        @block.gpsimd
        def _(eng):
            run("pool", eng)


class Builder:
    def __init__(self, nc, es, n_layers=DEPTH, mixers=True, dbg=None):
        self.nc = nc
        self.es = es
        self.P = Prog()
        self.L = n_layers
        self.mixers = mixers
        self.dbg = dbg
        self.nbuf = 0

    def buf(self, name="b"):
        self.nbuf += 1
        return Buf("%s%d" % (name, self.nbuf))

    def sb(self, name, shape, dt):
        t = self.es.enter_context(self.nc.sbuf_tensor("sb_" + name, list(shape), dt))
        return t

    def sbv(self, name, shape, dt):
        t = self.sb(name, shape, dt)
        return V(t[:], [self.buf(name)])

    def dram_in(self, name, shape, dt=F32):
        return self.nc.dram_tensor(name, list(shape), dt, kind="ExternalInput").ap()

    def mm(self, out, lhsT, rhs, start=True, stop=True, extra_reads=()):
        o, l, r = out.ap, lhsT.ap, rhs.ap
        self.P.op("pe", lambda e: e.matmul(o, lhsT=l, rhs=r, start=start, stop=stop),
                  reads=_bufs([lhsT, rhs]) + list(extra_reads), writes=() if not start else (), pwrites=out.bufs)

    def tr(self, out, in_, ident):
        o, i, d = out.ap, in_.ap, ident.ap
        self.P.op("pe", lambda e: e.transpose(o, i, d), reads=_bufs([in_, ident]), pwrites=out.bufs)

    def act(self, out, in_, func, bias=None, scale=None, accum_out=None, eng="act"):
        o, i = out.ap, in_.ap
        kw = {}
        if bias is not None:
            kw["bias"] = _ap(bias)
        if scale is not None:
            kw["scale"] = _ap(scale)
        if accum_out is not None:
            kw["accum_out"] = accum_out.ap
        self.P.op("act", lambda e: e.activation(o, i, func, **kw),
                  reads=_bufs([in_, bias, scale]), pwrites=_bufs([out, accum_out]))

    def tt(self, out, in0, in1, op, eng="dve"):
        o, a, b = out.ap, in0.ap, in1.ap
        self.P.op(eng, lambda e: e.tensor_tensor(o, a, b, op), reads=_bufs([in0, in1]), pwrites=out.bufs)

    def ts(self, out, in0, s1, op0, s2=None, op1=None, eng="dve"):
        o, a = out.ap, in0.ap
        s1a, s2a = _ap(s1), _ap(s2)
        if op1 is None:
            fn = lambda e: e.tensor_scalar(o, a, s1a, None, op0)
        else:
            fn = lambda e: e.tensor_scalar(o, a, s1a, s2a, op0, op1)
        self.P.op(eng, fn, reads=_bufs([in0, s1, s2]), pwrites=out.bufs)

    def stt(self, out, in0, scalar, in1, op0, op1, eng="dve"):
        o, a, b = out.ap, in0.ap, in1.ap
        s = _ap(scalar)
        self.P.op(eng, lambda e: e.scalar_tensor_tensor(o, a, s, b, op0, op1),
                  reads=_bufs([in0, scalar, in1]), pwrites=out.bufs)

    def copy(self, out, in_, eng="dve"):
        o, i = out.ap, in_.ap
        if eng == "act":
            self.P.op("act", lambda e: e.copy(o, i), reads=in_.bufs, pwrites=out.bufs)
        else:
            self.P.op(eng, lambda e: e.tensor_copy(o, i), reads=in_.bufs, pwrites=out.bufs)

    def recip(self, out, in_):
        o, i = out.ap, in_.ap
        self.P.op("dve", lambda e: e.reciprocal(o, i), reads=in_.bufs, pwrites=out.bufs)

    def memset(self, out, val, eng="dve"):
        o = out.ap
        self.P.op(eng, lambda e: e.memset(o, val), reads=(), pwrites=out.bufs)

    def dma(self, out, in_, eng="sp", sembuf=None):
        o, i = _ap(out), _ap(in_)
        rb = in_.bufs if isinstance(in_, V) else []
        wb = out.bufs if isinstance(out, V) else []
        if sembuf is None:
            sembuf = (wb or rb)[0]
        return self.P.op(eng, lambda e: e.dma_start(out=o, in_=i), reads=rb, pwrites=wb, dma_buf=sembuf)

    def build(self):
        nc, P, L = self.nc, self.P, self.L
        x_d = self.dram_in("x", [SEQ, D])
        ctx_d = self.dram_in("ctx", [CTXL, D])
        cc_d = self.dram_in("cc", [128, KC, 2])
        wada_d = self.dram_in("w_ada", [DEPTH, D, 6 * D])
        bada_d = self.dram_in("b_adaT", [128, DEPTH, 48])
        gv_d = self.dram_in("gvecs", [128, DEPTH, 2, KC])
        gfin_d = self.dram_in("g_finalT", [128, KC])
        wg_d = self.dram_in("w_gate", [DEPTH, D, FFN])
        wu_d = self.dram_in("w_up", [DEPTH, D, FFN])
        wd_d = self.dram_in("w_down", [DEPTH, FFN, D])
        ident_d = self.dram_in("ident", [128, 128])
        if self.mixers:
            self.win_d = self.dram_in("w_in", [DEPTH, D, IN_W])
            self.wpe_d = self.dram_in("w_pe2", [DEPTH, D, 2, 96])
            self.wq2_d = self.dram_in("w_q2", [DEPTH, 256, 2, 384])
            self.wkv_d = self.dram_in("w_kv_nv", [DEPTH, 128, 512])
            mlag_d = self.dram_in("mla_g", [128, DEPTH, 3])
            self.rope_d = self.dram_in("ropeCS", [128, 2, SEQ])
            sel_d = self.dram_in("sel65", [128, 64])
            self.nab_d = self.dram_in("nab", [DEPTH, 128, 4, 19, 64])
            self.wout_d = self.dram_in("w_out", [DEPTH, D, D])
            gdnc_d = self.dram_in("gdnc", [128, 832])
            gabc_d = self.dram_in("gdn_ab", [128, DEPTH, 16])
            gcw_d = self.dram_in("gdn_cw", [128, DEPTH, 3, 4, 5])
            ggo_d = self.dram_in("gdn_go", [128, DEPTH])
        out_d = nc.dram_tensor("out", [SEQ, D], F32, kind="ExternalOutput").ap()
        self.out_d = out_d
        if self.dbg:
            self.dbg_d = nc.dram_tensor("dbg", list(self.dbg), F32, kind="ExternalOutput").ap()

        POOLW = 40960
        XW = KC * T
        pool_t = self.sb("pool", [128, POOLW], F32)
        self.pool_t, self.POOLW, self.XW = pool_t, POOLW, XW
        xT_t = pool_t[:, 0:XW].rearrange("p (k t) -> p k t", k=KC)
        hT_t = self.sb("hT", [128, KC, T], BF16)
        self.xT_t, self.hT_t = xT_t, hT_t
        xb = [[self.buf("x") for _ in TB] for _ in range(KC)]
        hb = [[self.buf("h") for _ in TB] for _ in range(KC)]

        def xT(kc, tb):
            o, n = TB[tb]
            return V(xT_t[:, kc, o:o + n], [xb[kc][tb]])

        def hT(kc, tb):
            o, n = TB[tb]
            return V(hT_t[:, kc, o:o + n], [hb[kc][tb]])

        self.xT, self.hT = xT, hT
        modT = self.sbv("modT", [128, DEPTH, 6, KC, 2], F32)
        gsc = self.sbv("gsc", [128, DEPTH, 2, KC, 2], F32)
        bT = self.sbv("bT", [128, DEPTH, 48], F32)
        gv = self.sbv("gv", [128, DEPTH, 2, KC], F32)
        gfin = self.sbv("gfin", [128, KC], F32)
        ident = self.sbv("ident", [128, 128], F32)
        ones_bf = self.sbv("ones_bf", [128, 128], BF16)
        cc = self.sbv("cc", [128, KC, 2], F32)
        scc = self.sbv("scc", [128, KC, 2], F32)
        self.modT, self.gsc, self.ident, self.ones_bf = modT, gsc, ident, ones_bf
        self.xb, self.hb = xb, hb
        if self.mixers:
            self.mlag = self.sbv("mlag", [128, DEPTH, 3], F32)
            self.sel65 = self.sbv("sel65", [128, 64], F32)
            self.dma(self.mlag, mlag_d)
            self.gdnc = self.sbv("gdnc", [128, 832], F32)
            self.gabc = self.sbv("gabc", [128, DEPTH, 16], F32)
            self.gcw = self.sbv("gcw", [128, DEPTH, 3, 4, 5], F32)
            self.ggo = self.sbv("ggo", [128, DEPTH], F32)
            self.one_t = self.sbv("one_t", [128, 1], F32)
            self.ident_bf = self.sbv("ident_bf", [128, 128], BF16)
            self.dma(self.gdnc, gdnc_d)
            self.dma(self.gabc, gabc_d)
            self.dma(self.gcw, gcw_d)
            self.dma(self.ggo, ggo_d)
            self.memset(self.one_t, 1.0)
            self.dma(self.sel65, sel_d)
        self.scr_base = XW
        self.scr_lim = POOLW
        self.ps = [self.es.enter_context(nc.psum_tensor("ps%d" % i, [128, 512], F32)) for i in range(6)]
        self.psD_t = self.es.enter_context(nc.psum_tensor("psD", [128, 1024], F32))
        self.psb = [self.buf("ps") for _ in range(8)]
        for b_ in self.psb:
            b_.excl = True

        def PS(i, n=512, p=128):
            if i >= 6:
                return V(self.psD_t[0:p, (i - 6) * 512:(i - 6) * 512 + n], [self.psb[i]])
            return V(self.ps[i][0:p, 0:n], [self.psb[i]])

        self.PS = PS

        self.dma(ident, ident_d)
        self.dma(cc, cc_d)
        self.dma(bT, bada_d)
        self.dma(gv, gv_d)
        self.dma(gfin, gfin_d)
        self.memset(ones_bf, 1.0)
        self.act(scc, cc, AF.Silu)
        if self.mixers:
            self.copy(self.ident_bf, ident)

        scr_off = [0]

        def carve(n_f32, dt=F32, shape=None, name="c"):
            a = self.scr_base + scr_off[0]
            scr_off[0] += n_f32
            assert a + n_f32 <= self.scr_lim, (name, a + n_f32)
            ap = pool_t[:, a:a + n_f32]
            if dt != F32:
                ap = ap.bitcast(dt)
            if shape is not None:
                names = " ".join("d%d" % i for i in range(len(shape)))
                kw = {"d%d" % i: s for i, s in enumerate(shape[:-1])}
                ap = ap.rearrange("p (%s) -> p %s" % (names, names), **kw)
            return V(ap, [self.buf(name)])

        self.carve = carve
        self.scr_off = scr_off
        wa = [carve(KC * 512, F32, [KC, 512], "wa") for _ in range(2)]
        n = 0
        for l in range(L):
            pst = PS(l % 2, 96)
            for cb in range(12):
                w = wa[n % 2]
                n += 1
                self.dma(w, wada_d[l, :, cb * 512:(cb + 1) * 512].rearrange("(k p) c -> p k c", p=128),
                         eng="sp" if n % 2 else "act")
                for jj in range(4):
                    j = cb * 4 + jj
                    for k in range(KC):
                        self.mm(pst[:, 2 * j:2 * j + 2], w[:, k, jj * 128:(jj + 1) * 128], scc[:, k, :],
                                start=(k == 0), stop=(k == KC - 1))
            o = modT.ap[:, l].rearrange("p s k w -> p (s k) w")
            i0 = pst.ap.rearrange("p (j w) -> p j w", w=2)
            i1 = bT.ap[:, l, :].unsqueeze(2).to_broadcast([128, 48, 2])
            self.tt(V(o, modT.bufs), V(i0, pst.bufs), V(i1, bT.bufs), ALU.add)
            for which, (si, gi) in enumerate(((1, 0), (4, 1))):
                g_b = gv.ap[:, l, gi, :].unsqueeze(2).to_broadcast([128, KC, 2])
                self.stt(V(gsc.ap[:, l, which], gsc.bufs), V(modT.ap[:, l, si], modT.bufs), 1.0,
                         V(g_b, gv.bufs), ALU.add, ALU.mult)
        P.barrier()
        scr_off[0] = 0

        stg = [carve(4 * D, F32, [4, D], "stg") for _ in range(2)]
        n = 0
        for tb, (o, nt) in enumerate(TB):
            s = stg[tb % 2]
            ntile = nt // 128
            if tb < 4:
                src = x_d[o:o + nt, :]
            else:
                src = ctx_d[:, :]
            self.dma(s[:, 0:ntile, :], src.rearrange("(t p) d -> p t d", p=128), eng="sp")
            for kc in range(KC):
                pb = PS(n % 8, nt)
                n += 1
                for t in range(ntile):
                    self.tr(pb[:, t * 128:(t + 1) * 128], s[:, t, kc * 128:(kc + 1) * 128], ident)
                self.copy(xT(kc, tb), pb, eng="act" if kc % 2 else "dve")
        P.barrier()
        scr_off[0] = 0

        for l in range(L):
            self.norm_mod(l, 0)
            if self.mixers:
                self.mixer_phase(l)
            self.norm_mod(l, 1)
            self.ffn_phase(l, wg_d, wu_d, wd_d)
        self.final_phase(gfin)
        P.emit(nc, self.es, self.final_waits)

    def rstd_block(self, tb, sqs, rs_tmp, rstd):
        o, nt = TB[tb]
        ps = self.PS(tb % 2, nt)
        for kc in range(KC):
            sq = sqs[kc % 2]
            self.act(sq[:, 0:nt], self.xT(kc, tb), AF.Square)
            self.mm(ps, self.ones_bf, sq[:, 0:nt], start=(kc == 0), stop=(kc == KC - 1))
        self.act(rs_tmp[:, 0:nt], ps, AF.Sqrt, bias=self.eps_t, scale=1.0 / D)
        self.recip(rstd[:, 0:nt], rs_tmp[:, 0:nt])

    def norm_mod(self, l, which):
        P = self.P
        self.scr_off[0] = 0
        if not hasattr(self, "eps_t"):
            self.eps_t = self.sbv("eps_t", [128, 1], F32)
            self.memset(self.eps_t, EPS)
        sqs = [self.carve(256, BF16, None, "sq") for _ in range(2)]
        rs_tmp = self.carve(512, F32, None, "rs")
        rstds = [self.carve(512, F32, None, "rstd") for _ in range(2)]
        tmps = [self.carve(512, F32, None, "nt") for _ in range(2)]
        shift_i = 0 if which == 0 else 3
        n = 0
        for tb, (o, nt) in enumerate(TB):
            rstd = rstds[tb % 2]
            self.rstd_block(tb, sqs, rs_tmp, rstd)
            w = 0 if tb < 4 else 1
            for kc in range(KC):
                tmp = tmps[n % 2]
                n += 1
                self.stt(tmp[:, 0:nt], self.xT(kc, tb), V(self.gsc.ap[:, l, which, kc, w:w + 1], self.gsc.bufs),
                         rstd[:, 0:nt], ALU.mult, ALU.mult)
                self.act(self.hT(kc, tb), tmp[:, 0:nt], AF.Identity,
                         bias=V(self.modT.ap[:, l, shift_i, kc, w:w + 1], self.modT.bufs), scale=1.0)
        P.barrier()
        self.scr_off[0] = 0

    def ffn_phase(self, l, wg_d, wu_d, wd_d):
        P = self.P
        self.scr_off[0] = 0
        HC = NH // 2
        hid = self.sb_scr_hid()
        wgs = [self.carve(KC * 64, BF16, [KC, 128], "wg") for _ in range(3)]
        wus = [self.carve(KC * 64, BF16, [KC, 128], "wu") for _ in range(3)]
        wds = [self.carve(HC * 64, BF16, [HC, 128], "wd") for _ in range(2)]
        sil = [self.carve(512, F32, None, "sil") for _ in range(2)]
        n = 0
        nd = 0
        npb = 0
        for half in range(2):
            for jj in range(HC):
                j = half * HC + jj
                wg, wu = wgs[n % 3], wus[n % 3]
                n += 1
                self.dma(wg, wg_d[l, :, j * 128:(j + 1) * 128].rearrange("(k p) c -> p k c", p=128), eng="pool")
                self.dma(wu, wu_d[l, :, j * 128:(j + 1) * 128].rearrange("(k p) c -> p k c", p=128), eng="pool")
                for tb, (o, nt) in enumerate(TB):
                    pg = self.PS(npb % 4, nt)
                    pu = self.PS(4 + npb % 4, nt)
                    npb += 1
                    for kc in range(KC):
                        self.mm(pg, wg[:, kc, :], self.hT(kc, tb), start=(kc == 0), stop=(kc == KC - 1))
                    for kc in range(KC):
                        self.mm(pu, wu[:, kc, :], self.hT(kc, tb), start=(kc == 0), stop=(kc == KC - 1))
                    s = sil[npb % 2]
                    self.act(s[:, 0:nt], pg, AF.Silu)
                    self.tt(V(hid.ap[:, jj, o:o + nt], [self.hidb[jj][tb]]), s[:, 0:nt], pu, ALU.mult)
            for m in range(KC):
                wd = wds[nd % 2]
                nd += 1
                self.dma(wd, wd_d[l, half * HC * 128:(half + 1) * HC * 128, m * 128:(m + 1) * 128]
                         .rearrange("(j p) c -> p j c", p=128), eng="pool")
                for tb, (o, nt) in enumerate(TB):
                    po = self.PS(npb % 8, nt)
                    npb += 1
                    for jj in range(HC):
                        self.mm(po, wd[:, jj, :], V(hid.ap[:, jj, o:o + nt], [self.hidb[jj][tb]]),
                                start=(jj == 0), stop=(jj == HC - 1))
                    w = 0 if tb < 4 else 1
                    self.stt(self.xT(m, tb), po, V(self.modT.ap[:, l, 5, m, w:w + 1], self.modT.bufs),
                             self.xT(m, tb), ALU.mult, ALU.add)
        P.barrier()
        self.scr_off[0] = 0

    def sb_scr_hid(self):
        HC = NH // 2
        hid = self.carve(HC * T // 2, BF16, [HC, T], "hid")
        self.hidb = [[self.buf("hid") for _ in TB] for _ in range(HC)]
        return hid

    def final_phase(self, gfin):
        P = self.P
        self.scr_off[0] = 0
        sqs = [self.carve(256, BF16, None, "sq") for _ in range(2)]
        rs_tmp = self.carve(512, F32, None, "rs")
        rstds = [self.carve(512, F32, None, "rstd") for _ in range(2)]
        ys = [self.carve(512, F32, None, "y") for _ in range(3)]
        outs = [self.carve(4 * D, F32, [4, D], "os") for _ in range(2)]
        self.final_waits = []
        n = 0
        for tb in range(4):
            o, nt = TB[tb]
            rstd = rstds[tb % 2]
            self.rstd_block(tb, sqs, rs_tmp, rstd)
            ost = outs[tb % 2]
            for kc in range(KC):
                y = ys[n % 3]
                self.stt(y, self.xT(kc, tb), gfin[:, kc:kc + 1], rstd, ALU.mult, ALU.mult)
                pb = self.PS(2 + n % 6, 512)
                n += 1
                for t in range(4):
                    self.tr(pb[:, t * 128:(t + 1) * 128], y[:, t * 128:(t + 1) * 128], self.ident)
                dst = V(ost.ap[:, :, kc * 128:(kc + 1) * 128], ost.bufs)
                srcv = V(pb.ap.rearrange("p (t c) -> p t c", c=128), pb.bufs)
                self.copy(dst, srcv, eng="act" if kc % 2 else "dve")
            r = self.dma(self.out_d[o:o + nt, :].rearrange("(t p) d -> p t d", p=128), ost, eng="sp")
            self.final_waits.append(r)

    def load_w(self, dst, src_ap, eng="pool"):
        return self.dma(dst, src_ap, eng=eng)

    def mixer_phase(self, l):
        P = self.P
        nc = self.nc
        if not hasattr(self, "xsp_d"):
            self.xsp_d = nc.dram_tensor("xspill", [128, KC * T], F32).ap()
            self.xsp_b = self.buf("xsp")
        xall = V(self.pool_t[:, 0:self.XW], [b for row in self.xb for b in row])
        self.dma(V(self.xsp_d, [self.xsp_b]), xall, eng="sp", sembuf=self.xsp_b)
        P.barrier()
        self.scr_base = 0
        self.scr_lim = self.POOLW
        self.scr_off[0] = 0
        if not hasattr(self, "mix_d"):
            self.mix_d = nc.dram_tensor("mixspill", [128, KC, T], BF16).ap()
        self.mixb = [self.buf("mix") for _ in range(KC)]
        self.mla(l)
        P.barrier()
        self.scr_off[0] = 0
        self.na(l)
        P.barrier()
        self.scr_off[0] = 0
        self.gdn(l)
        P.barrier()
        self.dma(xall, V(self.xsp_d, [self.xsp_b]), eng="sp", sembuf=self.xsp_b)
        self.scr_base = self.XW
        self.scr_off[0] = 0
        mixs = self.carve(KC * T // 2, BF16, [KC, T], "mixs")
        mix_t = mixs.ap
        self.mix_t = mix_t
        for k in range(KC):
            self.dma(V(mix_t[:, k, :], [mixs.bufs[0]]), V(self.mix_d[:, k, :], [self.mixb[k]]), eng="sp" if k % 2 else "act",
                     sembuf=mixs.bufs[0])
        self.mixb = [mixs.bufs[0]] * KC
        if self.dbg and l == 0:
            self.dump_mix()
        wos = [self.carve(KC * 64, BF16, [KC, 128], "wo") for _ in range(2)]
        npb = 0
        for m in range(KC):
            wo = wos[m % 2]
            self.load_w(wo, self.wout_d[l, :, m * 128:(m + 1) * 128].rearrange("(k p) c -> p k c", p=128))
            for tb, (o, nt) in enumerate(TB):
                po = self.PS(npb % 8, nt)
                npb += 1
                for kk in range(KC):
                    self.mm(po, wo[:, kk, :], V(mix_t[:, kk, o:o + nt], [self.mixb[kk]]),
                            start=(kk == 0), stop=(kk == KC - 1))
                w = 0 if tb < 4 else 1
                self.stt(self.xT(m, tb), po, V(self.modT.ap[:, l, 2, m, w:w + 1], self.modT.bufs),
                         self.xT(m, tb), ALU.mult, ALU.add)
        P.barrier()
        self.scr_off[0] = 0
        self.scr_lim = self.POOLW

    def dump_mix(self):
        st = [self.carve(T, F32, None, "dst") for _ in range(2)]
        for k in range(KC):
            s_ = st[k % 2]
            self.copy(s_, V(self.mix_t[:, k, :], [self.mixb[k]]), eng="act")
            self.dma(self.dbg_d[k * 128:(k + 1) * 128, :], s_, eng="sp")

    def finish_attn(self, O_ps, n, pb, chunk, tok0, tmp):
        osb, rden, obuf = tmp
        self.copy(osb[0:65, 0:n], O_ps[0:65, 0:n], eng="act")
        den = self.PS(5, n, 64)
        self.mm(den, self.sel65[0:65, :], osb[0:65, 0:n])
        self.recip(rden[0:64, 0:n], den)
        self.tt(obuf[0:64, 0:n], osb[0:64, 0:n], rden[0:64, 0:n], ALU.mult)
        dst = V(self.mix_d[pb:pb + 64, chunk, tok0:tok0 + n], [self.mixb[chunk]])
        self.dma(dst, obuf[0:64, 0:n], eng="sp", sembuf=obuf.bufs[0])

    def attn_dense(self, QT, KT_fn, V_fn, key_tiles, n, scale, dst, st):
        pts, fin_tmp = st["pts"], st["fin"]
        O_ps = self.PS(3 + st["nO"] % 2, n, 65)
        st["nO"] += 1
        nk = len(key_tiles)
        LA = 2
        pend = []
        for i in range(nk + LA):
            if i < nk:
                t = key_tiles[i]
                S_ps = self.PS(st["nS"] % 3, n)
                pt = pts[st["nS"] % 3]
                st["nS"] += 1
                self.mm(S_ps, KT_fn(t), QT)
                pend.append((i, t, S_ps, pt))
            if i >= LA:
                j, t, S_ps, pt = pend.pop(0)
                self.act(pt[:, 0:n], S_ps, AF.Exp, scale=scale)
                self.mm(O_ps, V_fn(t), pt[:, 0:n], start=(j == 0), stop=(j == nk - 1))
        self.finish_attn(O_ps, n, dst[0], dst[1], dst[2], fin_tmp[st["nO"] % 2])

    def attn_state(self):
        pts = [self.carve(256, BF16, None, "pt") for _ in range(3)]
        fin = [(self.carve(512, F32, None, "osb"), self.carve(512, F32, None, "rden"),
                self.carve(256, BF16, None, "obuf")) for _ in range(2)]
        return {"pts": pts, "fin": fin, "nO": 0, "nS": 0}

    def mla(self, l):
        hT = self.hT
        SC = 96 ** -0.5
        w_in3 = [self.carve(KC * 64, BF16, [KC, 128], "wmi") for _ in range(3)]
        for c in range(3):
            self.load_w(w_in3[c], self.win_d[l, :, c * 128:(c + 1) * 128].rearrange("(k p) c -> p k c", p=128))
        wpe = self.carve(KC * 96, BF16, [KC, 2, 96], "wpe")
        self.load_w(wpe, self.wpe_d[l].rearrange("(k p) a c -> p k a c", p=128))
        wq2 = self.carve(2 * 384, BF16, [2, 2, 384], "wq2")
        self.load_w(wq2, self.wq2_d[l].rearrange("(k p) a c -> p k a c", p=128))
        wkv = self.carve(256, BF16, None, "wkv")
        self.load_w(wkv, self.wkv_d[l])
        rope = self.carve(2 * SEQ, F32, [2, SEQ], "rope")
        self.dma(rope, self.rope_d, eng="sp")
        mqn = self.carve(T, BF16, [2, T], "mqn")
        mkvn = self.carve(T // 2, BF16, None, "mkvn")
        peR = self.carve(T // 2, BF16, None, "peR")
        vaug = self.carve(18 * 4 * 65 // 2, BF16, [18, 4, 65], "vaug")
        self.memset(V(vaug.ap[:, :, :, 64:65], vaug.bufs), 1.0, eng="pool")
        raw = [self.carve(512, F32, None, "raw") for _ in range(3)]
        sqs = [self.carve(256, BF16, None, "sq") for _ in range(3)]
        rt = [self.carve(512, F32, None, "rt") for _ in range(4)]
        r1 = self.carve(512, F32, None, "r1")
        r2 = self.carve(512, F32, None, "r2")
        for tb, (o, nt) in enumerate(TB):
            pr = [self.PS(6, nt), self.PS(7, nt), self.PS(5, nt)]
            for c in range(3):
                for kc in range(KC):
                    self.mm(pr[c], w_in3[c][:, kc, :], hT(kc, tb), start=(kc == 0), stop=(kc == KC - 1))
                self.copy(raw[c][:, 0:nt], pr[c], eng="act" if c % 2 else "dve")
                self.act(sqs[c][:, 0:nt], raw[c][:, 0:nt], AF.Square)
            ps_q = self.PS(3, nt)
            self.mm(ps_q, self.ones_bf, sqs[0][:, 0:nt], start=True, stop=False)
            self.mm(ps_q, self.ones_bf, sqs[1][:, 0:nt], start=False, stop=True)
            ps_k = self.PS(4, nt)
            self.mm(ps_k, self.ones_bf, sqs[2][:, 0:nt])
            self.act(rt[0][:, 0:nt], ps_q, AF.Sqrt, bias=self.eps_t, scale=1.0 / 256)
            self.recip(rt[1][:, 0:nt], rt[0][:, 0:nt])
            self.act(rt[2][:, 0:nt], ps_k, AF.Sqrt, bias=self.eps_t, scale=1.0 / 128)
            self.recip(rt[3][:, 0:nt], rt[2][:, 0:nt])
            for c in range(2):
                self.stt(V(mqn.ap[:, c, o:o + nt], mqn.bufs), raw[c][:, 0:nt],
                         V(self.mlag.ap[:, l, c:c + 1], self.mlag.bufs), rt[1][:, 0:nt], ALU.mult, ALU.mult)
            self.stt(mkvn[:, o:o + nt], raw[2][:, 0:nt], V(self.mlag.ap[:, l, 2:3], self.mlag.bufs),
                     rt[3][:, 0:nt], ALU.mult, ALU.mult)
            pp = [self.PS(0, nt, 96), self.PS(1, nt, 96)]
            for a in range(2):
                for kc in range(KC):
                    self.mm(pp[a], wpe[:, kc, a, :], hT(kc, tb), start=(kc == 0), stop=(kc == KC - 1))
            if tb < 4:
                self.tt(r1[64:96, 0:nt], pp[0][64:96, :], rope[64:96, 0, o:o + nt], ALU.mult)
                self.tt(r2[64:96, 0:nt], pp[1][64:96, :], rope[64:96, 1, o:o + nt], ALU.mult)
                self.tt(peR[64:96, o:o + nt], r1[64:96, 0:nt], r2[64:96, 0:nt], ALU.add)
            else:
                self.copy(peR[64:96, o:o + nt], pp[0][64:96, :], eng="act")
        for t in range(18):
            pv = self.PS(6 + t % 2, 256)
            self.mm(pv, mkvn[:, t * 128:(t + 1) * 128], wkv[:, 256:512])
            self.copy(V(vaug.ap[:, t, :, 0:64], vaug.bufs), V(pv.ap.rearrange("p (h d) -> p h d", h=4), pv.bufs),
                      eng="act" if t % 2 else "dve")
        QTs = [self.carve(T // 2, BF16, None, "QT") for _ in range(2)]
        KTs = [self.carve(T // 2, BF16, None, "KT") for _ in range(2)]
        st = self.attn_state()
        for h in range(4):
            QT, KT = QTs[h % 2], KTs[h % 2]
            for tb, (o, nt) in enumerate(TB):
                pq = [self.PS(6, nt, 96), self.PS(7, nt, 96)]
                na_ = 2 if tb < 4 else 1
                for a in range(na_):
                    for c in range(2):
                        self.mm(pq[a], wq2[:, c, a, h * 96:(h + 1) * 96], V(mqn.ap[:, c, o:o + nt], mqn.bufs),
                                start=(c == 0), stop=(c == 1))
                if tb < 4:
                    self.copy(QT[0:64, o:o + nt], pq[0][0:64, :], eng="act")
                    self.tt(r1[64:96, 0:nt], pq[0][64:96, :], rope[64:96, 0, o:o + nt], ALU.mult)
                    self.tt(r2[64:96, 0:nt], pq[1][64:96, :], rope[64:96, 1, o:o + nt], ALU.mult)
                    self.tt(QT[64:96, o:o + nt], r1[64:96, 0:nt], r2[64:96, 0:nt], ALU.add)
                else:
                    self.copy(QT[0:96, o:o + nt], pq[0][0:96, :], eng="act")
                pk = self.PS(5, nt, 64)
                self.mm(pk, wkv[:, h * 64:(h + 1) * 64], mkvn[:, o:o + nt])
                self.copy(KT[0:64, o:o + nt], pk, eng="dve")
            self.copy(KT[64:96, :], peR[64:96, :], eng="act")
            KT_fn = lambda t, KT=KT: KT[0:96, t * 128:(t + 1) * 128]
            V_fn = lambda t, h=h: V(vaug.ap[:, t, h, :], vaug.bufs)
            for qb in range(4):
                self.attn_dense(QT[0:96, qb * 512:(qb + 1) * 512], KT_fn, V_fn, list(range(18)), 512, SC,
                                ((h % 2) * 64, h // 2, qb * 512), st)
            self.attn_dense(QT[0:96, SEQ:T], KT_fn, V_fn, [16, 17], 256, SC, ((h % 2) * 64, h // 2, SEQ), st)

    def na(self, l):
        hT = self.hT
        SC = 64 ** -0.5
        nqk = self.carve(2 * T, BF16, [2, 2, T], "nqk")
        vaug = self.carve(18 * 4 * 65 // 2, BF16, [18, 4, 65], "vaugn")
        self.memset(V(vaug.ap[:, :, :, 64:65], vaug.bufs), 1.0, eng="pool")
        ws = [self.carve(KC * 64, BF16, [KC, 128], "wn") for _ in range(2)]
        n = 0
        for qk in range(2):
            for c in range(2):
                w = ws[n % 2]
                n += 1
                col = 416 + qk * 256 + c * 128
                self.load_w(w, self.win_d[l, :, col:col + 128].rearrange("(k p) c -> p k c", p=128))
                for tb, (o, nt) in enumerate(TB):
                    pp = self.PS(6 + tb % 2, nt)
                    for kc in range(KC):
                        self.mm(pp, w[:, kc, :], hT(kc, tb), start=(kc == 0), stop=(kc == KC - 1))
                    self.copy(V(nqk.ap[:, qk, c, o:o + nt], nqk.bufs), pp, eng="act" if tb % 2 else "dve")
        wv = self.carve(KC * 128, BF16, [KC, 256], "wnv")
        self.load_w(wv, self.win_d[l, :, 928:1184].rearrange("(k p) c -> p k c", p=128))
        for t in range(18):
            pv = self.PS(6 + t % 2, 256)
            for kc in range(KC):
                self.mm(pv, V(self.hT_t[:, kc, t * 128:(t + 1) * 128], [self.hb[kc][t // 4]]), wv[:, kc, :],
                        start=(kc == 0), stop=(kc == KC - 1))
            self.copy(V(vaug.ap[:, t, :, 0:64], vaug.bufs), V(pv.ap.rearrange("p (h d) -> p h d", h=4), pv.bufs),
                      eng="act" if t % 2 else "dve")
        nabs = self.carve(4 * 19 * 64, F32, [4, 19, 64], "nabs")
        self.dma(nabs, self.nab_d[l], eng="sp")
        E = self.carve(4 * 19 * 32, BF16, [4, 19, 64], "E")
        self.act(E, nabs, AF.Exp)
        st = self.attn_state()
        tmps = [self.carve(7 * 32, BF16, [7, 64], "ntmp") for _ in range(3)]
        ptl = [self.carve(5 * 32, BF16, [5, 64], "ptl") for _ in range(3)]
        nn = 0
        for h in range(4):
            c, pb = h // 2, (h % 2) * 64
            V_fn = lambda t, h=h: V(vaug.ap[:, t, h, :], vaug.bufs)
            LA = 2
            pend = []
            for rr_ in range(32 + LA):
                if rr_ < 32:
                    r = rr_
                    sr = min(max(r - 4, 0), 24)
                    if sr % 2 == 0:
                        nloc, t0 = 4, sr // 2
                        off = sr - r + 7
                        Esel = V(E.ap[:, h, off:off + 7:2, :], E.bufs)
                    else:
                        nloc, t0 = 5, (sr - 1) // 2
                        Esel = V(E.ap[:, h, 14:19, :], E.bufs)
                    tiles = [t0 + i for i in range(nloc)] + [16, 17]
                    ntl = len(tiles)
                    S_ps = self.PS(nn % 3, ntl * 64)
                    tmp = tmps[nn % 3]
                    pl = ptl[nn % 3]
                    nn += 1
                    q = V(nqk.ap[pb:pb + 64, 0, c, r * 64:(r + 1) * 64], nqk.bufs)
                    for i, t in enumerate(tiles):
                        k = V(nqk.ap[pb:pb + 64, 1, c, t * 128:(t + 1) * 128], nqk.bufs)
                        self.mm(S_ps[:, i * 64:(i + 1) * 64], k, q)
                    pend.append((r, tiles, nloc, Esel, S_ps, tmp, pl))
                if rr_ >= LA:
                    r, tiles, nloc, Esel, S_ps, tmp, pl = pend.pop(0)
                    ntl = len(tiles)
                    if r % 8 == 0:
                        O_ps = self.PS(3 + st["nO"] % 2, 512, 65)
                        st["nO"] += 1
                    self.act(V(tmp.ap[:, 0:ntl, :], tmp.bufs), V(S_ps.ap.rearrange("p (t q) -> p t q", q=64), S_ps.bufs),
                             AF.Exp, scale=SC)
                    self.tt(V(pl.ap[:, 0:nloc, :], pl.bufs), V(tmp.ap[:, 0:nloc, :], tmp.bufs), Esel, ALU.mult)
                    Or = O_ps[0:65, (r % 8) * 64:(r % 8 + 1) * 64]
                    for i, t in enumerate(tiles):
                        if i < nloc:
                            p_ = V(pl.ap[:, i, :], pl.bufs)
                        else:
                            p_ = V(tmp.ap[:, i, :], tmp.bufs)
                        self.mm(Or, V_fn(t), p_, start=(i == 0), stop=(i == ntl - 1))
                    if r % 8 == 7:
                        self.finish_attn(O_ps, 512, pb, 2 + c, (r - 7) * 64, st["fin"][st["nO"] % 2])
            KT_fn = lambda t, c=c, pb=pb: V(nqk.ap[pb:pb + 64, 1, c, t * 128:(t + 1) * 128], nqk.bufs)
            self.attn_dense(V(nqk.ap[pb:pb + 64, 0, c, SEQ:T], nqk.bufs), KT_fn, V_fn, [16, 17], 256, SC,
                            (pb, 2 + c, SEQ), st)

    def gdn(self, l):
        hT = self.hT
        NCH = 36
        DK = 128 ** -0.5
        gc_ = self.gdnc
        U = [gc_[0:64, d * 320:d * 320 + 64] for d in range(2)]
        SU = [gc_[0:64, d * 320 + 64:d * 320 + 128] for d in range(2)]
        MN = [gc_[0:64, d * 320 + 128:d * 320 + 320] for d in range(2)]
        I64 = gc_[0:64, 640:704]
        ONE = gc_[0:64, 704:832]
        ORD = [[32, 33, 34, 35] + list(range(32)), [35, 34, 33, 32] + list(range(31, -1, -1))]
        psD = lambda n0, n1, p=64: V(self.psD_t[0:p, n0:n1], [self.psb[6], self.psb[7]])
        wab = self.carve(KC * 8, BF16, [KC, 16], "wab")
        self.load_w(wab, self.win_d[l, :, 3232:3248].rearrange("(k p) c -> p k c", p=128))
        ab = self.carve(NCH * 16, F32, [NCH, 16], "ab")
        for n in range(NCH):
            for kc in range(KC):
                self.mm(psD(n * 16, n * 16 + 16),
                        V(self.hT_t[:, kc, n * 64:(n + 1) * 64], [self.hb[kc][n // 8]]), wab[:, kc, :],
                        start=(kc == 0), stop=(kc == KC - 1))
        self.copy(ab[0:64], V(psD(0, NCH * 16).ap.rearrange("p (n c) -> p n c", c=16), [self.psb[6], self.psb[7]]))
        abc = self.gabc
        nA = self.carve(8, F32, None, "nA")
        self.act(nA[0:64], V(abc.ap[0:64, l, 0:8], abc.bufs), AF.Exp)
        self.ts(nA[0:64], nA[0:64], -1.0, ALU.mult)
        g = self.carve(NCH * 8, F32, [NCH, 8], "g")
        beta = self.carve(NCH * 8, F32, [NCH, 8], "beta")
        sp = self.carve(NCH * 8, F32, [NCH, 8], "sp")
        dtb = V(abc.ap[0:64, l, 8:16].unsqueeze(1).to_broadcast([64, NCH, 8]), abc.bufs)
        self.tt(sp[0:64], V(ab.ap[0:64, :, 0:8], ab.bufs), dtb, ALU.add)
        self.act(sp[0:64], sp[0:64], AF.Exp)
        self.act(sp[0:64], sp[0:64], AF.Ln, bias=self.one_t[0:64], scale=1.0)
        self.tt(g[0:64], sp[0:64], V(nA.ap[0:64].unsqueeze(1).to_broadcast([64, NCH, 8]), nA.bufs), ALU.mult)
        self.act(beta[0:64], V(ab.ap[0:64, :, 8:16], ab.bufs), AF.Sigmoid)
        gcs = self.carve(NCH * 8, F32, [NCH, 8], "gcs")
        for d in range(2):
            pc = self.PS(d, NCH * 4, 64)
            self.mm(pc, U[d], V(g.ap[0:64, :, d * 4:(d + 1) * 4], g.bufs))
            self.copy(V(gcs.ap[0:64, :, d * 4:(d + 1) * 4], gcs.bufs),
                      V(pc.ap.rearrange("p (n c) -> p n c", c=4), pc.bufs))
        ptot = self.PS(2, NCH * 8, 128)
        self.mm(ptot, ONE, V(g.ap[0:64].rearrange("p n c -> p (n c)"), g.bufs))
        glast = self.carve(NCH * 8, F32, [NCH, 8], "glast")
        self.act(V(glast.ap.rearrange("p n c -> p (n c)"), glast.bufs), ptot, AF.Exp)
        etail = self.carve(NCH * 8, F32, [NCH, 8], "etail")
        self.tt(V(etail.ap[0:64].rearrange("p n c -> p (n c)"), etail.bufs), ptot[0:64, :],
                V(gcs.ap[0:64].rearrange("p n c -> p (n c)"), gcs.bufs), ALU.subtract)
        self.act(etail[0:64], etail[0:64], AF.Exp)
        eg = self.carve(NCH * 8, F32, [NCH, 8], "eg")
        self.act(eg[0:64], gcs[0:64], AF.Exp)
        s_kbg = self.carve(NCH * 8, F32, [NCH, 8], "skbg")
        self.tt(s_kbg[0:64], beta[0:64], eg[0:64], ALU.mult)
        s_q = self.carve(NCH * 8, F32, [NCH, 8], "sq_")
        self.ts(s_q[0:64], eg[0:64], DK, ALU.mult)
        import os
        STOP = float(os.environ.get("GDN_STOP", "99"))
        if STOP <= 0:
            return
        kT = self.carve(T // 2, BF16, None, "kT")
        qT = self.carve(T // 2, BF16, None, "qT")
        szT = self.carve(T // 2, BF16, None, "szT")
        k_tok = self.carve(NCH * 64, BF16, [NCH, 128], "ktok")
        v_tok = self.carve(NCH * 64, BF16, [NCH, 128], "vtok")
        kbT_sh = self.carve(T // 2, BF16, None, "kbT")
        oT = self.carve(T, F32, None, "oT")
        oTb = [self.buf("oT") for _ in range(NCH)]
        off_r = self.scr_base + self.scr_off[0]
        rawp = self.carve(T + 8, F32, None, "rawp")
        acc = self.carve(T, F32, None, "acc")
        XYall = self.pool_t[:, off_r:off_r + 9 * 512].rearrange("p (g a i c) -> p g a i c", g=9, a=2, i=4)
        XYb = [self.buf("XYb") for _ in range(9)]
        Qgb = [self.buf("Qgb") for _ in range(9)]
        perd = []
        for d in range(2):
            perd.append(dict(
                kbT=kbT_sh, qdT=self.carve(T // 2, BF16, None, "qdT"),
                QT=self.carve(NCH * 32, BF16, [NCH, 64], "QTa"), AiT=self.carve(NCH * 32, BF16, [NCH, 64], "AiT"),
                nwT=self.carve(NCH * 32, BF16, [NCH, 64], "nwT"),
                S=self.carve(128, F32, None, "S"), Sb=self.carve(64, BF16, None, "Sb")))
        wq_ = [self.carve(KC * 64, BF16, [KC, 128], "wg_") for _ in range(2)]
        sqb = [self.carve(256, BF16, None, "sqb") for _ in range(2)]
        rr = [self.carve(512, F32, None, "rr") for _ in range(2)]
        diag = [self.carve(512, F32, [8, 64], "diag") for _ in range(2)]
        GU = [self.carve(256, F32, [4, 64], "GU") for _ in range(2)]
        Dall = [self.carve(768, F32, [4, 192], "Dall") for _ in range(2)]
        Qall = self.carve(9 * 256, F32, [9, 4, 64], "Qall").ap
        kbg = [self.carve(256, BF16, [4, 128], "kbg") for _ in range(2)]
        vbt = [self.carve(64, BF16, None, "vb") for _ in range(4)]
        ktt = [self.carve(64, BF16, None, "kt") for _ in range(4)]
        vnw = [self.carve(64, BF16, None, "vn") for _ in range(4)]
        gts = [self.carve(512, F32, None, "gt") for _ in range(2)]
        gos = [self.carve(256, BF16, None, "go") for _ in range(2)]
        self.memset(rawp[:, 0:2], 0.0)
        self.memset(rawp[:, 2 + SEQ:2 + SEQ + 4], 0.0)
        self.memset(rawp[:, T + 6:T + 8], 0.0)
        nw = 0
        for h in range(4):
            for which in range(4):
                w = wq_[nw % 2]
                nw += 1
                col = 1184 + which * 512 + h * 128
                self.load_w(w, self.win_d[l, :, col:col + 128].rearrange("(k p) c -> p k c", p=128))
                for tb, (o, nt) in enumerate(TB):
                    pp = self.PS(tb % 2, nt)
                    for kc in range(KC):
                        self.mm(pp, w[:, kc, :], hT(kc, tb), start=(kc == 0), stop=(kc == KC - 1))
                    if which == 3:
                        self.act(szT[:, o:o + nt], pp, AF.Silu)
                    else:
                        oo = 2 + o if tb < 4 else 6 + o
                        self.copy(rawp[:, oo:oo + nt], pp, eng="act")
                if which == 3:
                    continue
                cw = lambda j: V(self.gcw.ap[:, l, which, h, j:j + 1], self.gcw.bufs)
                for (o0, n0, p0) in ((0, SEQ, 0), (SEQ, CTXL, SEQ + 4)):
                    a_ = acc[:, o0:o0 + n0]
                    self.ts(a_, rawp[:, p0:p0 + n0], cw(0), ALU.mult)
                    for j in range(1, 5):
                        self.stt(a_, rawp[:, p0 + j:p0 + j + n0], cw(j), a_, ALU.mult, ALU.add)
                self.act(acc, acc, AF.Silu)
                if which == 2:
                    for g4 in range(0, NCH, 4):
                        pt_ = self.PS(4 + (g4 // 4) % 2, 512, 64)
                        for i in range(4):
                            n = g4 + i
                            self.tr(pt_[:, i * 128:(i + 1) * 128], acc[:, n * 64:(n + 1) * 64], self.ident)
                        self.copy(V(v_tok.ap[0:64, g4:g4 + 4, :], v_tok.bufs),
                                  V(pt_.ap.rearrange("p (n c) -> p n c", c=128), pt_.bufs), eng="act")
                    continue
                dst = qT if which == 0 else kT

                def n1(tb):
                    o, nt = TB[tb]
                    sq = sqb[tb % 2]
                    self.act(sq[:, 0:nt], acc[:, o:o + nt], AF.Square)
                    pn = self.PS(2 + tb % 2, nt)
                    self.mm(pn, self.ones_bf, sq[:, 0:nt])

                def n2(tb):
                    o, nt = TB[tb]
                    pn = self.PS(2 + tb % 2, nt)
                    r_ = rr[tb % 2]
                    self.act(r_[:, 0:nt], pn, AF.Sqrt, bias=self.eps_t, scale=1.0)
                    self.recip(r_[:, 0:nt], r_[:, 0:nt])

                def n3(tb, dst=dst):
                    o, nt = TB[tb]
                    r_ = rr[tb % 2]
                    self.tt(dst[:, o:o + nt], acc[:, o:o + nt], r_[:, 0:nt], ALU.mult)

                nb_ = len(TB)
                for i in range(nb_ + 2):
                    if i < nb_:
                        n1(i)
                    if 1 <= i <= nb_:
                        n2(i - 1)
                    if i >= 2:
                        n3(i - 2)
            for (src, dstt) in ((kT, k_tok),):
                for g8 in range(0, NCH, 8):
                    cnt = min(8, NCH - g8)
                    pt_ = V(self.ps[4 + (g8 // 8) % 2][0:64, 0:512].bitcast(BF16), [self.psb[4 + (g8 // 8) % 2]])
                    for i in range(cnt):
                        n = g8 + i
                        self.tr(pt_[:, i * 128:(i + 1) * 128], src[:, n * 64:(n + 1) * 64], self.ident_bf)
                    self.copy(V(dstt.ap[0:64, g8:g8 + cnt, :], dstt.bufs),
                              V(pt_.ap[:, 0:cnt * 128].rearrange("p (n c) -> p n c", c=128), pt_.bufs), eng="act")
            self.P.barrier()
            if STOP <= 1:
                continue
            for d in range(2):
                c = d * 4 + h
                pd = perd[d]
                jobs = []
                for (srcT, scal, dstT) in ((kT, beta, pd["kbT"]), (qT, s_q, pd["qdT"])):
                    for n0 in range(0, NCH, 8):
                        jobs.append((srcT, scal, dstT, n0, min(8, NCH - n0)))
                pendb = []
                for bi in range(len(jobs) + 1):
                    if bi < len(jobs):
                        srcT, scal, dstT, n0, cnt = jobs[bi]
                        dg = diag[bi % 2]
                        self.tt(V(dg.ap[0:64, 0:cnt, :], dg.bufs),
                                V(I64.ap.unsqueeze(1).to_broadcast([64, cnt, 64]), I64.bufs),
                                V(scal.ap[0:64, n0:n0 + cnt, c:c + 1].to_broadcast([64, cnt, 64]), scal.bufs), ALU.mult)
                        pb_ = self.PS(bi % 2, cnt * 64)
                        self.mm(pb_, ONE, V(dg.ap[0:64, 0:cnt, :].rearrange("p n c -> p (n c)"), dg.bufs))
                        pendb.append((srcT, dstT, n0, cnt, pb_))
                    if bi >= 1:
                        srcT, dstT, n0, cnt, pb_ = pendb.pop(0)
                        self.tt(dstT[:, n0 * 64:(n0 + cnt) * 64], srcT[:, n0 * 64:(n0 + cnt) * 64], pb_, ALU.mult)
                if STOP <= 2.1:
                    continue
                NG = NCH // 4
                def setup_front(gi):
                    n0 = gi * 4
                    gu, da = GU[gi % 2], Dall[gi % 2]
                    self.tt(gu[0:64], V(U[d].ap.unsqueeze(1).to_broadcast([64, 4, 64]), U[d].bufs),
                            V(g.ap[0:64, n0:n0 + 4, c:c + 1].to_broadcast([64, 4, 64]), g.bufs), ALU.mult)
                    for i in range(4):
                        b0 = i * 256
                        for rg in range(3):
                            reg = psD(b0 + rg * 64, b0 + rg * 64 + 64)
                            self.mm(reg, I64, MN[d][:, rg * 64:(rg + 1) * 64], start=True, stop=False)
                            if rg == 0:
                                self.mm(reg, V(gu.ap[0:64, i, :], gu.bufs), SU[d], start=False, stop=True)
                            else:
                                self.mm(reg, SU[d], V(gu.ap[0:64, i, :], gu.bufs), start=False, stop=True)
                    pk = self.PS(5 if gi % 2 == 0 else 3, 512, 64)
                    pa = self.PS(4 if gi % 2 == 0 else 2, 256, 64)
                    for i in range(4):
                        n = n0 + i
                        self.mm(pk[:, i * 128:i * 128 + 64], pd["kbT"][:, n * 64:(n + 1) * 64], kT[:, n * 64:(n + 1) * 64])
                        self.mm(pk[:, i * 128 + 64:i * 128 + 128], kT[:, n * 64:(n + 1) * 64], pd["kbT"][:, n * 64:(n + 1) * 64])
                        self.mm(pa[:, i * 64:(i + 1) * 64], kT[:, n * 64:(n + 1) * 64], qT[:, n * 64:(n + 1) * 64])
                    self.act(da[0:64], V(psD(0, 1024).ap.rearrange("p (i c) -> p i c", c=256)[:, :, 0:192],
                                         [self.psb[6], self.psb[7]]), AF.Exp)
                    return pk, pa

                def setup_back(gi, pk, pa):
                    n0 = gi * 4
                    da = Dall[gi % 2]
                    pk3 = V(pk.ap.rearrange("p (i c) -> p i c", c=128), pk.bufs)
                    X = V(XYall[0:64, gi, 0], [XYb[gi]])
                    Y = V(XYall[0:64, gi, 1], [XYb[gi]])
                    Q = V(Qall[0:64, gi], [Qgb[gi]])
                    self.stt(X, pk3[:, :, 0:64], -1.0, V(da.ap[0:64, :, 0:64], da.bufs), ALU.mult, ALU.mult)
                    self.stt(Y, pk3[:, :, 64:128], -1.0, V(da.ap[0:64, :, 64:128], da.bufs), ALU.mult, ALU.mult)
                    self.stt(V(pd["AiT"].ap[0:64, n0:n0 + 4, :], pd["AiT"].bufs),
                             V(pa.ap.rearrange("p (i c) -> p i c", c=64), pa.bufs), DK,
                             V(da.ap[0:64, :, 128:192], da.bufs), ALU.mult, ALU.mult)
                    self.tt(Q, Y, V(I64.ap.unsqueeze(1).to_broadcast([64, 4, 64]), I64.bufs), ALU.add, eng="dve")

                prev = None
                for gi in range(NG + 1):
                    cur = None
                    if gi < NG:
                        cur = (gi,) + setup_front(gi)
                    if prev is not None:
                        setup_back(*prev)
                    prev = cur
                if STOP <= 2.2:
                    continue
                nps = 0
                for m in range(1, 6):
                    for gi in range(NG):
                        pxy = self.PS(nps % 3, 512, 64)
                        nps += 1
                        Xi = lambda i: V(XYall[0:64, gi, 0, i, :], [XYb[gi]])
                        Yi = lambda i: V(XYall[0:64, gi, 1, i, :], [XYb[gi]])
                        for i in range(4):
                            self.mm(pxy[:, i * 64:(i + 1) * 64], Yi(i), Xi(i))
                        nc_ = 256
                        if m < 5:
                            nc_ = 512
                            for i in range(4):
                                self.mm(pxy[:, 256 + i * 64:256 + (i + 1) * 64], Xi(i), Yi(i))
                        self.copy(V(XYall[0:64, gi].rearrange("p a i c -> p (a i c)")[:, 0:nc_], [XYb[gi]]), pxy[:, 0:nc_],
                                  eng="act")
                    for gi in range(NG):
                        pq_ = self.PS(3 + gi % 2, 256, 64)
                        for i in range(4):
                            self.mm(pq_[:, i * 64:(i + 1) * 64], V(XYall[0:64, gi, 0, i, :], [XYb[gi]]),
                                    V(Qall[0:64, gi, i, :], [Qgb[gi]]))
                        Qf_ = V(Qall[0:64, gi].rearrange("p i c -> p (i c)"), [Qgb[gi]])
                        self.tt(Qf_, Qf_, pq_, ALU.add)
                if STOP <= 2.3:
                    continue
                for gi in range(NG + 1):
                    if gi < NG:
                        n0 = gi * 4
                        self.copy(V(pd["QT"].ap[0:64, n0:n0 + 4, :], pd["QT"].bufs), V(Qall[0:64, gi], [Qgb[gi]]), eng="act")
                        kb_ = kbg[gi % 2]
                        self.tt(kb_[0:64], V(k_tok.ap[0:64, n0:n0 + 4, :], k_tok.bufs),
                                V(s_kbg.ap[0:64, n0:n0 + 4, c:c + 1].to_broadcast([64, 4, 128]), s_kbg.bufs), ALU.mult,
                                eng="dve")
                    if gi >= 1:
                        g1 = gi - 1
                        n0 = g1 * 4
                        kb_ = kbg[g1 % 2]
                        pw = self.PS(g1 % 2, 256, 128)
                        for i in range(4):
                            n = n0 + i
                            self.mm(pw[:, i * 64:(i + 1) * 64], V(kb_.ap[0:64, i, :], kb_.bufs),
                                    V(pd["QT"].ap[0:64, n, :], pd["QT"].bufs))
                        self.ts(V(pd["nwT"].ap[:, n0:n0 + 4, :].rearrange("p i c -> p (i c)"), pd["nwT"].bufs), pw, -1.0, ALU.mult)
            self.P.barrier()
            if STOP <= 2:
                continue
            self.memset(oT, 0.0, eng="dve")
            for d in range(2):
                self.memset(perd[d]["S"], 0.0)
                self.memset(perd[d]["Sb"], 0.0)
            def prep_step(step):
                out = []
                for d in range(2):
                    n = ORD[d][step]
                    c = d * 4 + h
                    pd = perd[d]
                    sl = (step % 2) * 2 + d
                    vb, kt, vn = vbt[sl], ktt[sl], vnw[sl]
                    self.ts(vb[0:64], V(v_tok.ap[0:64, n, :], v_tok.bufs), V(beta.ap[0:64, n, c:c + 1], beta.bufs),
                            ALU.mult, eng="dve")
                    self.ts(kt[0:64], V(k_tok.ap[0:64, n, :], k_tok.bufs), V(etail.ap[0:64, n, c:c + 1], etail.bufs),
                            ALU.mult, eng="dve")
                    out.append((d, n, c, pd, vb, kt, vn))
                return out

            nxt = prep_step(0)
            for step in range(NCH):
                ctxs = nxt
                pvs, pos = {}, {}
                for (d, n, c, pd, vb, kt, vn) in ctxs:
                    pv_ = self.PS(d * 3, 128, 64)
                    self.mm(pv_, V(pd["QT"].ap[0:64, n, :], pd["QT"].bufs), vb[0:64], start=True, stop=False)
                    self.mm(pv_, V(pd["nwT"].ap[:, n, :], pd["nwT"].bufs), pd["Sb"], start=False, stop=True)
                    po_ = self.PS(d * 3 + 1, 64, 128)
                    self.mm(po_, pd["Sb"], pd["qdT"][:, n * 64:(n + 1) * 64], start=True, stop=False)
                    pvs[d], pos[d] = pv_, po_
                for (d, n, c, pd, vb, kt, vn) in ctxs:
                    self.copy(vn[0:64], pvs[d], eng="act")
                pss = {}
                for (d, n, c, pd, vb, kt, vn) in ctxs:
                    self.mm(pos[d], vn[0:64], V(pd["AiT"].ap[0:64, n, :], pd["AiT"].bufs), start=False, stop=True)
                    ps_ = self.PS(d * 3 + 2, 128, 128)
                    self.mm(ps_, kt[0:64], vn[0:64])
                    pss[d] = ps_
                if step + 1 < NCH:
                    nxt = prep_step(step + 1)
                for (d, n, c, pd, vb, kt, vn) in ctxs:
                    self.stt(pd["Sb"], pd["S"], V(glast.ap[:, n, c:c + 1], glast.bufs), pss[d], ALU.mult, ALU.add)
                for (d, n, c, pd, vb, kt, vn) in ctxs:
                    self.stt(pd["S"], pd["S"], V(glast.ap[:, n, c:c + 1], glast.bufs), pss[d], ALU.mult, ALU.add)
                for (d, n, c, pd, vb, kt, vn) in ctxs:
                    ov = V(oT.ap[:, n * 64:(n + 1) * 64], [oTb[n]])
                    self.tt(ov, ov, pos[d], ALU.add)
            if STOP <= 3:
                continue
            def ovb_(tb):
                o, nt = TB[tb]
                return V(oT.ap[:, o:o + nt], [oTb[n] for n in range(o // 64, (o + nt) // 64)] + oT.bufs)

            def g1(tb):
                o, nt = TB[tb]
                sq = sqb[tb % 2]
                self.act(sq[:, 0:nt], ovb_(tb), AF.Square)
                pn = self.PS(6 + tb % 2, nt)
                self.mm(pn, self.ones_bf, sq[:, 0:nt])

            def g2(tb):
                o, nt = TB[tb]
                pn = self.PS(6 + tb % 2, nt)
                r_ = rr[tb % 2]
                self.act(r_[:, 0:nt], pn, AF.Sqrt, bias=self.eps_t, scale=1.0 / 128)
                self.recip(r_[:, 0:nt], r_[:, 0:nt])

            def g3(tb):
                o, nt = TB[tb]
                r_ = rr[tb % 2]
                gt_ = gts[tb % 2]
                self.stt(gt_[:, 0:nt], ovb_(tb), V(self.ggo.ap[:, l:l + 1], self.ggo.bufs), r_[:, 0:nt], ALU.mult, ALU.mult)
                go = gos[tb % 2]
                self.tt(go[:, 0:nt], gt_[:, 0:nt], szT[:, o:o + nt], ALU.mult)
                self.dma(V(self.mix_d[:, 4 + h, o:o + nt], [self.mixb[4 + h]]), go[:, 0:nt], eng="sp", sembuf=go.bufs[0])

            nb_ = len(TB)
            for i in range(nb_ + 2):
                if i < nb_:
                    g1(i)
                if 1 <= i <= nb_:
                    g2(i - 1)
                if i >= 2:
                    g3(i - 2)
            self.P.barrier()

def host_inputs(inputs, b, mixers=True):
    f = lambda a: np.ascontiguousarray(a, dtype=np.float32)
    m = {}
    m["x"] = f(inputs["x"][b])
    m["ctx"] = f(inputs["ctx"][b])
    cc = np.stack([inputs["c"][b], inputs["c_ctx"]], axis=-1)
    m["cc"] = f(cc.reshape(KC, 128, 2).transpose(1, 0, 2))
    m["w_ada"] = f(inputs["w_ada"])
    m["b_adaT"] = f(inputs["b_ada"].reshape(DEPTH, 48, 128).transpose(2, 0, 1))
    gv = np.stack([inputs["g_mix"], inputs["g_ffn"]], axis=1)
    m["gvecs"] = f(gv.reshape(DEPTH, 2, KC, 128).transpose(3, 0, 1, 2))
    m["g_finalT"] = f(inputs["g_final"].reshape(KC, 128).T)
    m["w_gate"] = f(inputs["w_gate"])
    m["w_up"] = f(inputs["w_up"])
    m["w_down"] = f(inputs["w_down"])
    m["ident"] = np.eye(128, dtype=np.float32)
    if not mixers:
        return m
    w_in = inputs["w_in"]
    m["w_in"] = f(w_in)
    idx = np.arange(32)
    a_, hf, fr = idx // 16, (idx // 8) % 2, idx % 8
    partner = a_ * 16 + (1 - hf) * 8 + fr
    wpe = np.zeros((DEPTH, D, 2, 96), np.float32)
    wpe[:, :, 0, 64:96] = w_in[:, :, 384:416]
    wpe[:, :, 1, 64:96] = w_in[:, :, 384 + partner]
    m["w_pe2"] = wpe
    wq = inputs["mla_w_q_up"]
    wq2 = np.stack([wq, wq], axis=2).astype(np.float32)
    for h in range(4):
        wq2[:, :, 1, h * 96 + 64 + idx] = wq[:, :, h * 96 + 64 + partner]
    m["w_q2"] = f(wq2)
    wkv = inputs["mla_w_kv_up"].reshape(DEPTH, 128, 4, 128)
    m["w_kv_nv"] = f(np.concatenate([wkv[..., :64].reshape(DEPTH, 128, 256), wkv[..., 64:].reshape(DEPTH, 128, 256)], -1))
    gq = inputs["mla_g_q"].reshape(DEPTH, 2, 128)
    mg = np.concatenate([gq, inputs["mla_g_kv"].reshape(DEPTH, 1, 128)], axis=1)
    m["mla_g"] = f(mg.transpose(2, 0, 1))
    m["ropeCS"] = _rope_tables()
    sel = np.zeros((128, 64), np.float32)
    sel[64, :] = 1.0
    m["sel65"] = sel
    m["nab"] = _na_tables(inputs["na_rel_bias"])
    m["w_out"] = f(inputs["w_out"])
    m["gdnc"] = _gdn_consts()
    ab = np.concatenate([inputs["dn_a_log"].reshape(DEPTH, 8), inputs["dn_dt_bias"].reshape(DEPTH, 8)], axis=1)
    m["gdn_ab"] = f(np.broadcast_to(ab[None], (128, DEPTH, 16)))
    cw = inputs["dn_conv_w"].reshape(DEPTH, 5, 3, 4, 128)
    m["gdn_cw"] = f(cw.transpose(4, 0, 2, 3, 1))
    m["gdn_go"] = f(inputs["dn_g_out"].T)
    return m


def _gdn_consts():
    if "gdn" in _CONST:
        return _CONST["gdn"]
    NEG = -30000.0
    t = np.arange(64)[:, None]
    i = np.arange(64)[None, :]
    out = np.zeros((128, 832), np.float32)
    for d in range(2):
        U = (t <= i) if d == 0 else (t >= i)
        SU = (t > i) if d == 0 else (t < i)
        Ma = (i < t) if d == 0 else (i > t)
        Mb = (t < i) if d == 0 else (t > i)
        Mc = (t <= i) if d == 0 else (t >= i)
        base = d * 320
        out[0:64, base:base + 64] = U
        out[0:64, base + 64:base + 128] = SU
        out[0:64, base + 128:base + 192] = np.where(Ma, 0.0, NEG)
        out[0:64, base + 192:base + 256] = np.where(Mb, 0.0, NEG)
        out[0:64, base + 256:base + 320] = np.where(Mc, 0.0, NEG)
    out[0:64, 640:704] = np.eye(64)
    out[0:64, 704:832] = 1.0
    _CONST["gdn"] = out
    return out


_CONST = {}


def _rope_tables():
    if "rope" in _CONST:
        return _CONST["rope"]
    t = np.arange(SEQ)
    pos = np.stack([t // 64, t % 64], axis=-1).astype(np.float32)
    inv = np.power(np.float32(10000.0), -np.arange(8, dtype=np.float32) / np.float32(8)).astype(np.float32)
    ang = (pos[:, :, None] * inv).astype(np.float32)
    cos, sin = np.cos(ang).astype(np.float32), np.sin(ang).astype(np.float32)
    tab = np.zeros((128, 2, SEQ), np.float32)
    for i in range(32):
        a_, hf, fr = i // 16, (i // 8) % 2, i % 8
        tab[64 + i, 0, :] = cos[:, a_, fr]
        tab[64 + i, 1, :] = sin[:, a_, fr] * (-1.0 if hf == 0 else 1.0)
    _CONST["rope"] = tab
    return tab


def _na_tables(rel_bias):
    NEG = np.float32(-30000.0)
    kc = np.arange(64)[:, None]
    qc = np.arange(64)[None, :]
    cs = np.clip(qc - 8, 0, 48)
    ok = (kc >= cs) & (kc < cs + 16)
    dc = np.clip(kc - qc + 15, 0, 30)
    tb = rel_bias[:, :, :, dc]
    tb = np.where(ok[None, None, None], tb, NEG).astype(np.float32)
    mask = np.full((DEPTH, 4, 64, 64), NEG, np.float32)
    tiles = []
    for dra in range(14):
        tiles.append(np.concatenate([tb[:, :, dra], tb[:, :, dra + 1]], axis=2))
    tiles.append(np.concatenate([mask, tb[:, :, 3]], axis=2))
    for dra in (4, 6, 8):
        tiles.append(np.concatenate([tb[:, :, dra], tb[:, :, dra + 1]], axis=2))
    tiles.append(np.concatenate([tb[:, :, 10], mask], axis=2))
    nab = np.stack(tiles, axis=2)
    return np.ascontiguousarray(nab.transpose(0, 3, 1, 2, 4), dtype=np.float32)


def build_nc(n_layers=DEPTH, mixers=True, dbg=None):
    nc = bass.Bass("TRN2", target_bir_lowering=False)
    es = ExitStack()
    b = Builder(nc, es, n_layers=n_layers, mixers=mixers, dbg=dbg)
    with es:
        b.build()
    return nc


def kernel(**inputs):
    nc = build_nc()
    in_maps = [host_inputs(inputs, b) for b in range(8)]
    res = run_bass_kernel_spmd(nc, in_maps, core_ids=list(range(8)))
    out = np.stack([np.asarray(r["out"], dtype=np.float32) for r in res.results], axis=0)
    return out
```

```python
import numpy as np
from contextlib import ExitStack
import concourse.bass as bass
import concourse.mybir as mybir
from concourse.bass_utils import run_bass_kernel_spmd

F32 = mybir.dt.float32
BF16 = mybir.dt.bfloat16
AF = mybir.ActivationFunctionType
ALU = mybir.AluOpType
AX = mybir.AxisListType

D = 1024
SEQ = 2048
CTXL = 256
T = SEQ + CTXL
DEPTH = 4
KC = 8
FFN = 2816
NH = FFN // 128
IN_W = 3248
EPS = 1e-6
TB = [(0, 512), (512, 512), (1024, 512), (1536, 512), (2048, 256)]
ENGS = ("pe", "act", "dve", "pool", "sp")
import os as _os
NOPFIX = int(_os.environ.get("NOPFIX", "0"))


class Buf:
    __slots__ = ("name", "w_eng", "w_dma", "r_eng", "r_dma", "dslot", "excl")

    def __init__(self, name):
        self.name = name
        self.excl = False
        self.w_eng = {}
        self.w_dma = []
        self.r_eng = {}
        self.r_dma = []
        self.dslot = None


class Rec:
    __slots__ = ("eng", "fn", "deps", "is_dma", "sembuf", "dval", "dslot", "need_inc", "ival", "semi")

    def __init__(self):
        self.need_inc = False
        self.ival = 0
        self.semi = 0
        self.dval = 0


class V:
    __slots__ = ("ap", "bufs")

    def __init__(self, ap, bufs):
        self.ap = ap
        self.bufs = bufs

    def __getitem__(self, k):
        return V(self.ap[k], self.bufs)

    def bc(self, shape):
        return V(self.ap.to_broadcast(shape), self.bufs)


def _bufs(vs):
    out = []
    for v in vs:
        if v is None or isinstance(v, (int, float)):
            continue
        out.extend(v.bufs)
    return out


def _ap(v):
    return v.ap if isinstance(v, V) else v


class Prog:
    SEM_EPOCH = 20000

    def __init__(self):
        self.recs = {e: [] for e in ENGS}
        self.dma_all = []
        self.slots = []
        self.free = []
        self.live = []

    def op(self, eng, fn, reads=(), writes=(), pwrites=(), dma_buf=None):
        r = Rec()
        r.eng = eng
        r.fn = fn
        r.is_dma = dma_buf is not None
        r.sembuf = dma_buf
        deps = {}

        def add(d, kind):
            if d is r:
                return
            if (not r.is_dma) and (not d.is_dma) and d.eng == eng:
                if eng == "pe" or kind != "raw":
                    return
            deps[id(d)] = d

        for b in reads:
            for w in b.w_eng.values():
                add(w, "raw")
            for w in b.w_dma:
                add(w, "raw")
            if b.excl:
                for x in b.r_eng.values():
                    if x.eng != eng:
                        add(x, "raw")
        for b in list(writes) + list(pwrites):
            for w in b.w_eng.values():
                add(w, "waw")
            for w in b.w_dma:
                add(w, "waw")
            for x in b.r_eng.values():
                add(x, "war")
            for x in b.r_dma:
                add(x, "war")
        r.deps = list(deps.values())
        for b in reads:
            if r.is_dma:
                b.r_dma.append(r)
            else:
                b.r_eng[eng] = r
        for b in writes:
            b.w_eng = {}
            b.w_dma = []
            b.r_eng = {}
            b.r_dma = []
        for b in list(writes) + list(pwrites):
            if r.is_dma:
                b.w_dma.append(r)
            else:
                b.w_eng[eng] = r
            b.r_eng = {}
            b.r_dma = []
        if r.is_dma:
            if dma_buf.dslot is None:
                if self.free:
                    dma_buf.dslot = self.free.pop()
                else:
                    self.slots.append(0)
                    dma_buf.dslot = len(self.slots) - 1
                self.live.append(dma_buf)
            self.slots[dma_buf.dslot] += 16
            r.dslot = dma_buf.dslot
            r.dval = self.slots[dma_buf.dslot]
            self.dma_all.append(r)
        self.recs[eng].append(r)
        return r

    def barrier(self):
        last = []
        for e in ENGS:
            for r in reversed(self.recs[e]):
                if not r.is_dma and r.fn is not None:
                    last.append(r)
                    break
        pend = list(self.dma_all)
        self.dma_all = []
        for b in self.live:
            if self.slots[b.dslot] < 40000:
                self.free.append(b.dslot)
            b.dslot = None
        self.live = []
        for e in ENGS:
            r = Rec()
            r.eng = e
            r.fn = None
            r.is_dma = False
            r.sembuf = None
            r.deps = [d for d in last if d.eng != e] + pend
            self.recs[e].append(r)

    def emit(self, nc, es, final_waits):
        for e in ENGS:
            for r in self.recs[e]:
                for d in r.deps:
                    if not d.is_dma:
                        d.need_inc = True
        nsem = {}
        for e in ENGS:
            c = 0
            si = 0
            for r in self.recs[e]:
                if r.need_inc:
                    c += 1
                    if c > self.SEM_EPOCH:
                        si += 1
                        c = 1
                    r.ival = c
                    r.semi = si
            nsem[e] = si + 1
        sems = {e: [es.enter_context(nc.semaphore("s_%s%d" % (e, i))) for i in range(nsem[e])] for e in ENGS}
        dsems = [es.enter_context(nc.semaphore("d_%d" % i)) for i in range(len(self.slots))]
        recs = self.recs

        def run(e, eng):
            seen = {}
            for r in recs[e]:
                need = {}
                for d in r.deps:
                    if d.is_dma:
                        key = ("d", d.dslot)
                        sem = dsems[d.dslot]
                        val = d.dval
                    else:
                        key = (d.eng, d.semi)
                        sem = sems[d.eng][d.semi]
                        val = d.ival
                    if seen.get(key, 0) >= val:
                        continue
                    if key not in need or need[key][1] < val:
                        need[key] = (sem, val)
                for key, (sem, val) in need.items():
                    eng.wait_ge(sem, val)
                    seen[key] = val
                if NOPFIX and e == "pe" and len(need) >= 2:
                    eng.nop()
                if r.fn is None:
                    continue
                ins = r.fn(eng)
                if r.is_dma:
                    ins.then_inc(dsems[r.dslot], 16)
                elif r.need_inc:
                    ins.then_inc(sems[e][r.semi], 1)
            if e == "sp":
                for d in final_waits:
                    eng.wait_ge(dsems[d.dslot], d.dval)

        block = es.enter_context(nc.Block())

        @block.sync
        def _(eng):
            run("sp", eng)

        @block.tensor
        def _(eng):
            run("pe", eng)

        @block.scalar
        def _(eng):
            run("act", eng)

        @block.vector
        def _(eng):
            run("dve", eng)

        @block.gpsimd
        def _(eng):
            run("pool", eng)


class Builder:
    def __init__(self, nc, es, n_layers=DEPTH, mixers=True, dbg=None):
        self.nc = nc
        self.es = es
        self.P = Prog()
        self.L = n_layers
        self.mixers = mixers
        self.dbg = dbg
        self.nbuf = 0

    def buf(self, name="b"):
        self.nbuf += 1
        return Buf("%s%d" % (name, self.nbuf))

    def sb(self, name, shape, dt):
        t = self.es.enter_context(self.nc.sbuf_tensor("sb_" + name, list(shape), dt))
        return t

    def sbv(self, name, shape, dt):
        t = self.sb(name, shape, dt)
        return V(t[:], [self.buf(name)])

    def dram_in(self, name, shape, dt=F32):
        return self.nc.dram_tensor(name, list(shape), dt, kind="ExternalInput").ap()

    def mm(self, out, lhsT, rhs, start=True, stop=True, extra_reads=()):
        o, l, r = out.ap, lhsT.ap, rhs.ap
        self.P.op("pe", lambda e: e.matmul(o, lhsT=l, rhs=r, start=start, stop=stop),
                  reads=_bufs([lhsT, rhs]) + list(extra_reads), writes=() if not start else (), pwrites=out.bufs)

    def tr(self, out, in_, ident):
        o, i, d = out.ap, in_.ap, ident.ap
        self.P.op("pe", lambda e: e.transpose(o, i, d), reads=_bufs([in_, ident]), pwrites=out.bufs)

    def act(self, out, in_, func, bias=None, scale=None, accum_out=None, eng="act"):
        o, i = out.ap, in_.ap
        kw = {}
        if bias is not None:
            kw["bias"] = _ap(bias)
        if scale is not None:
            kw["scale"] = _ap(scale)
        if accum_out is not None:
            kw["accum_out"] = accum_out.ap
        self.P.op("act", lambda e: e.activation(o, i, func, **kw),
                  reads=_bufs([in_, bias, scale]), pwrites=_bufs([out, accum_out]))

    def tt(self, out, in0, in1, op, eng="dve"):
        o, a, b = out.ap, in0.ap, in1.ap
        self.P.op(eng, lambda e: e.tensor_tensor(o, a, b, op), reads=_bufs([in0, in1]), pwrites=out.bufs)

    def ts(self, out, in0, s1, op0, s2=None, op1=None, eng="dve"):
        o, a = out.ap, in0.ap
        s1a, s2a = _ap(s1), _ap(s2)
        if op1 is None:
            fn = lambda e: e.tensor_scalar(o, a, s1a, None, op0)
        else:
            fn = lambda e: e.tensor_scalar(o, a, s1a, s2a, op0, op1)
        self.P.op(eng, fn, reads=_bufs([in0, s1, s2]), pwrites=out.bufs)

    def stt(self, out, in0, scalar, in1, op0, op1, eng="dve"):
        o, a, b = out.ap, in0.ap, in1.ap
        s = _ap(scalar)
        self.P.op(eng, lambda e: e.scalar_tensor_tensor(o, a, s, b, op0, op1),
                  reads=_bufs([in0, scalar, in1]), pwrites=out.bufs)

    def copy(self, out, in_, eng="dve"):
        o, i = out.ap, in_.ap
        if eng == "act":
            self.P.op("act", lambda e: e.copy(o, i), reads=in_.bufs, pwrites=out.bufs)
        else:
            self.P.op(eng, lambda e: e.tensor_copy(o, i), reads=in_.bufs, pwrites=out.bufs)

    def recip(self, out, in_):
        o, i = out.ap, in_.ap
        self.P.op("dve", lambda e: e.reciprocal(o, i), reads=in_.bufs, pwrites=out.bufs)

    def memset(self, out, val, eng="dve"):
        o = out.ap
        self.P.op(eng, lambda e: e.memset(o, val), reads=(), pwrites=out.bufs)

    def dma(self, out, in_, eng="sp", sembuf=None):
        o, i = _ap(out), _ap(in_)
        rb = in_.bufs if isinstance(in_, V) else []
        wb = out.bufs if isinstance(out, V) else []
        if sembuf is None:
            sembuf = (wb or rb)[0]
        return self.P.op(eng, lambda e: e.dma_start(out=o, in_=i), reads=rb, pwrites=wb, dma_buf=sembuf)

    def build(self):
        nc, P, L = self.nc, self.P, self.L
        x_d = self.dram_in("x", [SEQ, D])
        ctx_d = self.dram_in("ctx", [CTXL, D])
        cc_d = self.dram_in("cc", [128, KC, 2])
        wada_d = self.dram_in("w_ada", [DEPTH, D, 6 * D])
        bada_d = self.dram_in("b_adaT", [128, DEPTH, 48])
        gv_d = self.dram_in("gvecs", [128, DEPTH, 2, KC])
        gfin_d = self.dram_in("g_finalT", [128, KC])
        wg_d = self.dram_in("w_gate", [DEPTH, D, FFN])
        wu_d = self.dram_in("w_up", [DEPTH, D, FFN])
        wd_d = self.dram_in("w_down", [DEPTH, FFN, D])
        ident_d = self.dram_in("ident", [128, 128])
        if self.mixers:
            self.win_d = self.dram_in("w_in", [DEPTH, D, IN_W])
            self.wpe_d = self.dram_in("w_pe2", [DEPTH, D, 2, 96])
            self.wq2_d = self.dram_in("w_q2", [DEPTH, 256, 2, 384])
            self.wkv_d = self.dram_in("w_kv_nv", [DEPTH, 128, 512])
            mlag_d = self.dram_in("mla_g", [128, DEPTH, 3])
            self.rope_d = self.dram_in("ropeCS", [128, 2, SEQ])
            sel_d = self.dram_in("sel65", [128, 64])
            self.nab_d = self.dram_in("nab", [DEPTH, 128, 4, 19, 64])
            self.wout_d = self.dram_in("w_out", [DEPTH, D, D])
            gdnc_d = self.dram_in("gdnc", [128, 832])
            gabc_d = self.dram_in("gdn_ab", [128, DEPTH, 16])
            gcw_d = self.dram_in("gdn_cw", [128, DEPTH, 3, 4, 5])
            ggo_d = self.dram_in("gdn_go", [128, DEPTH])
        out_d = nc.dram_tensor("out", [SEQ, D], F32, kind="ExternalOutput").ap()
        self.out_d = out_d
        if self.dbg:
            self.dbg_d = nc.dram_tensor("dbg", list(self.dbg), F32, kind="ExternalOutput").ap()

        POOLW = 40960
        XW = KC * T
        pool_t = self.sb("pool", [128, POOLW], F32)
        self.pool_t, self.POOLW, self.XW = pool_t, POOLW, XW
        xT_t = pool_t[:, 0:XW].rearrange("p (k t) -> p k t", k=KC)
        hT_t = self.sb("hT", [128, KC, T], BF16)
        self.xT_t, self.hT_t = xT_t, hT_t
        xb = [[self.buf("x") for _ in TB] for _ in range(KC)]
        hb = [[self.buf("h") for _ in TB] for _ in range(KC)]

        def xT(kc, tb):
            o, n = TB[tb]
            return V(xT_t[:, kc, o:o + n], [xb[kc][tb]])

        def hT(kc, tb):
            o, n = TB[tb]
            return V(hT_t[:, kc, o:o + n], [hb[kc][tb]])

        self.xT, self.hT = xT, hT
        modT = self.sbv("modT", [128, DEPTH, 6, KC, 2], F32)
        gsc = self.sbv("gsc", [128, DEPTH, 2, KC, 2], F32)
        bT = self.sbv("bT", [128, DEPTH, 48], F32)
        gv = self.sbv("gv", [128, DEPTH, 2, KC], F32)
        gfin = self.sbv("gfin", [128, KC], F32)
        ident = self.sbv("ident", [128, 128], F32)
        ones_bf = self.sbv("ones_bf", [128, 128], BF16)
        cc = self.sbv("cc", [128, KC, 2], F32)
        scc = self.sbv("scc", [128, KC, 2], F32)
        self.modT, self.gsc, self.ident, self.ones_bf = modT, gsc, ident, ones_bf
        self.xb, self.hb = xb, hb
        if self.mixers:
            self.mlag = self.sbv("mlag", [128, DEPTH, 3], F32)
            self.sel65 = self.sbv("sel65", [128, 64], F32)
            self.dma(self.mlag, mlag_d)
            self.gdnc = self.sbv("gdnc", [128, 832], F32)
            self.gabc = self.sbv("gabc", [128, DEPTH, 16], F32)
            self.gcw = self.sbv("gcw", [128, DEPTH, 3, 4, 5], F32)
            self.ggo = self.sbv("ggo", [128, DEPTH], F32)
            self.one_t = self.sbv("one_t", [128, 1], F32)
            self.ident_bf = self.sbv("ident_bf", [128, 128], BF16)
            self.dma(self.gdnc, gdnc_d)
            self.dma(self.gabc, gabc_d)
            self.dma(self.gcw, gcw_d)
            self.dma(self.ggo, ggo_d)
            self.memset(self.one_t, 1.0)
            self.dma(self.sel65, sel_d)
        self.scr_base = XW
        self.scr_lim = POOLW
        self.ps = [self.es.enter_context(nc.psum_tensor("ps%d" % i, [128, 512], F32)) for i in range(6)]
        self.psD_t = self.es.enter_context(nc.psum_tensor("psD", [128, 1024], F32))
        self.psb = [self.buf("ps") for _ in range(8)]
        for b_ in self.psb:
            b_.excl = True

        def PS(i, n=512, p=128):
            if i >= 6:
                return V(self.psD_t[0:p, (i - 6) * 512:(i - 6) * 512 + n], [self.psb[i]])
            return V(self.ps[i][0:p, 0:n], [self.psb[i]])

        self.PS = PS

        self.dma(ident, ident_d)
        self.dma(cc, cc_d)
        self.dma(bT, bada_d)
        self.dma(gv, gv_d)
        self.dma(gfin, gfin_d)
        self.memset(ones_bf, 1.0)
        self.act(scc, cc, AF.Silu)
        if self.mixers:
            self.copy(self.ident_bf, ident)

        scr_off = [0]

        def carve(n_f32, dt=F32, shape=None, name="c"):
            a = self.scr_base + scr_off[0]
            scr_off[0] += n_f32
            assert a + n_f32 <= self.scr_lim, (name, a + n_f32)
            ap = pool_t[:, a:a + n_f32]
            if dt != F32:
                ap = ap.bitcast(dt)
            if shape is not None:
                names = " ".join("d%d" % i for i in range(len(shape)))
                kw = {"d%d" % i: s for i, s in enumerate(shape[:-1])}
                ap = ap.rearrange("p (%s) -> p %s" % (names, names), **kw)
            return V(ap, [self.buf(name)])

        self.carve = carve
        self.scr_off = scr_off
        wa = [carve(KC * 256, BF16, [KC, 512], "wa") for _ in range(3)]
        scc_bf = self.sbv("scc_bf", [128, KC, 2], BF16)
        self.copy(scc_bf, scc)
        n = 0
        for l in range(L):
            pst = PS(l % 2, 96)
            for cb in range(12):
                w = wa[n % 3]
                n += 1
                self.dma(w, wada_d[l, :, cb * 512:(cb + 1) * 512].rearrange("(k p) c -> p k c", p=128),
                         eng="pool")
                for jj in range(4):
                    j = cb * 4 + jj
                    for k in range(KC):
                        self.mm(pst[:, 2 * j:2 * j + 2], w[:, k, jj * 128:(jj + 1) * 128], scc_bf[:, k, :],
                                start=(k == 0), stop=(k == KC - 1))
            o = modT.ap[:, l].rearrange("p s k w -> p (s k) w")
            i0 = pst.ap.rearrange("p (j w) -> p j w", w=2)
            i1 = bT.ap[:, l, :].unsqueeze(2).to_broadcast([128, 48, 2])
            self.tt(V(o, modT.bufs), V(i0, pst.bufs), V(i1, bT.bufs), ALU.add)
            for which, (si, gi) in enumerate(((1, 0), (4, 1))):
                g_b = gv.ap[:, l, gi, :].unsqueeze(2).to_broadcast([128, KC, 2])
                self.stt(V(gsc.ap[:, l, which], gsc.bufs), V(modT.ap[:, l, si], modT.bufs), 1.0,
                         V(g_b, gv.bufs), ALU.add, ALU.mult)
        P.barrier()
        scr_off[0] = 0

        stg = [carve(4 * D, F32, [4, D], "stg") for _ in range(2)]
        n = 0
        for tb, (o, nt) in enumerate(TB):
            s = stg[tb % 2]
            ntile = nt // 128
            if tb < 4:
                src = x_d[o:o + nt, :]
            else:
                src = ctx_d[:, :]
            self.dma(s[:, 0:ntile, :], src.rearrange("(t p) d -> p t d", p=128), eng="sp")
            for kc in range(KC):
                pb = PS(n % 8, nt)
                n += 1
                for t in range(ntile):
                    self.tr(pb[:, t * 128:(t + 1) * 128], s[:, t, kc * 128:(kc + 1) * 128], ident)
                self.copy(xT(kc, tb), pb, eng="act" if kc % 2 else "dve")
        P.barrier()
        scr_off[0] = 0

        for l in range(L):
            self.norm_mod(l, 0)
            if self.mixers:
                self.mixer_phase(l)
            self.norm_mod(l, 1)
            self.ffn_phase(l, wg_d, wu_d, wd_d)
        self.final_phase(gfin)
        P.emit(nc, self.es, self.final_waits)

    def rstd_block(self, tb, sqs, rs_tmp, rstd):
        o, nt = TB[tb]
        ps = self.PS(tb % 2, nt)
        for kc in range(KC):
            sq = sqs[kc % 2]
            self.act(sq[:, 0:nt], self.xT(kc, tb), AF.Square)
            self.mm(ps, self.ones_bf, sq[:, 0:nt], start=(kc == 0), stop=(kc == KC - 1))
        self.act(rs_tmp[:, 0:nt], ps, AF.Ln, bias=self.eps_t, scale=1.0 / D)
        self.act(rstd[:, 0:nt], rs_tmp[:, 0:nt], AF.Exp, scale=-0.5)

    def norm_mod(self, l, which):
        P = self.P
        self.scr_off[0] = 0
        if not hasattr(self, "eps_t"):
            self.eps_t = self.sbv("eps_t", [128, 1], F32)
            self.memset(self.eps_t, EPS)
        sqs = [self.carve(256, BF16, None, "sq") for _ in range(2)]
        rs_tmp = self.carve(512, F32, None, "rs")
        rstds = [self.carve(512, F32, None, "rstd") for _ in range(2)]
        tmps = [self.carve(512, F32, None, "nt") for _ in range(2)]
        shift_i = 0 if which == 0 else 3
        n = 0
        for tb, (o, nt) in enumerate(TB):
            rstd = rstds[tb % 2]
            self.rstd_block(tb, sqs, rs_tmp, rstd)
            w = 0 if tb < 4 else 1
            for kc in range(KC):
                tmp = tmps[n % 2]
                n += 1
                self.stt(tmp[:, 0:nt], self.xT(kc, tb), V(self.gsc.ap[:, l, which, kc, w:w + 1], self.gsc.bufs),
                         rstd[:, 0:nt], ALU.mult, ALU.mult)
                self.act(self.hT(kc, tb), tmp[:, 0:nt], AF.Identity,
                         bias=V(self.modT.ap[:, l, shift_i, kc, w:w + 1], self.modT.bufs), scale=1.0)
        P.barrier()
        self.scr_off[0] = 0

    def ffn_phase(self, l, wg_d, wu_d, wd_d):
        P = self.P
        self.scr_off[0] = 0
        HC = NH // 2
        hid = self.sb_scr_hid()
        wgs = [self.carve(KC * 64, BF16, [KC, 128], "wg") for _ in range(3)]
        wus = [self.carve(KC * 64, BF16, [KC, 128], "wu") for _ in range(3)]
        wds = [self.carve(HC * 64, BF16, [HC, 128], "wd") for _ in range(2)]
        sil = [self.carve(512, F32, None, "sil") for _ in range(2)]
        n = 0
        nd = 0
        npb = 0
        for half in range(2):
            for jj in range(HC):
                j = half * HC + jj
                wg, wu = wgs[n % 3], wus[n % 3]
                n += 1
                self.dma(wg, wg_d[l, :, j * 128:(j + 1) * 128].rearrange("(k p) c -> p k c", p=128), eng="pool")
                self.dma(wu, wu_d[l, :, j * 128:(j + 1) * 128].rearrange("(k p) c -> p k c", p=128), eng="pool")
                for tb, (o, nt) in enumerate(TB):
                    pg = self.PS(npb % 4, nt)
                    pu = self.PS(4 + npb % 4, nt)
                    npb += 1
                    for kc in range(KC):
                        self.mm(pg, wg[:, kc, :], self.hT(kc, tb), start=(kc == 0), stop=(kc == KC - 1))
                    for kc in range(KC):
                        self.mm(pu, wu[:, kc, :], self.hT(kc, tb), start=(kc == 0), stop=(kc == KC - 1))
                    s = sil[npb % 2]
                    self.act(s[:, 0:nt], pg, AF.Silu)
                    self.tt(V(hid.ap[:, jj, o:o + nt], [self.hidb[jj][tb]]), s[:, 0:nt], pu, ALU.mult)
            for m in range(KC):
                wd = wds[nd % 2]
                nd += 1
                self.dma(wd, wd_d[l, half * HC * 128:(half + 1) * HC * 128, m * 128:(m + 1) * 128]
                         .rearrange("(j p) c -> p j c", p=128), eng="pool")
                for tb, (o, nt) in enumerate(TB):
                    po = self.PS(npb % 8, nt)
                    npb += 1
                    for jj in range(HC):
                        self.mm(po, wd[:, jj, :], V(hid.ap[:, jj, o:o + nt], [self.hidb[jj][tb]]),
                                start=(jj == 0), stop=(jj == HC - 1))
                    w = 0 if tb < 4 else 1
                    self.stt(self.xT(m, tb), po, V(self.modT.ap[:, l, 5, m, w:w + 1], self.modT.bufs),
                             self.xT(m, tb), ALU.mult, ALU.add)
        P.barrier()
        self.scr_off[0] = 0

    def sb_scr_hid(self):
        HC = NH // 2
        hid = self.carve(HC * T // 2, BF16, [HC, T], "hid")
        self.hidb = [[self.buf("hid") for _ in TB] for _ in range(HC)]
        return hid

    def final_phase(self, gfin):
        P = self.P
        self.scr_off[0] = 0
        sqs = [self.carve(256, BF16, None, "sq") for _ in range(2)]
        rs_tmp = self.carve(512, F32, None, "rs")
        rstds = [self.carve(512, F32, None, "rstd") for _ in range(2)]
        ys = [self.carve(512, F32, None, "y") for _ in range(3)]
        outs = [self.carve(4 * D, F32, [4, D], "os") for _ in range(2)]
        self.final_waits = []
        n = 0
        for tb in range(4):
            o, nt = TB[tb]
            rstd = rstds[tb % 2]
            self.rstd_block(tb, sqs, rs_tmp, rstd)
            ost = outs[tb % 2]
            for kc in range(KC):
                y = ys[n % 3]
                self.stt(y, self.xT(kc, tb), gfin[:, kc:kc + 1], rstd, ALU.mult, ALU.mult)
                pb = self.PS(2 + n % 6, 512)
                n += 1
                for t in range(4):
                    self.tr(pb[:, t * 128:(t + 1) * 128], y[:, t * 128:(t + 1) * 128], self.ident)
                dst = V(ost.ap[:, :, kc * 128:(kc + 1) * 128], ost.bufs)
                srcv = V(pb.ap.rearrange("p (t c) -> p t c", c=128), pb.bufs)
                self.copy(dst, srcv, eng="act" if kc % 2 else "dve")
            r = self.dma(self.out_d[o:o + nt, :].rearrange("(t p) d -> p t d", p=128), ost, eng="sp")
            self.final_waits.append(r)

    def load_w(self, dst, src_ap, eng="pool"):
        return self.dma(dst, src_ap, eng=eng)

    def mixer_phase(self, l):
        P = self.P
        nc = self.nc
        if not hasattr(self, "xsp_d"):
            self.xsp_d = nc.dram_tensor("xspill", [128, KC * T], F32).ap()
            self.xsp_b = self.buf("xsp")
        xall = V(self.pool_t[:, 0:self.XW], [b for row in self.xb for b in row])
        self.dma(V(self.xsp_d, [self.xsp_b]), xall, eng="sp", sembuf=self.xsp_b)
        P.barrier()
        self.scr_base = 0
        self.scr_lim = self.POOLW
        self.scr_off[0] = 0
        if not hasattr(self, "mix_d"):
            self.mix_d = nc.dram_tensor("mixspill", [128, KC, T], BF16).ap()
        self.mixb = [self.buf("mix") for _ in range(KC)]
        self.mla(l)
        P.barrier()
        self.scr_off[0] = 0
        self.na(l)
        P.barrier()
        self.scr_off[0] = 0
        self.gdn(l)
        P.barrier()
        self.dma(xall, V(self.xsp_d, [self.xsp_b]), eng="sp", sembuf=self.xsp_b)
        self.scr_base = self.XW
        self.scr_off[0] = 0
        mixs = self.carve(KC * T // 2, BF16, [KC, T], "mixs")
        mix_t = mixs.ap
        self.mix_t = mix_t
        for k in range(KC):
            self.dma(V(mix_t[:, k, :], [mixs.bufs[0]]), V(self.mix_d[:, k, :], [self.mixb[k]]), eng="sp" if k % 2 else "act",
                     sembuf=mixs.bufs[0])
        self.mixb = [mixs.bufs[0]] * KC
        if self.dbg and l == 0:
            self.dump_mix()
        wos = [self.carve(KC * 64, BF16, [KC, 128], "wo") for _ in range(2)]
        npb = 0
        for m in range(KC):
            wo = wos[m % 2]
            self.load_w(wo, self.wout_d[l, :, m * 128:(m + 1) * 128].rearrange("(k p) c -> p k c", p=128))
            for tb, (o, nt) in enumerate(TB):
                po = self.PS(npb % 8, nt)
                npb += 1
                for kk in range(KC):
                    self.mm(po, wo[:, kk, :], V(mix_t[:, kk, o:o + nt], [self.mixb[kk]]),
                            start=(kk == 0), stop=(kk == KC - 1))
                w = 0 if tb < 4 else 1
                self.stt(self.xT(m, tb), po, V(self.modT.ap[:, l, 2, m, w:w + 1], self.modT.bufs),
                         self.xT(m, tb), ALU.mult, ALU.add)
        P.barrier()
        self.scr_off[0] = 0
        self.scr_lim = self.POOLW

    def dump_mix(self):
        st = [self.carve(T, F32, None, "dst") for _ in range(2)]
        for k in range(KC):
            s_ = st[k % 2]
            self.copy(s_, V(self.mix_t[:, k, :], [self.mixb[k]]), eng="act")
            self.dma(self.dbg_d[k * 128:(k + 1) * 128, :], s_, eng="sp")

    def finish_attn(self, O_ps, n, pb, chunk, tok0, tmp):
        osb, rden, obuf = tmp
        self.copy(osb[0:65, 0:n], O_ps[0:65, 0:n], eng="act")
        den = self.PS(5, n, 64)
        self.mm(den, self.sel65[0:65, :], osb[0:65, 0:n])
        self.act(rden[0:64, 0:n], den, AF.Ln)
        self.act(rden[0:64, 0:n], rden[0:64, 0:n], AF.Exp, scale=-1.0)
        self.tt(obuf[0:64, 0:n], osb[0:64, 0:n], rden[0:64, 0:n], ALU.mult)
        dst = V(self.mix_d[pb:pb + 64, chunk, tok0:tok0 + n], [self.mixb[chunk]])
        self.dma(dst, obuf[0:64, 0:n], eng="sp", sembuf=obuf.bufs[0])

    def attn_dense(self, QT, KT_fn, V_fn, key_tiles, n, scale, dst, st):
        pts, fin_tmp = st["pts"], st["fin"]
        O_ps = self.PS(3 + st["nO"] % 2, n, 65)
        st["nO"] += 1
        nk = len(key_tiles)
        LA = 2
        pend = []
        for i in range(nk + LA):
            if i < nk:
                t = key_tiles[i]
                S_ps = self.PS(st["nS"] % 3, n)
                pt = pts[st["nS"] % 3]
                st["nS"] += 1
                self.mm(S_ps, KT_fn(t), QT)
                pend.append((i, t, S_ps, pt))
            if i >= LA:
                j, t, S_ps, pt = pend.pop(0)
                self.act(pt[:, 0:n], S_ps, AF.Exp, scale=scale)
                self.mm(O_ps, V_fn(t), pt[:, 0:n], start=(j == 0), stop=(j == nk - 1))
        self.finish_attn(O_ps, n, dst[0], dst[1], dst[2], fin_tmp[st["nO"] % 2])

    def attn_state(self):
        pts = [self.carve(256, BF16, None, "pt") for _ in range(3)]
        fin = [(self.carve(512, F32, None, "osb"), self.carve(512, F32, None, "rden"),
                self.carve(256, BF16, None, "obuf")) for _ in range(2)]
        return {"pts": pts, "fin": fin, "nO": 0, "nS": 0}

    def mla(self, l):
        hT = self.hT
        SC = 96 ** -0.5
        w_in3 = [self.carve(KC * 64, BF16, [KC, 128], "wmi") for _ in range(3)]
        for c in range(3):
            self.load_w(w_in3[c], self.win_d[l, :, c * 128:(c + 1) * 128].rearrange("(k p) c -> p k c", p=128))
        wpe = self.carve(KC * 96, BF16, [KC, 2, 96], "wpe")
        self.load_w(wpe, self.wpe_d[l].rearrange("(k p) a c -> p k a c", p=128))
        wq2 = self.carve(2 * 384, BF16, [2, 2, 384], "wq2")
        self.load_w(wq2, self.wq2_d[l].rearrange("(k p) a c -> p k a c", p=128))
        wkv = self.carve(256, BF16, None, "wkv")
        self.load_w(wkv, self.wkv_d[l])
        rope = self.carve(2 * SEQ, F32, [2, SEQ], "rope")
        self.dma(rope, self.rope_d, eng="sp")
        mqn = self.carve(T, BF16, [2, T], "mqn")
        mkvn = self.carve(T // 2, BF16, None, "mkvn")
        peR = self.carve(T // 2, BF16, None, "peR")
        vaug = self.carve(18 * 4 * 65 // 2, BF16, [18, 4, 65], "vaug")
        self.memset(V(vaug.ap[:, :, :, 64:65], vaug.bufs), 1.0, eng="pool")
        raw = [self.carve(512, F32, None, "raw") for _ in range(3)]
        sqs = [self.carve(256, BF16, None, "sq") for _ in range(3)]
        rt = [self.carve(512, F32, None, "rt") for _ in range(4)]
        r1 = self.carve(512, F32, None, "r1")
        r2 = self.carve(512, F32, None, "r2")
        for tb, (o, nt) in enumerate(TB):
            pr = [self.PS(6, nt), self.PS(7, nt), self.PS(5, nt)]
            for c in range(3):
                for kc in range(KC):
                    self.mm(pr[c], w_in3[c][:, kc, :], hT(kc, tb), start=(kc == 0), stop=(kc == KC - 1))
                self.copy(raw[c][:, 0:nt], pr[c], eng="act" if c % 2 else "dve")
                self.act(sqs[c][:, 0:nt], raw[c][:, 0:nt], AF.Square)
            ps_q = self.PS(3, nt)
            self.mm(ps_q, self.ones_bf, sqs[0][:, 0:nt], start=True, stop=False)
            self.mm(ps_q, self.ones_bf, sqs[1][:, 0:nt], start=False, stop=True)
            ps_k = self.PS(4, nt)
            self.mm(ps_k, self.ones_bf, sqs[2][:, 0:nt])
            self.act(rt[0][:, 0:nt], ps_q, AF.Ln, bias=self.eps_t, scale=1.0 / 256)
            self.act(rt[1][:, 0:nt], rt[0][:, 0:nt], AF.Exp, scale=-0.5)
            self.act(rt[2][:, 0:nt], ps_k, AF.Ln, bias=self.eps_t, scale=1.0 / 128)
            self.act(rt[3][:, 0:nt], rt[2][:, 0:nt], AF.Exp, scale=-0.5)
            for c in range(2):
                self.stt(V(mqn.ap[:, c, o:o + nt], mqn.bufs), raw[c][:, 0:nt],
                         V(self.mlag.ap[:, l, c:c + 1], self.mlag.bufs), rt[1][:, 0:nt], ALU.mult, ALU.mult)
            self.stt(mkvn[:, o:o + nt], raw[2][:, 0:nt], V(self.mlag.ap[:, l, 2:3], self.mlag.bufs),
                     rt[3][:, 0:nt], ALU.mult, ALU.mult)
            pp = [self.PS(0, nt, 96), self.PS(1, nt, 96)]
            for a in range(2):
                for kc in range(KC):
                    self.mm(pp[a], wpe[:, kc, a, :], hT(kc, tb), start=(kc == 0), stop=(kc == KC - 1))
            if tb < 4:
                self.tt(r1[64:96, 0:nt], pp[0][64:96, :], rope[64:96, 0, o:o + nt], ALU.mult)
                self.tt(r2[64:96, 0:nt], pp[1][64:96, :], rope[64:96, 1, o:o + nt], ALU.mult)
                self.tt(peR[64:96, o:o + nt], r1[64:96, 0:nt], r2[64:96, 0:nt], ALU.add)
            else:
                self.copy(peR[64:96, o:o + nt], pp[0][64:96, :], eng="act")
        for t in range(18):
            pv = self.PS(6 + t % 2, 256)
            self.mm(pv, mkvn[:, t * 128:(t + 1) * 128], wkv[:, 256:512])
            self.copy(V(vaug.ap[:, t, :, 0:64], vaug.bufs), V(pv.ap.rearrange("p (h d) -> p h d", h=4), pv.bufs),
                      eng="act" if t % 2 else "dve")
        QTs = [self.carve(T // 2, BF16, None, "QT") for _ in range(2)]
        KTs = [self.carve(T // 2, BF16, None, "KT") for _ in range(2)]
        st = self.attn_state()
        for h in range(4):
            QT, KT = QTs[h % 2], KTs[h % 2]
            for tb, (o, nt) in enumerate(TB):
                pq = [self.PS(6, nt, 96), self.PS(7, nt, 96)]
                na_ = 2 if tb < 4 else 1
                for a in range(na_):
                    for c in range(2):
                        self.mm(pq[a], wq2[:, c, a, h * 96:(h + 1) * 96], V(mqn.ap[:, c, o:o + nt], mqn.bufs),
                                start=(c == 0), stop=(c == 1))
                if tb < 4:
                    self.copy(QT[0:64, o:o + nt], pq[0][0:64, :], eng="act")
                    self.tt(r1[64:96, 0:nt], pq[0][64:96, :], rope[64:96, 0, o:o + nt], ALU.mult)
                    self.tt(r2[64:96, 0:nt], pq[1][64:96, :], rope[64:96, 1, o:o + nt], ALU.mult)
                    self.tt(QT[64:96, o:o + nt], r1[64:96, 0:nt], r2[64:96, 0:nt], ALU.add)
                else:
                    self.copy(QT[0:96, o:o + nt], pq[0][0:96, :], eng="act")
                pk = self.PS(5, nt, 64)
                self.mm(pk, wkv[:, h * 64:(h + 1) * 64], mkvn[:, o:o + nt])
                self.copy(KT[0:64, o:o + nt], pk, eng="dve")
            self.copy(KT[64:96, :], peR[64:96, :], eng="act")
            KT_fn = lambda t, KT=KT: KT[0:96, t * 128:(t + 1) * 128]
            V_fn = lambda t, h=h: V(vaug.ap[:, t, h, :], vaug.bufs)
            for qb in range(4):
                self.attn_dense(QT[0:96, qb * 512:(qb + 1) * 512], KT_fn, V_fn, list(range(18)), 512, SC,
                                ((h % 2) * 64, h // 2, qb * 512), st)
            self.attn_dense(QT[0:96, SEQ:T], KT_fn, V_fn, [16, 17], 256, SC, ((h % 2) * 64, h // 2, SEQ), st)

    def na(self, l):
        hT = self.hT
        SC = 64 ** -0.5
        nqk = self.carve(2 * T, BF16, [2, 2, T], "nqk")
        vaug = self.carve(18 * 4 * 65 // 2, BF16, [18, 4, 65], "vaugn")
        self.memset(V(vaug.ap[:, :, :, 64:65], vaug.bufs), 1.0, eng="pool")
        ws = [self.carve(KC * 64, BF16, [KC, 128], "wn") for _ in range(2)]
        n = 0
        for qk in range(2):
            for c in range(2):
                w = ws[n % 2]
                n += 1
                col = 416 + qk * 256 + c * 128
                self.load_w(w, self.win_d[l, :, col:col + 128].rearrange("(k p) c -> p k c", p=128))
                for tb, (o, nt) in enumerate(TB):
                    pp = self.PS(6 + tb % 2, nt)
                    for kc in range(KC):
                        self.mm(pp, w[:, kc, :], hT(kc, tb), start=(kc == 0), stop=(kc == KC - 1))
                    self.copy(V(nqk.ap[:, qk, c, o:o + nt], nqk.bufs), pp, eng="act" if tb % 2 else "dve")
        wv = self.carve(KC * 128, BF16, [KC, 256], "wnv")
        self.load_w(wv, self.win_d[l, :, 928:1184].rearrange("(k p) c -> p k c", p=128))
        for t in range(18):
            pv = self.PS(6 + t % 2, 256)
            for kc in range(KC):
                self.mm(pv, V(self.hT_t[:, kc, t * 128:(t + 1) * 128], [self.hb[kc][t // 4]]), wv[:, kc, :],
                        start=(kc == 0), stop=(kc == KC - 1))
            self.copy(V(vaug.ap[:, t, :, 0:64], vaug.bufs), V(pv.ap.rearrange("p (h d) -> p h d", h=4), pv.bufs),
                      eng="act" if t % 2 else "dve")
        nabs = self.carve(4 * 19 * 64, F32, [4, 19, 64], "nabs")
        self.dma(nabs, self.nab_d[l], eng="sp")
        E = self.carve(4 * 19 * 32, BF16, [4, 19, 64], "E")
        self.act(E, nabs, AF.Exp)
        st = self.attn_state()
        tmps = [self.carve(7 * 32, BF16, [7, 64], "ntmp") for _ in range(3)]
        ptl = [self.carve(5 * 32, BF16, [5, 64], "ptl") for _ in range(3)]
        nn = 0
        for h in range(4):
            c, pb = h // 2, (h % 2) * 64
            V_fn = lambda t, h=h: V(vaug.ap[:, t, h, :], vaug.bufs)
            LA = 2
            pend = []
            for rr_ in range(32 + LA):
                if rr_ < 32:
                    r = rr_
                    sr = min(max(r - 4, 0), 24)
                    if sr % 2 == 0:
                        nloc, t0 = 4, sr // 2
                        off = sr - r + 7
                        Esel = V(E.ap[:, h, off:off + 7:2, :], E.bufs)
                    else:
                        nloc, t0 = 5, (sr - 1) // 2
                        Esel = V(E.ap[:, h, 14:19, :], E.bufs)
                    tiles = [t0 + i for i in range(nloc)] + [16, 17]
                    ntl = len(tiles)
                    S_ps = self.PS(nn % 3, ntl * 64)
                    tmp = tmps[nn % 3]
                    pl = ptl[nn % 3]
                    nn += 1
                    q = V(nqk.ap[pb:pb + 64, 0, c, r * 64:(r + 1) * 64], nqk.bufs)
                    for i, t in enumerate(tiles):
                        k = V(nqk.ap[pb:pb + 64, 1, c, t * 128:(t + 1) * 128], nqk.bufs)
                        self.mm(S_ps[:, i * 64:(i + 1) * 64], k, q)
                    pend.append((r, tiles, nloc, Esel, S_ps, tmp, pl))
                if rr_ >= LA:
                    r, tiles, nloc, Esel, S_ps, tmp, pl = pend.pop(0)
                    ntl = len(tiles)
                    if r % 8 == 0:
                        O_ps = self.PS(3 + st["nO"] % 2, 512, 65)
                        st["nO"] += 1
                    self.act(V(tmp.ap[:, 0:ntl, :], tmp.bufs), V(S_ps.ap.rearrange("p (t q) -> p t q", q=64), S_ps.bufs),
                             AF.Exp, scale=SC)
                    self.tt(V(pl.ap[:, 0:nloc, :], pl.bufs), V(tmp.ap[:, 0:nloc, :], tmp.bufs), Esel, ALU.mult)
                    Or = O_ps[0:65, (r % 8) * 64:(r % 8 + 1) * 64]
                    for i, t in enumerate(tiles):
                        if i < nloc:
                            p_ = V(pl.ap[:, i, :], pl.bufs)
                        else:
                            p_ = V(tmp.ap[:, i, :], tmp.bufs)
                        self.mm(Or, V_fn(t), p_, start=(i == 0), stop=(i == ntl - 1))
                    if r % 8 == 7:
                        self.finish_attn(O_ps, 512, pb, 2 + c, (r - 7) * 64, st["fin"][st["nO"] % 2])
            KT_fn = lambda t, c=c, pb=pb: V(nqk.ap[pb:pb + 64, 1, c, t * 128:(t + 1) * 128], nqk.bufs)
            self.attn_dense(V(nqk.ap[pb:pb + 64, 0, c, SEQ:T], nqk.bufs), KT_fn, V_fn, [16, 17], 256, SC,
                            (pb, 2 + c, SEQ), st)

    def gdn(self, l):
        hT = self.hT
        NCH = 36
        DK = 128 ** -0.5
        gc_ = self.gdnc
        U = [gc_[0:64, d * 320:d * 320 + 64] for d in range(2)]
        SU = [gc_[0:64, d * 320 + 64:d * 320 + 128] for d in range(2)]
        MN = [gc_[0:64, d * 320 + 128:d * 320 + 320] for d in range(2)]
        I64 = gc_[0:64, 640:704]
        ONE = gc_[0:64, 704:832]
        ORD = [[32, 33, 34, 35] + list(range(32)), [35, 34, 33, 32] + list(range(31, -1, -1))]
        MPOS = []
        for d_ in range(2):
            mp = self.carve(64, F32, None, "mpos")
            self.ts(mp[0:64], MN[d_][:, 0:64], -1.0, ALU.mult)
            MPOS.append(mp[0:64])
        psD = lambda n0, n1, p=64: V(self.psD_t[0:p, n0:n1], [self.psb[6], self.psb[7]])
        wab = self.carve(KC * 8, BF16, [KC, 16], "wab")
        self.load_w(wab, self.win_d[l, :, 3232:3248].rearrange("(k p) c -> p k c", p=128))
        ab = self.carve(NCH * 16, F32, [NCH, 16], "ab")
        for n in range(NCH):
            for kc in range(KC):
                self.mm(psD(n * 16, n * 16 + 16),
                        V(self.hT_t[:, kc, n * 64:(n + 1) * 64], [self.hb[kc][n // 8]]), wab[:, kc, :],
                        start=(kc == 0), stop=(kc == KC - 1))
        self.copy(ab[0:64], V(psD(0, NCH * 16).ap.rearrange("p (n c) -> p n c", c=16), [self.psb[6], self.psb[7]]))
        abc = self.gabc
        nA = self.carve(8, F32, None, "nA")
        self.act(nA[0:64], V(abc.ap[0:64, l, 0:8], abc.bufs), AF.Exp)
        self.ts(nA[0:64], nA[0:64], -1.0, ALU.mult)
        g = self.carve(NCH * 8, F32, [NCH, 8], "g")
        beta = self.carve(NCH * 8, F32, [NCH, 8], "beta")
        sp = self.carve(NCH * 8, F32, [NCH, 8], "sp")
        dtb = V(abc.ap[0:64, l, 8:16].unsqueeze(1).to_broadcast([64, NCH, 8]), abc.bufs)
        self.tt(sp[0:64], V(ab.ap[0:64, :, 0:8], ab.bufs), dtb, ALU.add)
        self.act(sp[0:64], sp[0:64], AF.Exp)
        self.act(sp[0:64], sp[0:64], AF.Ln, bias=self.one_t[0:64], scale=1.0)
        self.tt(g[0:64], sp[0:64], V(nA.ap[0:64].unsqueeze(1).to_broadcast([64, NCH, 8]), nA.bufs), ALU.mult)
        self.act(beta[0:64], V(ab.ap[0:64, :, 8:16], ab.bufs), AF.Sigmoid)
        gcs = self.carve(NCH * 8, F32, [NCH, 8], "gcs")
        for d in range(2):
            pc = self.PS(d, NCH * 4, 64)
            self.mm(pc, U[d], V(g.ap[0:64, :, d * 4:(d + 1) * 4], g.bufs))
            self.copy(V(gcs.ap[0:64, :, d * 4:(d + 1) * 4], gcs.bufs),
                      V(pc.ap.rearrange("p (n c) -> p n c", c=4), pc.bufs))
        ptot = self.PS(2, NCH * 8, 128)
        self.mm(ptot, ONE, V(g.ap[0:64].rearrange("p n c -> p (n c)"), g.bufs))
        glast = self.carve(NCH * 8, F32, [NCH, 8], "glast")
        self.act(V(glast.ap.rearrange("p n c -> p (n c)"), glast.bufs), ptot, AF.Exp)
        etail = self.carve(NCH * 8, F32, [NCH, 8], "etail")
        self.tt(V(etail.ap[0:64].rearrange("p n c -> p (n c)"), etail.bufs), ptot[0:64, :],
                V(gcs.ap[0:64].rearrange("p n c -> p (n c)"), gcs.bufs), ALU.subtract)
        self.act(etail[0:64], etail[0:64], AF.Exp)
        eg = self.carve(NCH * 8, F32, [NCH, 8], "eg")
        self.act(eg[0:64], gcs[0:64], AF.Exp)
        s_kbg = self.carve(NCH * 8, F32, [NCH, 8], "skbg")
        self.tt(s_kbg[0:64], beta[0:64], eg[0:64], ALU.mult)
        s_q = self.carve(NCH * 8, F32, [NCH, 8], "sq_")
        self.ts(s_q[0:64], eg[0:64], DK, ALU.mult)
        import os
        STOP = float(os.environ.get("GDN_STOP", "99"))
        if STOP <= 0:
            return
        kT = self.carve(T // 2, BF16, None, "kT")
        qT = self.carve(T // 2, BF16, None, "qT")
        szT = self.carve(T // 2, BF16, None, "szT")
        k_tok = self.carve(NCH * 64, BF16, [NCH, 128], "ktok")
        v_tok = self.carve(NCH * 64, BF16, [NCH, 128], "vtok")
        kbT_sh = self.carve(T // 2, BF16, None, "kbT")
        oT = self.carve(T, F32, None, "oT")
        oTb = [self.buf("oT") for _ in range(NCH)]
        off_r = self.scr_base + self.scr_off[0]
        rawp = self.carve(T + 8, F32, None, "rawp")
        acc = self.carve(T, F32, None, "acc")
        XYall = self.pool_t[:, off_r:off_r + 9 * 512].rearrange("p (g a i c) -> p g a i c", g=9, a=2, i=4)
        XYb = [self.buf("XYb") for _ in range(9)]
        Qgb = [self.buf("Qgb") for _ in range(9)]
        perd = []
        for d in range(2):
            perd.append(dict(
                kbT=kbT_sh, qdT=self.carve(T // 2, BF16, None, "qdT"),
                QT=self.carve(NCH * 32, BF16, [NCH, 64], "QTa"), AiT=self.carve(NCH * 32, BF16, [NCH, 64], "AiT"),
                nwT=self.carve(NCH * 32, BF16, [NCH, 64], "nwT"),
                S=self.carve(128, F32, None, "S"), Sb=self.carve(64, BF16, None, "Sb")))
        wq_ = [self.carve(KC * 64, BF16, [KC, 128], "wg_") for _ in range(2)]
        sqb = [self.carve(256, BF16, None, "sqb") for _ in range(2)]
        rr = [self.carve(512, F32, None, "rr") for _ in range(2)]
        diag = [self.carve(512, F32, [8, 64], "diag") for _ in range(2)]
        GU = [self.carve(256, F32, [4, 64], "GU") for _ in range(2)]
        Dall = [self.carve(768, F32, [4, 192], "Dall") for _ in range(2)]
        Qall = self.carve(9 * 256, F32, [9, 4, 64], "Qall").ap
        kbg = [self.carve(256, BF16, [4, 128], "kbg") for _ in range(2)]
        vbt = [self.carve(64, BF16, None, "vb") for _ in range(4)]
        ktt = [self.carve(64, BF16, None, "kt") for _ in range(4)]
        vnw = [self.carve(64, BF16, None, "vn") for _ in range(4)]
        gts = [self.carve(512, F32, None, "gt") for _ in range(2)]
        gos = [self.carve(256, BF16, None, "go") for _ in range(2)]
        self.memset(rawp[:, 0:2], 0.0)
        self.memset(rawp[:, 2 + SEQ:2 + SEQ + 4], 0.0)
        self.memset(rawp[:, T + 6:T + 8], 0.0)
        nw = 0
        for h in range(4):
            for which in range(4):
                w = wq_[nw % 2]
                nw += 1
                col = 1184 + which * 512 + h * 128
                self.load_w(w, self.win_d[l, :, col:col + 128].rearrange("(k p) c -> p k c", p=128))
                for tb, (o, nt) in enumerate(TB):
                    pp = self.PS(tb % 2, nt)
                    for kc in range(KC):
                        self.mm(pp, w[:, kc, :], hT(kc, tb), start=(kc == 0), stop=(kc == KC - 1))
                    if which == 3:
                        self.act(szT[:, o:o + nt], pp, AF.Silu)
                    else:
                        oo = 2 + o if tb < 4 else 6 + o
                        self.copy(rawp[:, oo:oo + nt], pp, eng="act")
                if which == 3:
                    continue
                cw = lambda j: V(self.gcw.ap[:, l, which, h, j:j + 1], self.gcw.bufs)
                for (o0, n0, p0) in ((0, SEQ, 0), (SEQ, CTXL, SEQ + 4)):
                    a_ = acc[:, o0:o0 + n0]
                    self.ts(a_, rawp[:, p0:p0 + n0], cw(0), ALU.mult)
                    for j in range(1, 5):
                        self.stt(a_, rawp[:, p0 + j:p0 + j + n0], cw(j), a_, ALU.mult, ALU.add)
                self.act(acc, acc, AF.Silu)
                if which == 2:
                    for g4 in range(0, NCH, 4):
                        pt_ = self.PS(4 + (g4 // 4) % 2, 512, 64)
                        for i in range(4):
                            n = g4 + i
                            self.tr(pt_[:, i * 128:(i + 1) * 128], acc[:, n * 64:(n + 1) * 64], self.ident)
                        self.copy(V(v_tok.ap[0:64, g4:g4 + 4, :], v_tok.bufs),
                                  V(pt_.ap.rearrange("p (n c) -> p n c", c=128), pt_.bufs), eng="act")
                    continue
                dst = qT if which == 0 else kT

                def n1(tb):
                    o, nt = TB[tb]
                    sq = sqb[tb % 2]
                    self.act(sq[:, 0:nt], acc[:, o:o + nt], AF.Square)
                    pn = self.PS(2 + tb % 2, nt)
                    self.mm(pn, self.ones_bf, sq[:, 0:nt])

                def n2(tb):
                    o, nt = TB[tb]
                    pn = self.PS(2 + tb % 2, nt)
                    r_ = rr[tb % 2]
                    self.act(r_[:, 0:nt], pn, AF.Ln, bias=self.eps_t, scale=1.0)
                    self.act(r_[:, 0:nt], r_[:, 0:nt], AF.Exp, scale=-0.5)

                def n3(tb, dst=dst):
                    o, nt = TB[tb]
                    r_ = rr[tb % 2]
                    self.tt(dst[:, o:o + nt], acc[:, o:o + nt], r_[:, 0:nt], ALU.mult)

                nb_ = len(TB)
                for i in range(nb_ + 2):
                    if i < nb_:
                        n1(i)
                    if 1 <= i <= nb_:
                        n2(i - 1)
                    if i >= 2:
                        n3(i - 2)
            for (src, dstt) in ((kT, k_tok),):
                for g8 in range(0, NCH, 8):
                    cnt = min(8, NCH - g8)
                    pt_ = V(self.ps[4 + (g8 // 8) % 2][0:64, 0:512].bitcast(BF16), [self.psb[4 + (g8 // 8) % 2]])
                    for i in range(cnt):
                        n = g8 + i
                        self.tr(pt_[:, i * 128:(i + 1) * 128], src[:, n * 64:(n + 1) * 64], self.ident_bf)
                    self.copy(V(dstt.ap[0:64, g8:g8 + cnt, :], dstt.bufs),
                              V(pt_.ap[:, 0:cnt * 128].rearrange("p (n c) -> p n c", c=128), pt_.bufs), eng="act")
            self.P.barrier()
            if STOP <= 1:
                continue
            for d in range(2):
                c = d * 4 + h
                pd = perd[d]
                jobs = []
                for (srcT, scal, dstT) in ((kT, beta, pd["kbT"]), (qT, s_q, pd["qdT"])):
                    for n0 in range(0, NCH, 8):
                        jobs.append((srcT, scal, dstT, n0, min(8, NCH - n0)))
                pendb = []
                for bi in range(len(jobs) + 1):
                    if bi < len(jobs):
                        srcT, scal, dstT, n0, cnt = jobs[bi]
                        dg = diag[bi % 2]
                        self.tt(V(dg.ap[0:64, 0:cnt, :], dg.bufs),
                                V(I64.ap.unsqueeze(1).to_broadcast([64, cnt, 64]), I64.bufs),
                                V(scal.ap[0:64, n0:n0 + cnt, c:c + 1].to_broadcast([64, cnt, 64]), scal.bufs), ALU.mult)
                        pb_ = self.PS(bi % 2, cnt * 64)
                        self.mm(pb_, ONE, V(dg.ap[0:64, 0:cnt, :].rearrange("p n c -> p (n c)"), dg.bufs))
                        pendb.append((srcT, dstT, n0, cnt, pb_))
                    if bi >= 1:
                        srcT, dstT, n0, cnt, pb_ = pendb.pop(0)
                        self.tt(dstT[:, n0 * 64:(n0 + cnt) * 64], srcT[:, n0 * 64:(n0 + cnt) * 64], pb_, ALU.mult)
                if STOP <= 2.1:
                    continue
                NG = NCH // 4
                def g_job(bi):
                    n0 = bi * 8
                    cnt = min(8, NCH - n0)
                    dg = diag[bi % 2]
                    self.tt(V(dg.ap[0:64, 0:cnt, :], dg.bufs),
                            V(I64.ap.unsqueeze(1).to_broadcast([64, cnt, 64]), I64.bufs),
                            V(gcs.ap[0:64, n0:n0 + cnt, c:c + 1].to_broadcast([64, cnt, 64]), gcs.bufs), ALU.mult)
                    gp = self.PS(6 + bi % 2, cnt * 64, 64)
                    self.mm(gp, ONE[:, 0:64], V(dg.ap[0:64, 0:cnt, :].rearrange("p n c -> p (n c)"), dg.bufs))
                    return gp

                gps = {}

                def setup_front(gi):
                    n0 = gi * 4
                    t1, da = GU[gi % 2], Dall[gi % 2]
                    if gi % 2 == 0:
                        gps[gi // 2] = g_job(gi // 2)
                    gp = gps[gi // 2]
                    pk = self.PS(5 if gi % 2 == 0 else 3, 512, 64)
                    pa = self.PS(4 if gi % 2 == 0 else 2, 256, 64)
                    for i in range(4):
                        n = n0 + i
                        self.mm(pk[:, i * 128:i * 128 + 64], pd["kbT"][:, n * 64:(n + 1) * 64], kT[:, n * 64:(n + 1) * 64])
                        self.mm(pk[:, i * 128 + 64:i * 128 + 128], kT[:, n * 64:(n + 1) * 64], pd["kbT"][:, n * 64:(n + 1) * 64])
                        self.mm(pa[:, i * 64:(i + 1) * 64], kT[:, n * 64:(n + 1) * 64], qT[:, n * 64:(n + 1) * 64])
                    gsl = V(gp.ap[:, (gi % 2) * 256:(gi % 2) * 256 + 256].rearrange("p (i c) -> p i c", c=64), gp.bufs)
                    self.tt(t1[0:64], gsl, V(gcs.ap[0:64, n0:n0 + 4, c:c + 1].to_broadcast([64, 4, 64]), gcs.bufs),
                            ALU.subtract)
                    da_a = V(da.ap[0:64, :, 0:64], da.bufs)
                    da_b = V(da.ap[0:64, :, 64:128], da.bufs)
                    self.tt(da_a, t1[0:64], V(MPOS[d].ap.unsqueeze(1).to_broadcast([64, 4, 64]), MPOS[d].bufs), ALU.max)
                    self.tt(da_b, t1[0:64], V(MN[d].ap[:, 64:128].unsqueeze(1).to_broadcast([64, 4, 64]), MN[d].bufs),
                            ALU.min)
                    self.act(da_a, da_a, AF.Exp, scale=-1.0)
                    self.act(da_b, da_b, AF.Exp)
                    self.tt(V(da.ap[0:64, :, 128:192], da.bufs), da_b,
                            V(I64.ap.unsqueeze(1).to_broadcast([64, 4, 64]), I64.bufs), ALU.add)
                    return pk, pa

                def setup_back(gi, pk, pa):
                    n0 = gi * 4
                    da = Dall[gi % 2]
                    pk3 = V(pk.ap.rearrange("p (i c) -> p i c", c=128), pk.bufs)
                    X = V(XYall[0:64, gi, 0], [XYb[gi]])
                    Y = V(XYall[0:64, gi, 1], [XYb[gi]])
                    Q = V(Qall[0:64, gi], [Qgb[gi]])
                    self.stt(X, pk3[:, :, 0:64], -1.0, V(da.ap[0:64, :, 0:64], da.bufs), ALU.mult, ALU.mult)
                    self.stt(Y, pk3[:, :, 64:128], -1.0, V(da.ap[0:64, :, 64:128], da.bufs), ALU.mult, ALU.mult)
                    self.stt(V(pd["AiT"].ap[0:64, n0:n0 + 4, :], pd["AiT"].bufs),
                             V(pa.ap.rearrange("p (i c) -> p i c", c=64), pa.bufs), DK,
                             V(da.ap[0:64, :, 128:192], da.bufs), ALU.mult, ALU.mult)
                    self.tt(Q, Y, V(I64.ap.unsqueeze(1).to_broadcast([64, 4, 64]), I64.bufs), ALU.add, eng="dve")

                prev = None
                for gi in range(NG + 1):
                    cur = None
                    if gi < NG:
                        cur = (gi,) + setup_front(gi)
                    if prev is not None:
                        setup_back(*prev)
                    prev = cur
                if STOP <= 2.2:
                    continue
                nps = 0
                for m in range(1, 6):
                    for gi in range(NG):
                        pxy = self.PS(nps % 3, 512, 64)
                        nps += 1
                        Xi = lambda i: V(XYall[0:64, gi, 0, i, :], [XYb[gi]])
                        Yi = lambda i: V(XYall[0:64, gi, 1, i, :], [XYb[gi]])
                        for i in range(4):
                            self.mm(pxy[:, i * 64:(i + 1) * 64], Yi(i), Xi(i))
                        nc_ = 256
                        if m < 5:
                            nc_ = 512
                            for i in range(4):
                                self.mm(pxy[:, 256 + i * 64:256 + (i + 1) * 64], Xi(i), Yi(i))
                        self.copy(V(XYall[0:64, gi].rearrange("p a i c -> p (a i c)")[:, 0:nc_], [XYb[gi]]), pxy[:, 0:nc_],
                                  eng="act")
                    for gi in range(NG):
                        pq_ = self.PS(3 + gi % 2, 256, 64)
                        for i in range(4):
                            self.mm(pq_[:, i * 64:(i + 1) * 64], V(XYall[0:64, gi, 0, i, :], [XYb[gi]]),
                                    V(Qall[0:64, gi, i, :], [Qgb[gi]]))
                        Qf_ = V(Qall[0:64, gi].rearrange("p i c -> p (i c)"), [Qgb[gi]])
                        self.tt(Qf_, Qf_, pq_, ALU.add)
                if STOP <= 2.3:
                    continue
                for gi in range(NG + 1):
                    if gi < NG:
                        n0 = gi * 4
                        self.copy(V(pd["QT"].ap[0:64, n0:n0 + 4, :], pd["QT"].bufs), V(Qall[0:64, gi], [Qgb[gi]]), eng="act")
                        kb_ = kbg[gi % 2]
                        self.tt(kb_[0:64], V(k_tok.ap[0:64, n0:n0 + 4, :], k_tok.bufs),
                                V(s_kbg.ap[0:64, n0:n0 + 4, c:c + 1].to_broadcast([64, 4, 128]), s_kbg.bufs), ALU.mult,
                                eng="dve")
                    if gi >= 1:
                        g1 = gi - 1
                        n0 = g1 * 4
                        kb_ = kbg[g1 % 2]
                        pw = self.PS(g1 % 2, 256, 128)
                        for i in range(4):
                            n = n0 + i
                            self.mm(pw[:, i * 64:(i + 1) * 64], V(kb_.ap[0:64, i, :], kb_.bufs),
                                    V(pd["QT"].ap[0:64, n, :], pd["QT"].bufs))
                        self.ts(V(pd["nwT"].ap[:, n0:n0 + 4, :].rearrange("p i c -> p (i c)"), pd["nwT"].bufs), pw, -1.0, ALU.mult)
            self.P.barrier()
            if STOP <= 2:
                continue
            self.memset(oT, 0.0, eng="dve")
            for d in range(2):
                self.memset(perd[d]["S"], 0.0)
                self.memset(perd[d]["Sb"], 0.0)
            def prep_step(step):
                out = []
                for d in range(2):
                    n = ORD[d][step]
                    c = d * 4 + h
                    pd = perd[d]
                    sl = (step % 2) * 2 + d
                    vb, kt, vn = vbt[sl], ktt[sl], vnw[sl]
                    self.ts(vb[0:64], V(v_tok.ap[0:64, n, :], v_tok.bufs), V(beta.ap[0:64, n, c:c + 1], beta.bufs),
                            ALU.mult, eng="dve")
                    self.ts(kt[0:64], V(k_tok.ap[0:64, n, :], k_tok.bufs), V(etail.ap[0:64, n, c:c + 1], etail.bufs),
                            ALU.mult, eng="dve")
                    out.append((d, n, c, pd, vb, kt, vn))
                return out

            nxt = prep_step(0)
            for step in range(NCH):
                ctxs = nxt
                pvs, pos = {}, {}
                for (d, n, c, pd, vb, kt, vn) in ctxs:
                    pv_ = self.PS(d * 3, 128, 64)
                    self.mm(pv_, V(pd["QT"].ap[0:64, n, :], pd["QT"].bufs), vb[0:64], start=True, stop=False)
                    self.mm(pv_, V(pd["nwT"].ap[:, n, :], pd["nwT"].bufs), pd["Sb"], start=False, stop=True)
                    po_ = self.PS(d * 3 + 1, 64, 128)
                    self.mm(po_, pd["Sb"], pd["qdT"][:, n * 64:(n + 1) * 64], start=True, stop=False)
                    pvs[d], pos[d] = pv_, po_
                for (d, n, c, pd, vb, kt, vn) in ctxs:
                    self.copy(vn[0:64], pvs[d], eng="act")
                pss = {}
                for (d, n, c, pd, vb, kt, vn) in ctxs:
                    self.mm(pos[d], vn[0:64], V(pd["AiT"].ap[0:64, n, :], pd["AiT"].bufs), start=False, stop=True)
                    ps_ = self.PS(d * 3 + 2, 128, 128)
                    self.mm(ps_, kt[0:64], vn[0:64])
                    pss[d] = ps_
                if step + 1 < NCH:
                    nxt = prep_step(step + 1)
                for (d, n, c, pd, vb, kt, vn) in ctxs:
                    self.stt(pd["Sb"], pd["S"], V(glast.ap[:, n, c:c + 1], glast.bufs), pss[d], ALU.mult, ALU.add)
                for (d, n, c, pd, vb, kt, vn) in ctxs:
                    self.stt(pd["S"], pd["S"], V(glast.ap[:, n, c:c + 1], glast.bufs), pss[d], ALU.mult, ALU.add)
                for (d, n, c, pd, vb, kt, vn) in ctxs:
                    ov = V(oT.ap[:, n * 64:(n + 1) * 64], [oTb[n]])
                    self.tt(ov, ov, pos[d], ALU.add)
            if STOP <= 3:
                continue
            def ovb_(tb):
                o, nt = TB[tb]
                return V(oT.ap[:, o:o + nt], [oTb[n] for n in range(o // 64, (o + nt) // 64)] + oT.bufs)

            def g1(tb):
                o, nt = TB[tb]
                sq = sqb[tb % 2]
                self.act(sq[:, 0:nt], ovb_(tb), AF.Square)
                pn = self.PS(6 + tb % 2, nt)
                self.mm(pn, self.ones_bf, sq[:, 0:nt])

            def g2(tb):
                o, nt = TB[tb]
                pn = self.PS(6 + tb % 2, nt)
                r_ = rr[tb % 2]
                self.act(r_[:, 0:nt], pn, AF.Ln, bias=self.eps_t, scale=1.0 / 128)
                self.act(r_[:, 0:nt], r_[:, 0:nt], AF.Exp, scale=-0.5)

            def g3(tb):
                o, nt = TB[tb]
                r_ = rr[tb % 2]
                gt_ = gts[tb % 2]
                self.stt(gt_[:, 0:nt], ovb_(tb), V(self.ggo.ap[:, l:l + 1], self.ggo.bufs), r_[:, 0:nt], ALU.mult, ALU.mult)
                go = gos[tb % 2]
                self.tt(go[:, 0:nt], gt_[:, 0:nt], szT[:, o:o + nt], ALU.mult)
                self.dma(V(self.mix_d[:, 4 + h, o:o + nt], [self.mixb[4 + h]]), go[:, 0:nt], eng="sp", sembuf=go.bufs[0])

            nb_ = len(TB)
            for i in range(nb_ + 2):
                if i < nb_:
                    g1(i)
                if 1 <= i <= nb_:
                    g2(i - 1)
                if i >= 2:
                    g3(i - 2)
            self.P.barrier()

def host_inputs(inputs, b, mixers=True):
    f = lambda a: np.ascontiguousarray(a, dtype=np.float32)
    m = {}
    m["x"] = f(inputs["x"][b])
    m["ctx"] = f(inputs["ctx"][b])
    cc = np.stack([inputs["c"][b], inputs["c_ctx"]], axis=-1)
    m["cc"] = f(cc.reshape(KC, 128, 2).transpose(1, 0, 2))
    m["w_ada"] = f(inputs["w_ada"])
    m["b_adaT"] = f(inputs["b_ada"].reshape(DEPTH, 48, 128).transpose(2, 0, 1))
    gv = np.stack([inputs["g_mix"], inputs["g_ffn"]], axis=1)
    m["gvecs"] = f(gv.reshape(DEPTH, 2, KC, 128).transpose(3, 0, 1, 2))
    m["g_finalT"] = f(inputs["g_final"].reshape(KC, 128).T)
    m["w_gate"] = f(inputs["w_gate"])
    m["w_up"] = f(inputs["w_up"])
    m["w_down"] = f(inputs["w_down"])
    m["ident"] = np.eye(128, dtype=np.float32)
    if not mixers:
        return m
    w_in = inputs["w_in"]
    m["w_in"] = f(w_in)
    idx = np.arange(32)
    a_, hf, fr = idx // 16, (idx // 8) % 2, idx % 8
    partner = a_ * 16 + (1 - hf) * 8 + fr
    wpe = np.zeros((DEPTH, D, 2, 96), np.float32)
    wpe[:, :, 0, 64:96] = w_in[:, :, 384:416]
    wpe[:, :, 1, 64:96] = w_in[:, :, 384 + partner]
    m["w_pe2"] = wpe
    wq = inputs["mla_w_q_up"]
    wq2 = np.stack([wq, wq], axis=2).astype(np.float32)
    for h in range(4):
        wq2[:, :, 1, h * 96 + 64 + idx] = wq[:, :, h * 96 + 64 + partner]
    m["w_q2"] = f(wq2)
    wkv = inputs["mla_w_kv_up"].reshape(DEPTH, 128, 4, 128)
    m["w_kv_nv"] = f(np.concatenate([wkv[..., :64].reshape(DEPTH, 128, 256), wkv[..., 64:].reshape(DEPTH, 128, 256)], -1))
    gq = inputs["mla_g_q"].reshape(DEPTH, 2, 128)
    mg = np.concatenate([gq, inputs["mla_g_kv"].reshape(DEPTH, 1, 128)], axis=1)
    m["mla_g"] = f(mg.transpose(2, 0, 1))
    m["ropeCS"] = _rope_tables()
    sel = np.zeros((128, 64), np.float32)
    sel[64, :] = 1.0
    m["sel65"] = sel
    m["nab"] = _na_tables(inputs["na_rel_bias"])
    m["w_out"] = f(inputs["w_out"])
    m["gdnc"] = _gdn_consts()
    ab = np.concatenate([inputs["dn_a_log"].reshape(DEPTH, 8), inputs["dn_dt_bias"].reshape(DEPTH, 8)], axis=1)
    m["gdn_ab"] = f(np.broadcast_to(ab[None], (128, DEPTH, 16)))
    cw = inputs["dn_conv_w"].reshape(DEPTH, 5, 3, 4, 128)
    m["gdn_cw"] = f(cw.transpose(4, 0, 2, 3, 1))
    m["gdn_go"] = f(inputs["dn_g_out"].T)
    return m


def _gdn_consts():
    if "gdn" in _CONST:
        return _CONST["gdn"]
    NEG = -30000.0
    t = np.arange(64)[:, None]
    i = np.arange(64)[None, :]
    out = np.zeros((128, 832), np.float32)
    for d in range(2):
        U = (t <= i) if d == 0 else (t >= i)
        SU = (t > i) if d == 0 else (t < i)
        Ma = (i < t) if d == 0 else (i > t)
        Mb = (t < i) if d == 0 else (t > i)
        Mc = (t <= i) if d == 0 else (t >= i)
        base = d * 320
        out[0:64, base:base + 64] = U
        out[0:64, base + 64:base + 128] = SU
        out[0:64, base + 128:base + 192] = np.where(Ma, 0.0, NEG)
        out[0:64, base + 192:base + 256] = np.where(Mb, 0.0, NEG)
        out[0:64, base + 256:base + 320] = np.where(Mc, 0.0, NEG)
    out[0:64, 640:704] = np.eye(64)
    out[0:64, 704:832] = 1.0
    _CONST["gdn"] = out
    return out


_CONST = {}


def _rope_tables():
    if "rope" in _CONST:
        return _CONST["rope"]
    t = np.arange(SEQ)
    pos = np.stack([t // 64, t % 64], axis=-1).astype(np.float32)
    inv = np.power(np.float32(10000.0), -np.arange(8, dtype=np.float32) / np.float32(8)).astype(np.float32)
    ang = (pos[:, :, None] * inv).astype(np.float32)
    cos, sin = np.cos(ang).astype(np.float32), np.sin(ang).astype(np.float32)
    tab = np.zeros((128, 2, SEQ), np.float32)
    for i in range(32):
        a_, hf, fr = i // 16, (i // 8) % 2, i % 8
        tab[64 + i, 0, :] = cos[:, a_, fr]
        tab[64 + i, 1, :] = sin[:, a_, fr] * (-1.0 if hf == 0 else 1.0)
    _CONST["rope"] = tab
    return tab


def _na_tables(rel_bias):
    NEG = np.float32(-30000.0)
    kc = np.arange(64)[:, None]
    qc = np.arange(64)[None, :]
    cs = np.clip(qc - 8, 0, 48)
    ok = (kc >= cs) & (kc < cs + 16)
    dc = np.clip(kc - qc + 15, 0, 30)
    tb = rel_bias[:, :, :, dc]
    tb = np.where(ok[None, None, None], tb, NEG).astype(np.float32)
    mask = np.full((DEPTH, 4, 64, 64), NEG, np.float32)
    tiles = []
    for dra in range(14):
        tiles.append(np.concatenate([tb[:, :, dra], tb[:, :, dra + 1]], axis=2))
    tiles.append(np.concatenate([mask, tb[:, :, 3]], axis=2))
    for dra in (4, 6, 8):
        tiles.append(np.concatenate([tb[:, :, dra], tb[:, :, dra + 1]], axis=2))
    tiles.append(np.concatenate([tb[:, :, 10], mask], axis=2))
    nab = np.stack(tiles, axis=2)
    return np.ascontiguousarray(nab.transpose(0, 3, 1, 2, 4), dtype=np.float32)


def build_nc(n_layers=DEPTH, mixers=True, dbg=None):
    nc = bass.Bass("TRN2", target_bir_lowering=False)
    es = ExitStack()
    b = Builder(nc, es, n_layers=n_layers, mixers=mixers, dbg=dbg)
    with es:
        b.build()
    return nc


def kernel(**inputs):
    nc = build_nc()
    in_maps = [host_inputs(inputs, b) for b in range(8)]
    res = run_bass_kernel_spmd(nc, in_maps, core_ids=list(range(8)))
    out = np.stack([np.asarray(r["out"], dtype=np.float32) for r in res.results], axis=0)
    return out
```

```python
import numpy as np
from contextlib import ExitStack
import concourse.bass as bass
import concourse.mybir as mybir
from concourse.bass_utils import run_bass_kernel_spmd

F32 = mybir.dt.float32
BF16 = mybir.dt.bfloat16
AF = mybir.ActivationFunctionType
ALU = mybir.AluOpType
AX = mybir.AxisListType

D = 1024
SEQ = 2048
CTXL = 256
T = SEQ + CTXL
DEPTH = 4
KC = 8
FFN = 2816
NH = FFN // 128
IN_W = 3248
EPS = 1e-6
TB = [(0, 512), (512, 512), (1024, 512), (1536, 512), (2048, 256)]
ENGS = ("pe", "act", "dve", "pool", "sp")
import os as _os
NOPFIX = int(_os.environ.get("NOPFIX", "0"))


class Buf:
    __slots__ = ("name", "w_eng", "w_dma", "r_eng", "r_dma", "dslot", "excl")

    def __init__(self, name):
        self.name = name
        self.excl = False
        self.w_eng = {}
        self.w_dma = []
        self.r_eng = {}
        self.r_dma = []
        self.dslot = None


class Rec:
    __slots__ = ("eng", "fn", "deps", "is_dma", "sembuf", "dval", "dslot", "need_inc", "ival", "semi")

    def __init__(self):
        self.need_inc = False
        self.ival = 0
        self.semi = 0
        self.dval = 0


class V:
    __slots__ = ("ap", "bufs")

    def __init__(self, ap, bufs):
        self.ap = ap
        self.bufs = bufs

    def __getitem__(self, k):
        return V(self.ap[k], self.bufs)

    def bc(self, shape):
        return V(self.ap.to_broadcast(shape), self.bufs)


def _bufs(vs):
    out = []
    for v in vs:
        if v is None or isinstance(v, (int, float)):
            continue
        out.extend(v.bufs)
    return out


def _ap(v):
    return v.ap if isinstance(v, V) else v


class Prog:
    SEM_EPOCH = 20000

    def __init__(self):
        self.recs = {e: [] for e in ENGS}
        self.dma_all = []
        self.slots = []
        self.free = []
        self.live = []

    def op(self, eng, fn, reads=(), writes=(), pwrites=(), dma_buf=None):
        r = Rec()
        r.eng = eng
        r.fn = fn
        r.is_dma = dma_buf is not None
        r.sembuf = dma_buf
        deps = {}

        def add(d, kind):
            if d is r:
                return
            if (not r.is_dma) and (not d.is_dma) and d.eng == eng:
                if eng == "pe" or kind != "raw":
                    return
            deps[id(d)] = d

        for b in reads:
            for w in b.w_eng.values():
                add(w, "raw")
            for w in b.w_dma:
                add(w, "raw")
            if b.excl:
                for x in b.r_eng.values():
                    if x.eng != eng:
                        add(x, "raw")
        for b in list(writes) + list(pwrites):
            for w in b.w_eng.values():
                add(w, "waw")
            for w in b.w_dma:
                add(w, "waw")
            for x in b.r_eng.values():
                add(x, "war")
            for x in b.r_dma:
                add(x, "war")
        r.deps = list(deps.values())
        for b in reads:
            if r.is_dma:
                b.r_dma.append(r)
            else:
                b.r_eng[eng] = r
        for b in writes:
            b.w_eng = {}
            b.w_dma = []
            b.r_eng = {}
            b.r_dma = []
        for b in list(writes) + list(pwrites):
            if r.is_dma:
                b.w_dma.append(r)
            else:
                b.w_eng[eng] = r
            b.r_eng = {}
            b.r_dma = []
        if r.is_dma:
            if dma_buf.dslot is None:
                if self.free:
                    dma_buf.dslot = self.free.pop()
                else:
                    self.slots.append(0)
                    dma_buf.dslot = len(self.slots) - 1
                self.live.append(dma_buf)
            self.slots[dma_buf.dslot] += 16
            r.dslot = dma_buf.dslot
            r.dval = self.slots[dma_buf.dslot]
            self.dma_all.append(r)
        self.recs[eng].append(r)
        return r

    def barrier(self):
        last = []
        for e in ENGS:
            for r in reversed(self.recs[e]):
                if not r.is_dma and r.fn is not None:
                    last.append(r)
                    break
        pend = list(self.dma_all)
        self.dma_all = []
        for b in self.live:
            if self.slots[b.dslot] < 40000:
                self.free.append(b.dslot)
            b.dslot = None
        self.live = []
        for e in ENGS:
            r = Rec()
            r.eng = e
            r.fn = None
            r.is_dma = False
            r.sembuf = None
            r.deps = [d for d in last if d.eng != e] + pend
            self.recs[e].append(r)

    def emit(self, nc, es, final_waits):
        for e in ENGS:
            for r in self.recs[e]:
                for d in r.deps:
                    if not d.is_dma:
                        d.need_inc = True
        nsem = {}
        for e in ENGS:
            c = 0
            si = 0
            for r in self.recs[e]:
                if r.need_inc:
                    c += 1
                    if c > self.SEM_EPOCH:
                        si += 1
                        c = 1
                    r.ival = c
                    r.semi = si
            nsem[e] = si + 1
        sems = {e: [es.enter_context(nc.semaphore("s_%s%d" % (e, i))) for i in range(nsem[e])] for e in ENGS}
        dsems = [es.enter_context(nc.semaphore("d_%d" % i)) for i in range(len(self.slots))]
        recs = self.recs

        def run(e, eng):
            seen = {}
            for r in recs[e]:
                need = {}
                for d in r.deps:
                    if d.is_dma:
                        key = ("d", d.dslot)
                        sem = dsems[d.dslot]
                        val = d.dval
                    else:
                        key = (d.eng, d.semi)
                        sem = sems[d.eng][d.semi]
                        val = d.ival
                    if seen.get(key, 0) >= val:
                        continue
                    if key not in need or need[key][1] < val:
                        need[key] = (sem, val)
                for key, (sem, val) in need.items():
                    eng.wait_ge(sem, val)
                    seen[key] = val
                if NOPFIX and e == "pe" and len(need) >= 2:
                    eng.nop()
                if r.fn is None:
                    continue
                ins = r.fn(eng)
                if r.is_dma:
                    ins.then_inc(dsems[r.dslot], 16)
                elif r.need_inc:
                    ins.then_inc(sems[e][r.semi], 1)
            if e == "sp":
                for d in final_waits:
                    eng.wait_ge(dsems[d.dslot], d.dval)

        block = es.enter_context(nc.Block())

        @block.sync
        def _(eng):
            run("sp", eng)

        @block.tensor
        def _(eng):
            run("pe", eng)

        @block.scalar
        def _(eng):
            run("act", eng)

        @block.vector
        def _(eng):
            run("dve", eng)

        @block.gpsimd
        def _(eng):
            run("pool", eng)


class Builder:
    def __init__(self, nc, es, n_layers=DEPTH, mixers=True, dbg=None):
        self.nc = nc
        self.es = es
        self.P = Prog()
        self.L = n_layers
        self.mixers = mixers
        self.dbg = dbg
        self.nbuf = 0

    def buf(self, name="b"):
        self.nbuf += 1
        return Buf("%s%d" % (name, self.nbuf))

    def sb(self, name, shape, dt):
        t = self.es.enter_context(self.nc.sbuf_tensor("sb_" + name, list(shape), dt))
        return t

    def sbv(self, name, shape, dt):
        t = self.sb(name, shape, dt)
        return V(t[:], [self.buf(name)])

    def dram_in(self, name, shape, dt=F32):
        return self.nc.dram_tensor(name, list(shape), dt, kind="ExternalInput").ap()

    def mm(self, out, lhsT, rhs, start=True, stop=True, extra_reads=()):
        o, l, r = out.ap, lhsT.ap, rhs.ap
        self.P.op("pe", lambda e: e.matmul(o, lhsT=l, rhs=r, start=start, stop=stop),
                  reads=_bufs([lhsT, rhs]) + list(extra_reads), writes=() if not start else (), pwrites=out.bufs)

    def tr(self, out, in_, ident):
        o, i, d = out.ap, in_.ap, ident.ap
        self.P.op("pe", lambda e: e.transpose(o, i, d), reads=_bufs([in_, ident]), pwrites=out.bufs)

    def act(self, out, in_, func, bias=None, scale=None, accum_out=None, eng="act"):
        o, i = out.ap, in_.ap
        kw = {}
        if bias is not None:
            kw["bias"] = _ap(bias)
        if scale is not None:
            kw["scale"] = _ap(scale)
        if accum_out is not None:
            kw["accum_out"] = accum_out.ap
        self.P.op("act", lambda e: e.activation(o, i, func, **kw),
                  reads=_bufs([in_, bias, scale]), pwrites=_bufs([out, accum_out]))

    def tt(self, out, in0, in1, op, eng="dve"):
        o, a, b = out.ap, in0.ap, in1.ap
        self.P.op(eng, lambda e: e.tensor_tensor(o, a, b, op), reads=_bufs([in0, in1]), pwrites=out.bufs)

    def ts(self, out, in0, s1, op0, s2=None, op1=None, eng="dve"):
        o, a = out.ap, in0.ap
        s1a, s2a = _ap(s1), _ap(s2)
        if op1 is None:
            fn = lambda e: e.tensor_scalar(o, a, s1a, None, op0)
        else:
            fn = lambda e: e.tensor_scalar(o, a, s1a, s2a, op0, op1)
        self.P.op(eng, fn, reads=_bufs([in0, s1, s2]), pwrites=out.bufs)

    def stt(self, out, in0, scalar, in1, op0, op1, eng="dve"):
        o, a, b = out.ap, in0.ap, in1.ap
        s = _ap(scalar)
        self.P.op(eng, lambda e: e.scalar_tensor_tensor(o, a, s, b, op0, op1),
                  reads=_bufs([in0, scalar, in1]), pwrites=out.bufs)

    def copy(self, out, in_, eng="dve"):
        o, i = out.ap, in_.ap
        if eng == "act":
            self.P.op("act", lambda e: e.copy(o, i), reads=in_.bufs, pwrites=out.bufs)
        else:
            self.P.op(eng, lambda e: e.tensor_copy(o, i), reads=in_.bufs, pwrites=out.bufs)

    def recip(self, out, in_):
        o, i = out.ap, in_.ap
        self.P.op("dve", lambda e: e.reciprocal(o, i), reads=in_.bufs, pwrites=out.bufs)

    def memset(self, out, val, eng="dve"):
        o = out.ap
        self.P.op(eng, lambda e: e.memset(o, val), reads=(), pwrites=out.bufs)

    def dma(self, out, in_, eng="sp", sembuf=None):
        o, i = _ap(out), _ap(in_)
        rb = in_.bufs if isinstance(in_, V) else []
        wb = out.bufs if isinstance(out, V) else []
        if sembuf is None:
            sembuf = (wb or rb)[0]
        return self.P.op(eng, lambda e: e.dma_start(out=o, in_=i), reads=rb, pwrites=wb, dma_buf=sembuf)

    def build(self):
        nc, P, L = self.nc, self.P, self.L
        x_d = self.dram_in("x", [SEQ, D])
        ctx_d = self.dram_in("ctx", [CTXL, D])
        cc_d = self.dram_in("cc", [128, KC, 2])
        wada_d = self.dram_in("w_ada", [DEPTH, D, 6 * D])
        bada_d = self.dram_in("b_adaT", [128, DEPTH, 48])
        gv_d = self.dram_in("gvecs", [128, DEPTH, 2, KC])
        gfin_d = self.dram_in("g_finalT", [128, KC])
        wg_d = self.dram_in("w_gate", [DEPTH, D, FFN])
        wu_d = self.dram_in("w_up", [DEPTH, D, FFN])
        wd_d = self.dram_in("w_down", [DEPTH, FFN, D])
        ident_d = self.dram_in("ident", [128, 128])
        if self.mixers:
            self.win_d = self.dram_in("w_in", [DEPTH, D, IN_W])
            self.wpe_d = self.dram_in("w_pe2", [DEPTH, D, 2, 96])
            self.wq2_d = self.dram_in("w_q2", [DEPTH, 256, 2, 384])
            self.wkv_d = self.dram_in("w_kv_nv", [DEPTH, 128, 512])
            mlag_d = self.dram_in("mla_g", [128, DEPTH, 3])
            self.rope_d = self.dram_in("ropeCS", [128, 2, SEQ])
            sel_d = self.dram_in("sel65", [128, 64])
            self.nab_d = self.dram_in("nab", [DEPTH, 128, 4, 19, 64])
            self.wout_d = self.dram_in("w_out", [DEPTH, D, D])
            gdnc_d = self.dram_in("gdnc", [128, 832])
            gabc_d = self.dram_in("gdn_ab", [128, DEPTH, 16])
            gcw_d = self.dram_in("gdn_cw", [128, DEPTH, 3, 4, 5])
            ggo_d = self.dram_in("gdn_go", [128, DEPTH])
        out_d = nc.dram_tensor("out", [SEQ, D], F32, kind="ExternalOutput").ap()
        self.out_d = out_d
        if self.dbg:
            self.dbg_d = nc.dram_tensor("dbg", list(self.dbg), F32, kind="ExternalOutput").ap()

        POOLW = 40960
        XW = KC * T
        pool_t = self.sb("pool", [128, POOLW], F32)
        self.pool_t, self.POOLW, self.XW = pool_t, POOLW, XW
        xT_t = pool_t[:, 0:XW].rearrange("p (k t) -> p k t", k=KC)
        hT_t = self.sb("hT", [128, KC, T], BF16)
        self.xT_t, self.hT_t = xT_t, hT_t
        xb = [[self.buf("x") for _ in TB] for _ in range(KC)]
        hb = [[self.buf("h") for _ in TB] for _ in range(KC)]

        def xT(kc, tb):
            o, n = TB[tb]
            return V(xT_t[:, kc, o:o + n], [xb[kc][tb]])

        def hT(kc, tb):
            o, n = TB[tb]
            return V(hT_t[:, kc, o:o + n], [hb[kc][tb]])

        self.xT, self.hT = xT, hT
        modT = self.sbv("modT", [128, DEPTH, 6, KC, 2], F32)
        gsc = self.sbv("gsc", [128, DEPTH, 2, KC, 2], F32)
        bT = self.sbv("bT", [128, DEPTH, 48], F32)
        gv = self.sbv("gv", [128, DEPTH, 2, KC], F32)
        gfin = self.sbv("gfin", [128, KC], F32)
        ident = self.sbv("ident", [128, 128], F32)
        ones_bf = self.sbv("ones_bf", [128, 128], BF16)
        cc = self.sbv("cc", [128, KC, 2], F32)
        scc = self.sbv("scc", [128, KC, 2], F32)
        self.modT, self.gsc, self.ident, self.ones_bf = modT, gsc, ident, ones_bf
        self.xb, self.hb = xb, hb
        if self.mixers:
            self.mlag = self.sbv("mlag", [128, DEPTH, 3], F32)
            self.sel65 = self.sbv("sel65", [128, 64], F32)
            self.dma(self.mlag, mlag_d)
            self.gdnc = self.sbv("gdnc", [128, 832], F32)
            self.gabc = self.sbv("gabc", [128, DEPTH, 16], F32)
            self.gcw = self.sbv("gcw", [128, DEPTH, 3, 4, 5], F32)
            self.ggo = self.sbv("ggo", [128, DEPTH], F32)
            self.one_t = self.sbv("one_t", [128, 1], F32)
            self.ident_bf = self.sbv("ident_bf", [128, 128], BF16)
            self.dma(self.gdnc, gdnc_d)
            self.dma(self.gabc, gabc_d)
            self.dma(self.gcw, gcw_d)
            self.dma(self.ggo, ggo_d)
            self.memset(self.one_t, 1.0)
            self.dma(self.sel65, sel_d)
        self.scr_base = XW
        self.scr_lim = POOLW
        self.ps = [self.es.enter_context(nc.psum_tensor("ps%d" % i, [128, 512], F32)) for i in range(6)]
        self.psD_t = self.es.enter_context(nc.psum_tensor("psD", [128, 1024], F32))
        self.psb = [self.buf("ps") for _ in range(8)]
        for b_ in self.psb:
            b_.excl = True

        def PS(i, n=512, p=128):
            if i >= 6:
                return V(self.psD_t[0:p, (i - 6) * 512:(i - 6) * 512 + n], [self.psb[i]])
            return V(self.ps[i][0:p, 0:n], [self.psb[i]])

        self.PS = PS

        self.dma(ident, ident_d)
        self.dma(cc, cc_d)
        self.dma(bT, bada_d)
        self.dma(gv, gv_d)
        self.dma(gfin, gfin_d)
        self.memset(ones_bf, 1.0)
        self.act(scc, cc, AF.Silu)
        if self.mixers:
            self.copy(self.ident_bf, ident)

        scr_off = [0]

        def carve(n_f32, dt=F32, shape=None, name="c"):
            a = self.scr_base + scr_off[0]
            scr_off[0] += n_f32
            assert a + n_f32 <= self.scr_lim, (name, a + n_f32)
            ap = pool_t[:, a:a + n_f32]
            if dt != F32:
                ap = ap.bitcast(dt)
            if shape is not None:
                names = " ".join("d%d" % i for i in range(len(shape)))
                kw = {"d%d" % i: s for i, s in enumerate(shape[:-1])}
                ap = ap.rearrange("p (%s) -> p %s" % (names, names), **kw)
            return V(ap, [self.buf(name)])

        self.carve = carve
        self.scr_off = scr_off
        wa = [carve(KC * 256, BF16, [KC, 512], "wa") for _ in range(3)]
        scc_bf = self.sbv("scc_bf", [128, KC, 2], BF16)
        self.copy(scc_bf, scc)
        n = 0
        for l in range(L):
            pst = PS(l % 2, 96)
            for cb in range(12):
                w = wa[n % 3]
                n += 1
                self.dma(w, wada_d[l, :, cb * 512:(cb + 1) * 512].rearrange("(k p) c -> p k c", p=128),
                         eng="pool")
                for jj in range(4):
                    j = cb * 4 + jj
                    for k in range(KC):
                        self.mm(pst[:, 2 * j:2 * j + 2], w[:, k, jj * 128:(jj + 1) * 128], scc_bf[:, k, :],
                                start=(k == 0), stop=(k == KC - 1))
            o = modT.ap[:, l].rearrange("p s k w -> p (s k) w")
            i0 = pst.ap.rearrange("p (j w) -> p j w", w=2)
            i1 = bT.ap[:, l, :].unsqueeze(2).to_broadcast([128, 48, 2])
            self.tt(V(o, modT.bufs), V(i0, pst.bufs), V(i1, bT.bufs), ALU.add)
            for which, (si, gi) in enumerate(((1, 0), (4, 1))):
                g_b = gv.ap[:, l, gi, :].unsqueeze(2).to_broadcast([128, KC, 2])
                self.stt(V(gsc.ap[:, l, which], gsc.bufs), V(modT.ap[:, l, si], modT.bufs), 1.0,
                         V(g_b, gv.bufs), ALU.add, ALU.mult)
        P.barrier()
        scr_off[0] = 0

        stg = [carve(4 * D, F32, [4, D], "stg") for _ in range(2)]
        n = 0
        for tb, (o, nt) in enumerate(TB):
            s = stg[tb % 2]
            ntile = nt // 128
            if tb < 4:
                src = x_d[o:o + nt, :]
            else:
                src = ctx_d[:, :]
            self.dma(s[:, 0:ntile, :], src.rearrange("(t p) d -> p t d", p=128), eng="sp")
            for kc in range(KC):
                pb = PS(n % 8, nt)
                n += 1
                for t in range(ntile):
                    self.tr(pb[:, t * 128:(t + 1) * 128], s[:, t, kc * 128:(kc + 1) * 128], ident)
                self.copy(xT(kc, tb), pb, eng="act" if kc % 2 else "dve")
        P.barrier()
        scr_off[0] = 0

        for l in range(L):
            self.norm_mod(l, 0)
            if self.mixers:
                self.mixer_phase(l)
            self.norm_mod(l, 1)
            self.ffn_phase(l, wg_d, wu_d, wd_d)
        self.final_phase(gfin)
        P.emit(nc, self.es, self.final_waits)

    def rstd_block(self, tb, sqs, rs_tmp, rstd):
        o, nt = TB[tb]
        ps = self.PS(tb % 2, nt)
        for kc in range(KC):
            sq = sqs[kc % 2]
            self.act(sq[:, 0:nt], self.xT(kc, tb), AF.Square)
            self.mm(ps, self.ones_bf, sq[:, 0:nt], start=(kc == 0), stop=(kc == KC - 1))
        self.act(rs_tmp[:, 0:nt], ps, AF.Ln, bias=self.eps_t, scale=1.0 / D)
        self.act(rstd[:, 0:nt], rs_tmp[:, 0:nt], AF.Exp, scale=-0.5)

    def norm_mod(self, l, which):
        P = self.P
        self.scr_off[0] = 0
        if not hasattr(self, "eps_t"):
            self.eps_t = self.sbv("eps_t", [128, 1], F32)
            self.memset(self.eps_t, EPS)
        sqs = [self.carve(256, BF16, None, "sq") for _ in range(2)]
        rs_tmp = self.carve(512, F32, None, "rs")
        rstds = [self.carve(512, F32, None, "rstd") for _ in range(2)]
        tmps = [self.carve(512, F32, None, "nt") for _ in range(2)]
        shift_i = 0 if which == 0 else 3
        n = 0
        for tb, (o, nt) in enumerate(TB):
            rstd = rstds[tb % 2]
            self.rstd_block(tb, sqs, rs_tmp, rstd)
            w = 0 if tb < 4 else 1
            for kc in range(KC):
                tmp = tmps[n % 2]
                n += 1
                self.stt(tmp[:, 0:nt], self.xT(kc, tb), V(self.gsc.ap[:, l, which, kc, w:w + 1], self.gsc.bufs),
                         rstd[:, 0:nt], ALU.mult, ALU.mult)
                self.act(self.hT(kc, tb), tmp[:, 0:nt], AF.Identity,
                         bias=V(self.modT.ap[:, l, shift_i, kc, w:w + 1], self.modT.bufs), scale=1.0)
        P.barrier()
        self.scr_off[0] = 0

    def ffn_phase(self, l, wg_d, wu_d, wd_d):
        P = self.P
        self.scr_off[0] = 0
        HC = NH // 2
        hid = self.sb_scr_hid()
        wgs = [self.carve(KC * 64, BF16, [KC, 128], "wg") for _ in range(3)]
        wus = [self.carve(KC * 64, BF16, [KC, 128], "wu") for _ in range(3)]
        wds = [self.carve(HC * 64, BF16, [HC, 128], "wd") for _ in range(2)]
        sil = [self.carve(512, F32, None, "sil") for _ in range(2)]
        n = 0
        nd = 0
        npb = 0
        for half in range(2):
            for jj in range(HC):
                j = half * HC + jj
                wg, wu = wgs[n % 3], wus[n % 3]
                n += 1
                self.dma(wg, wg_d[l, :, j * 128:(j + 1) * 128].rearrange("(k p) c -> p k c", p=128), eng="pool")
                self.dma(wu, wu_d[l, :, j * 128:(j + 1) * 128].rearrange("(k p) c -> p k c", p=128), eng="pool")
                for tb, (o, nt) in enumerate(TB):
                    pg = self.PS(npb % 4, nt)
                    pu = self.PS(4 + npb % 4, nt)
                    npb += 1
                    for kc in range(KC):
                        self.mm(pg, wg[:, kc, :], self.hT(kc, tb), start=(kc == 0), stop=(kc == KC - 1))
                    for kc in range(KC):
                        self.mm(pu, wu[:, kc, :], self.hT(kc, tb), start=(kc == 0), stop=(kc == KC - 1))
                    s = sil[npb % 2]
                    self.act(s[:, 0:nt], pg, AF.Silu)
                    self.tt(V(hid.ap[:, jj, o:o + nt], [self.hidb[jj][tb]]), s[:, 0:nt], pu, ALU.mult)
            for m in range(KC):
                wd = wds[nd % 2]
                nd += 1
                self.dma(wd, wd_d[l, half * HC * 128:(half + 1) * HC * 128, m * 128:(m + 1) * 128]
                         .rearrange("(j p) c -> p j c", p=128), eng="pool")
                for tb, (o, nt) in enumerate(TB):
                    po = self.PS(npb % 8, nt)
                    npb += 1
                    for jj in range(HC):
                        self.mm(po, wd[:, jj, :], V(hid.ap[:, jj, o:o + nt], [self.hidb[jj][tb]]),
                                start=(jj == 0), stop=(jj == HC - 1))
                    w = 0 if tb < 4 else 1
                    self.stt(self.xT(m, tb), po, V(self.modT.ap[:, l, 5, m, w:w + 1], self.modT.bufs),
                             self.xT(m, tb), ALU.mult, ALU.add)
        P.barrier()
        self.scr_off[0] = 0

    def sb_scr_hid(self):
        HC = NH // 2
        hid = self.carve(HC * T // 2, BF16, [HC, T], "hid")
        self.hidb = [[self.buf("hid") for _ in TB] for _ in range(HC)]
        return hid

    def final_phase(self, gfin):
        P = self.P
        self.scr_off[0] = 0
        sqs = [self.carve(256, BF16, None, "sq") for _ in range(2)]
        rs_tmp = self.carve(512, F32, None, "rs")
        rstds = [self.carve(512, F32, None, "rstd") for _ in range(2)]
        ys = [self.carve(512, F32, None, "y") for _ in range(3)]
        outs = [self.carve(4 * D, F32, [4, D], "os") for _ in range(2)]
        self.final_waits = []
        n = 0
        for tb in range(4):
            o, nt = TB[tb]
            rstd = rstds[tb % 2]
            self.rstd_block(tb, sqs, rs_tmp, rstd)
            ost = outs[tb % 2]
            for kc in range(KC):
                y = ys[n % 3]
                self.stt(y, self.xT(kc, tb), gfin[:, kc:kc + 1], rstd, ALU.mult, ALU.mult)
                pb = self.PS(2 + n % 6, 512)
                n += 1
                for t in range(4):
                    self.tr(pb[:, t * 128:(t + 1) * 128], y[:, t * 128:(t + 1) * 128], self.ident)
                dst = V(ost.ap[:, :, kc * 128:(kc + 1) * 128], ost.bufs)
                srcv = V(pb.ap.rearrange("p (t c) -> p t c", c=128), pb.bufs)
                self.copy(dst, srcv, eng="act" if kc % 2 else "dve")
            r = self.dma(self.out_d[o:o + nt, :].rearrange("(t p) d -> p t d", p=128), ost, eng="sp")
            self.final_waits.append(r)

    def load_w(self, dst, src_ap, eng="pool"):
        return self.dma(dst, src_ap, eng=eng)

    def mixer_phase(self, l):
        P = self.P
        nc = self.nc
        if not hasattr(self, "xsp_d"):
            self.xsp_d = nc.dram_tensor("xspill", [128, KC * T], F32).ap()
            self.xsp_b = self.buf("xsp")
        xall = V(self.pool_t[:, 0:self.XW], [b for row in self.xb for b in row])
        self.dma(V(self.xsp_d, [self.xsp_b]), xall, eng="sp", sembuf=self.xsp_b)
        P.barrier()
        self.scr_base = 0
        self.scr_lim = self.POOLW
        self.scr_off[0] = 0
        if not hasattr(self, "mix_d"):
            self.mix_d = nc.dram_tensor("mixspill", [128, KC, T], BF16).ap()
        self.mixb = [self.buf("mix") for _ in range(KC)]
        self.mla(l)
        P.barrier()
        self.scr_off[0] = 0
        self.na(l)
        P.barrier()
        self.scr_off[0] = 0
        self.gdn(l)
        P.barrier()
        self.dma(xall, V(self.xsp_d, [self.xsp_b]), eng="sp", sembuf=self.xsp_b)
        self.scr_base = self.XW
        self.scr_off[0] = 0
        mixs = self.carve(KC * T // 2, BF16, [KC, T], "mixs")
        mix_t = mixs.ap
        self.mix_t = mix_t
        for k in range(KC):
            self.dma(V(mix_t[:, k, :], [mixs.bufs[0]]), V(self.mix_d[:, k, :], [self.mixb[k]]), eng="sp" if k % 2 else "act",
                     sembuf=mixs.bufs[0])
        self.mixb = [mixs.bufs[0]] * KC
        if self.dbg and l == 0:
            self.dump_mix()
        wos = [self.carve(KC * 64, BF16, [KC, 128], "wo") for _ in range(2)]
        npb = 0
        for m in range(KC):
            wo = wos[m % 2]
            self.load_w(wo, self.wout_d[l, :, m * 128:(m + 1) * 128].rearrange("(k p) c -> p k c", p=128))
            for tb, (o, nt) in enumerate(TB):
                po = self.PS(npb % 8, nt)
                npb += 1
                for kk in range(KC):
                    self.mm(po, wo[:, kk, :], V(mix_t[:, kk, o:o + nt], [self.mixb[kk]]),
                            start=(kk == 0), stop=(kk == KC - 1))
                w = 0 if tb < 4 else 1
                self.stt(self.xT(m, tb), po, V(self.modT.ap[:, l, 2, m, w:w + 1], self.modT.bufs),
                         self.xT(m, tb), ALU.mult, ALU.add)
        P.barrier()
        self.scr_off[0] = 0
        self.scr_lim = self.POOLW

    def dump_mix(self):
        st = [self.carve(T, F32, None, "dst") for _ in range(2)]
        for k in range(KC):
            s_ = st[k % 2]
            self.copy(s_, V(self.mix_t[:, k, :], [self.mixb[k]]), eng="act")
            self.dma(self.dbg_d[k * 128:(k + 1) * 128, :], s_, eng="sp")

    def finish_attn(self, O_ps, n, pb, chunk, tok0, tmp):
        osb, rden, obuf = tmp
        self.copy(osb[0:65, 0:n], O_ps[0:65, 0:n], eng="act")
        den = self.PS(5, n, 64)
        self.mm(den, self.sel65[0:65, :], osb[0:65, 0:n])
        self.act(rden[0:64, 0:n], den, AF.Ln)
        self.act(rden[0:64, 0:n], rden[0:64, 0:n], AF.Exp, scale=-1.0)
        self.tt(obuf[0:64, 0:n], osb[0:64, 0:n], rden[0:64, 0:n], ALU.mult)
        dst = V(self.mix_d[pb:pb + 64, chunk, tok0:tok0 + n], [self.mixb[chunk]])
        self.dma(dst, obuf[0:64, 0:n], eng="sp", sembuf=obuf.bufs[0])

    def attn_dense(self, QT, KT_fn, V_fn, key_tiles, n, scale, dst, st):
        pts, fin_tmp = st["pts"], st["fin"]
        O_ps = self.PS(3 + st["nO"] % 2, n, 65)
        st["nO"] += 1
        nk = len(key_tiles)
        LA = 2
        pend = []
        for i in range(nk + LA):
            if i < nk:
                t = key_tiles[i]
                S_ps = self.PS(st["nS"] % 3, n)
                pt = pts[st["nS"] % 3]
                st["nS"] += 1
                self.mm(S_ps, KT_fn(t), QT)
                pend.append((i, t, S_ps, pt))
            if i >= LA:
                j, t, S_ps, pt = pend.pop(0)
                self.act(pt[:, 0:n], S_ps, AF.Exp, scale=scale)
                self.mm(O_ps, V_fn(t), pt[:, 0:n], start=(j == 0), stop=(j == nk - 1))
        self.finish_attn(O_ps, n, dst[0], dst[1], dst[2], fin_tmp[st["nO"] % 2])

    def attn_state(self):
        pts = [self.carve(256, BF16, None, "pt") for _ in range(3)]
        fin = [(self.carve(512, F32, None, "osb"), self.carve(512, F32, None, "rden"),
                self.carve(256, BF16, None, "obuf")) for _ in range(2)]
        return {"pts": pts, "fin": fin, "nO": 0, "nS": 0}

    def mla(self, l):
        hT = self.hT
        SC = 96 ** -0.5
        w_in3 = [self.carve(KC * 64, BF16, [KC, 128], "wmi") for _ in range(3)]
        for c in range(3):
            self.load_w(w_in3[c], self.win_d[l, :, c * 128:(c + 1) * 128].rearrange("(k p) c -> p k c", p=128))
        wpe = self.carve(KC * 96, BF16, [KC, 2, 96], "wpe")
        self.load_w(wpe, self.wpe_d[l].rearrange("(k p) a c -> p k a c", p=128))
        wq2 = self.carve(2 * 384, BF16, [2, 2, 384], "wq2")
        self.load_w(wq2, self.wq2_d[l].rearrange("(k p) a c -> p k a c", p=128))
        wkv = self.carve(256, BF16, None, "wkv")
        self.load_w(wkv, self.wkv_d[l])
        rope = self.carve(2 * SEQ, F32, [2, SEQ], "rope")
        self.dma(rope, self.rope_d, eng="sp")
        mqn = self.carve(T, BF16, [2, T], "mqn")
        mkvn = self.carve(T // 2, BF16, None, "mkvn")
        peR = self.carve(T // 2, BF16, None, "peR")
        vaug = self.carve(18 * 4 * 65 // 2, BF16, [18, 4, 65], "vaug")
        self.memset(V(vaug.ap[:, :, :, 64:65], vaug.bufs), 1.0, eng="pool")
        raw = [self.carve(512, F32, None, "raw") for _ in range(3)]
        sqs = [self.carve(256, BF16, None, "sq") for _ in range(3)]
        rt = [self.carve(512, F32, None, "rt") for _ in range(4)]
        r1 = self.carve(512, F32, None, "r1")
        r2 = self.carve(512, F32, None, "r2")
        for tb, (o, nt) in enumerate(TB):
            pr = [self.PS(6, nt), self.PS(7, nt), self.PS(5, nt)]
            for c in range(3):
                for kc in range(KC):
                    self.mm(pr[c], w_in3[c][:, kc, :], hT(kc, tb), start=(kc == 0), stop=(kc == KC - 1))
                self.copy(raw[c][:, 0:nt], pr[c], eng="act" if c % 2 else "dve")
                self.act(sqs[c][:, 0:nt], raw[c][:, 0:nt], AF.Square)
            ps_q = self.PS(3, nt)
            self.mm(ps_q, self.ones_bf, sqs[0][:, 0:nt], start=True, stop=False)
            self.mm(ps_q, self.ones_bf, sqs[1][:, 0:nt], start=False, stop=True)
            ps_k = self.PS(4, nt)
            self.mm(ps_k, self.ones_bf, sqs[2][:, 0:nt])
            self.act(rt[0][:, 0:nt], ps_q, AF.Ln, bias=self.eps_t, scale=1.0 / 256)
            self.act(rt[1][:, 0:nt], rt[0][:, 0:nt], AF.Exp, scale=-0.5)
            self.act(rt[2][:, 0:nt], ps_k, AF.Ln, bias=self.eps_t, scale=1.0 / 128)
            self.act(rt[3][:, 0:nt], rt[2][:, 0:nt], AF.Exp, scale=-0.5)
            for c in range(2):
                self.stt(V(mqn.ap[:, c, o:o + nt], mqn.bufs), raw[c][:, 0:nt],
                         V(self.mlag.ap[:, l, c:c + 1], self.mlag.bufs), rt[1][:, 0:nt], ALU.mult, ALU.mult)
            self.stt(mkvn[:, o:o + nt], raw[2][:, 0:nt], V(self.mlag.ap[:, l, 2:3], self.mlag.bufs),
                     rt[3][:, 0:nt], ALU.mult, ALU.mult)
            pp = [self.PS(0, nt, 96), self.PS(1, nt, 96)]
            for a in range(2):
                for kc in range(KC):
                    self.mm(pp[a], wpe[:, kc, a, :], hT(kc, tb), start=(kc == 0), stop=(kc == KC - 1))
            if tb < 4:
                self.tt(r1[64:96, 0:nt], pp[0][64:96, :], rope[64:96, 0, o:o + nt], ALU.mult)
                self.tt(r2[64:96, 0:nt], pp[1][64:96, :], rope[64:96, 1, o:o + nt], ALU.mult)
                self.tt(peR[64:96, o:o + nt], r1[64:96, 0:nt], r2[64:96, 0:nt], ALU.add)
            else:
                self.copy(peR[64:96, o:o + nt], pp[0][64:96, :], eng="act")
        for t in range(18):
            pv = self.PS(6 + t % 2, 256)
            self.mm(pv, mkvn[:, t * 128:(t + 1) * 128], wkv[:, 256:512])
            self.copy(V(vaug.ap[:, t, :, 0:64], vaug.bufs), V(pv.ap.rearrange("p (h d) -> p h d", h=4), pv.bufs),
                      eng="act" if t % 2 else "dve")
        QTs = [self.carve(T // 2, BF16, None, "QT") for _ in range(2)]
        KTs = [self.carve(T // 2, BF16, None, "KT") for _ in range(2)]
        st = self.attn_state()
        for h in range(4):
            QT, KT = QTs[h % 2], KTs[h % 2]
            for tb, (o, nt) in enumerate(TB):
                pq = [self.PS(6, nt, 96), self.PS(7, nt, 96)]
                na_ = 2 if tb < 4 else 1
                for a in range(na_):
                    for c in range(2):
                        self.mm(pq[a], wq2[:, c, a, h * 96:(h + 1) * 96], V(mqn.ap[:, c, o:o + nt], mqn.bufs),
                                start=(c == 0), stop=(c == 1))
                if tb < 4:
                    self.copy(QT[0:64, o:o + nt], pq[0][0:64, :], eng="act")
                    self.tt(r1[64:96, 0:nt], pq[0][64:96, :], rope[64:96, 0, o:o + nt], ALU.mult)
                    self.tt(r2[64:96, 0:nt], pq[1][64:96, :], rope[64:96, 1, o:o + nt], ALU.mult)
                    self.tt(QT[64:96, o:o + nt], r1[64:96, 0:nt], r2[64:96, 0:nt], ALU.add)
                else:
                    self.copy(QT[0:96, o:o + nt], pq[0][0:96, :], eng="act")
                pk = self.PS(5, nt, 64)
                self.mm(pk, wkv[:, h * 64:(h + 1) * 64], mkvn[:, o:o + nt])
                self.copy(KT[0:64, o:o + nt], pk, eng="dve")
            self.copy(KT[64:96, :], peR[64:96, :], eng="act")
            KT_fn = lambda t, KT=KT: KT[0:96, t * 128:(t + 1) * 128]
            V_fn = lambda t, h=h: V(vaug.ap[:, t, h, :], vaug.bufs)
            for qb in range(4):
                self.attn_dense(QT[0:96, qb * 512:(qb + 1) * 512], KT_fn, V_fn, list(range(18)), 512, SC,
                                ((h % 2) * 64, h // 2, qb * 512), st)
            self.attn_dense(QT[0:96, SEQ:T], KT_fn, V_fn, [16, 17], 256, SC, ((h % 2) * 64, h // 2, SEQ), st)

    def na(self, l):
        hT = self.hT
        SC = 64 ** -0.5
        nqk = self.carve(2 * T, BF16, [2, 2, T], "nqk")
        vaug = self.carve(18 * 4 * 65 // 2, BF16, [18, 4, 65], "vaugn")
        self.memset(V(vaug.ap[:, :, :, 64:65], vaug.bufs), 1.0, eng="pool")
        ws = [self.carve(KC * 64, BF16, [KC, 128], "wn") for _ in range(2)]
        n = 0
        for qk in range(2):
            for c in range(2):
                w = ws[n % 2]
                n += 1
                col = 416 + qk * 256 + c * 128
                self.load_w(w, self.win_d[l, :, col:col + 128].rearrange("(k p) c -> p k c", p=128))
                for tb, (o, nt) in enumerate(TB):
                    pp = self.PS(6 + tb % 2, nt)
                    for kc in range(KC):
                        self.mm(pp, w[:, kc, :], hT(kc, tb), start=(kc == 0), stop=(kc == KC - 1))
                    self.copy(V(nqk.ap[:, qk, c, o:o + nt], nqk.bufs), pp, eng="act" if tb % 2 else "dve")
        wv = self.carve(KC * 128, BF16, [KC, 256], "wnv")
        self.load_w(wv, self.win_d[l, :, 928:1184].rearrange("(k p) c -> p k c", p=128))
        for t in range(18):
            pv = self.PS(6 + t % 2, 256)
            for kc in range(KC):
                self.mm(pv, V(self.hT_t[:, kc, t * 128:(t + 1) * 128], [self.hb[kc][t // 4]]), wv[:, kc, :],
                        start=(kc == 0), stop=(kc == KC - 1))
            self.copy(V(vaug.ap[:, t, :, 0:64], vaug.bufs), V(pv.ap.rearrange("p (h d) -> p h d", h=4), pv.bufs),
                      eng="act" if t % 2 else "dve")
        nabs = self.carve(4 * 19 * 64, F32, [4, 19, 64], "nabs")
        self.dma(nabs, self.nab_d[l], eng="sp")
        E = self.carve(4 * 19 * 32, BF16, [4, 19, 64], "E")
        self.act(E, nabs, AF.Exp)
        st = self.attn_state()
        tmps = [self.carve(7 * 32, BF16, [7, 64], "ntmp") for _ in range(3)]
        ptl = [self.carve(5 * 32, BF16, [5, 64], "ptl") for _ in range(3)]
        nn = 0
        for h in range(4):
            c, pb = h // 2, (h % 2) * 64
            V_fn = lambda t, h=h: V(vaug.ap[:, t, h, :], vaug.bufs)
            LA = 2
            pend = []
            for rr_ in range(32 + LA):
                if rr_ < 32:
                    r = rr_
                    sr = min(max(r - 4, 0), 24)
                    if sr % 2 == 0:
                        nloc, t0 = 4, sr // 2
                        off = sr - r + 7
                        Esel = V(E.ap[:, h, off:off + 7:2, :], E.bufs)
                    else:
                        nloc, t0 = 5, (sr - 1) // 2
                        Esel = V(E.ap[:, h, 14:19, :], E.bufs)
                    tiles = [t0 + i for i in range(nloc)] + [16, 17]
                    ntl = len(tiles)
                    S_ps = self.PS(nn % 3, ntl * 64)
                    tmp = tmps[nn % 3]
                    pl = ptl[nn % 3]
                    nn += 1
                    q = V(nqk.ap[pb:pb + 64, 0, c, r * 64:(r + 1) * 64], nqk.bufs)
                    for i, t in enumerate(tiles):
                        k = V(nqk.ap[pb:pb + 64, 1, c, t * 128:(t + 1) * 128], nqk.bufs)
                        self.mm(S_ps[:, i * 64:(i + 1) * 64], k, q)
                    pend.append((r, tiles, nloc, Esel, S_ps, tmp, pl))
                if rr_ >= LA:
                    r, tiles, nloc, Esel, S_ps, tmp, pl = pend.pop(0)
                    ntl = len(tiles)
                    if r % 8 == 0:
                        O_ps = self.PS(3 + st["nO"] % 2, 512, 65)
                        st["nO"] += 1
                    self.act(V(tmp.ap[:, 0:ntl, :], tmp.bufs), V(S_ps.ap.rearrange("p (t q) -> p t q", q=64), S_ps.bufs),
                             AF.Exp, scale=SC)
                    self.tt(V(pl.ap[:, 0:nloc, :], pl.bufs), V(tmp.ap[:, 0:nloc, :], tmp.bufs), Esel, ALU.mult)
                    Or = O_ps[0:65, (r % 8) * 64:(r % 8 + 1) * 64]
                    for i, t in enumerate(tiles):
                        if i < nloc:
                            p_ = V(pl.ap[:, i, :], pl.bufs)
                        else:
                            p_ = V(tmp.ap[:, i, :], tmp.bufs)
                        self.mm(Or, V_fn(t), p_, start=(i == 0), stop=(i == ntl - 1))
                    if r % 8 == 7:
                        self.finish_attn(O_ps, 512, pb, 2 + c, (r - 7) * 64, st["fin"][st["nO"] % 2])
            KT_fn = lambda t, c=c, pb=pb: V(nqk.ap[pb:pb + 64, 1, c, t * 128:(t + 1) * 128], nqk.bufs)
            self.attn_dense(V(nqk.ap[pb:pb + 64, 0, c, SEQ:T], nqk.bufs), KT_fn, V_fn, [16, 17], 256, SC,
                            (pb, 2 + c, SEQ), st)

    def gdn(self, l):
        hT = self.hT
        NCH = 36
        DK = 128 ** -0.5
        gc_ = self.gdnc
        U = [gc_[0:64, d * 320:d * 320 + 64] for d in range(2)]
        SU = [gc_[0:64, d * 320 + 64:d * 320 + 128] for d in range(2)]
        MN = [gc_[0:64, d * 320 + 128:d * 320 + 320] for d in range(2)]
        I64 = gc_[0:64, 640:704]
        ONE = gc_[0:64, 704:832]
        ORD = [[32, 33, 34, 35] + list(range(32)), [35, 34, 33, 32] + list(range(31, -1, -1))]
        MPOS = []
        for d_ in range(2):
            mp = self.carve(64, F32, None, "mpos")
            self.ts(mp[0:64], MN[d_][:, 0:64], -1.0, ALU.mult)
            MPOS.append(mp[0:64])
        psD = lambda n0, n1, p=64: V(self.psD_t[0:p, n0:n1], [self.psb[6], self.psb[7]])
        wab = self.carve(KC * 8, BF16, [KC, 16], "wab")
        self.load_w(wab, self.win_d[l, :, 3232:3248].rearrange("(k p) c -> p k c", p=128))
        ab = self.carve(NCH * 16, F32, [NCH, 16], "ab")
        for n in range(NCH):
            for kc in range(KC):
                self.mm(psD(n * 16, n * 16 + 16),
                        V(self.hT_t[:, kc, n * 64:(n + 1) * 64], [self.hb[kc][n // 8]]), wab[:, kc, :],
                        start=(kc == 0), stop=(kc == KC - 1))
        self.copy(ab[0:64], V(psD(0, NCH * 16).ap.rearrange("p (n c) -> p n c", c=16), [self.psb[6], self.psb[7]]))
        abc = self.gabc
        nA = self.carve(8, F32, None, "nA")
        self.act(nA[0:64], V(abc.ap[0:64, l, 0:8], abc.bufs), AF.Exp)
        self.ts(nA[0:64], nA[0:64], -1.0, ALU.mult)
        g = self.carve(NCH * 8, F32, [NCH, 8], "g")
        beta = self.carve(NCH * 8, F32, [NCH, 8], "beta")
        sp = self.carve(NCH * 8, F32, [NCH, 8], "sp")
        dtb = V(abc.ap[0:64, l, 8:16].unsqueeze(1).to_broadcast([64, NCH, 8]), abc.bufs)
        self.tt(sp[0:64], V(ab.ap[0:64, :, 0:8], ab.bufs), dtb, ALU.add)
        self.act(sp[0:64], sp[0:64], AF.Exp)
        self.act(sp[0:64], sp[0:64], AF.Ln, bias=self.one_t[0:64], scale=1.0)
        self.tt(g[0:64], sp[0:64], V(nA.ap[0:64].unsqueeze(1).to_broadcast([64, NCH, 8]), nA.bufs), ALU.mult)
        self.act(beta[0:64], V(ab.ap[0:64, :, 8:16], ab.bufs), AF.Sigmoid)
        gcs = self.carve(NCH * 8, F32, [NCH, 8], "gcs")
        for d in range(2):
            pc = self.PS(d, NCH * 4, 64)
            self.mm(pc, U[d], V(g.ap[0:64, :, d * 4:(d + 1) * 4], g.bufs))
            self.copy(V(gcs.ap[0:64, :, d * 4:(d + 1) * 4], gcs.bufs),
                      V(pc.ap.rearrange("p (n c) -> p n c", c=4), pc.bufs))
        ptot = self.PS(2, NCH * 8, 128)
        self.mm(ptot, ONE, V(g.ap[0:64].rearrange("p n c -> p (n c)"), g.bufs))
        glast = self.carve(NCH * 8, F32, [NCH, 8], "glast")
        self.act(V(glast.ap.rearrange("p n c -> p (n c)"), glast.bufs), ptot, AF.Exp)
        etail = self.carve(NCH * 8, F32, [NCH, 8], "etail")
        self.tt(V(etail.ap[0:64].rearrange("p n c -> p (n c)"), etail.bufs), ptot[0:64, :],
                V(gcs.ap[0:64].rearrange("p n c -> p (n c)"), gcs.bufs), ALU.subtract)
        self.act(etail[0:64], etail[0:64], AF.Exp)
        eg = self.carve(NCH * 8, F32, [NCH, 8], "eg")
        self.act(eg[0:64], gcs[0:64], AF.Exp)
        s_kbg = self.carve(NCH * 8, F32, [NCH, 8], "skbg")
        self.tt(s_kbg[0:64], beta[0:64], eg[0:64], ALU.mult)
        s_q = self.carve(NCH * 8, F32, [NCH, 8], "sq_")
        self.ts(s_q[0:64], eg[0:64], DK, ALU.mult)
        import os
        STOP = float(os.environ.get("GDN_STOP", "99"))
        if STOP <= 0:
            return
        kT = self.carve(T // 2, BF16, None, "kT")
        qT = self.carve(T // 2, BF16, None, "qT")
        szT = self.carve(T // 2, BF16, None, "szT")
        k_tok = self.carve(NCH * 64, BF16, [NCH, 128], "ktok")
        v_tok = self.carve(NCH * 64, BF16, [NCH, 128], "vtok")
        kbT_sh = self.carve(T // 2, BF16, None, "kbT")
        oT = self.carve(T, F32, None, "oT")
        oTb = [self.buf("oT") for _ in range(NCH)]
        off_r = self.scr_base + self.scr_off[0]
        rawp = self.carve(T + 8, F32, None, "rawp")
        acc = self.carve(T, F32, None, "acc")
        XYall = self.pool_t[:, off_r:off_r + 9 * 512].rearrange("p (g a i c) -> p g a i c", g=9, a=2, i=4)
        XYb = [self.buf("XYb") for _ in range(9)]
        Qgb = [self.buf("Qgb") for _ in range(9)]
        perd = []
        for d in range(2):
            perd.append(dict(
                kbT=kbT_sh, qdT=self.carve(T // 2, BF16, None, "qdT"),
                QT=self.carve(NCH * 32, BF16, [NCH, 64], "QTa"), AiT=self.carve(NCH * 32, BF16, [NCH, 64], "AiT"),
                nwT=self.carve(NCH * 32, BF16, [NCH, 64], "nwT"),
                S=self.carve(128, F32, None, "S"), Sb=self.carve(64, BF16, None, "Sb")))
        wq_ = [self.carve(KC * 64, BF16, [KC, 128], "wg_") for _ in range(2)]
        sqb = [self.carve(256, BF16, None, "sqb") for _ in range(2)]
        rr = [self.carve(512, F32, None, "rr") for _ in range(2)]
        diag = [self.carve(512, F32, [8, 64], "diag") for _ in range(2)]
        GU = [self.carve(256, F32, [4, 64], "GU") for _ in range(2)]
        Dall = [self.carve(768, F32, [4, 192], "Dall") for _ in range(2)]
        Qall = self.carve(9 * 256, F32, [9, 4, 64], "Qall").ap
        kbg = [self.carve(256, BF16, [4, 128], "kbg") for _ in range(2)]
        vbt = [self.carve(64, BF16, None, "vb") for _ in range(4)]
        ktt = [self.carve(64, BF16, None, "kt") for _ in range(4)]
        vnw = [self.carve(64, BF16, None, "vn") for _ in range(4)]
        gts = [self.carve(512, F32, None, "gt") for _ in range(2)]
        gos = [self.carve(256, BF16, None, "go") for _ in range(2)]
        self.memset(rawp[:, 0:2], 0.0)
        self.memset(rawp[:, 2 + SEQ:2 + SEQ + 4], 0.0)
        self.memset(rawp[:, T + 6:T + 8], 0.0)
        nw = 0
        for h in range(4):
            for which in (0, 3, 1, 2):
                w = wq_[nw % 2]
                nw += 1
                col = 1184 + which * 512 + h * 128
                self.load_w(w, self.win_d[l, :, col:col + 128].rearrange("(k p) c -> p k c", p=128))
                for tb, (o, nt) in enumerate(TB):
                    pp = self.PS(tb % 2, nt)
                    for kc in range(KC):
                        self.mm(pp, w[:, kc, :], hT(kc, tb), start=(kc == 0), stop=(kc == KC - 1))
                    if which == 3:
                        self.act(szT[:, o:o + nt], pp, AF.Silu)
                    else:
                        oo = 2 + o if tb < 4 else 6 + o
                        self.copy(rawp[:, oo:oo + nt], pp, eng="act")
                if which == 3:
                    continue
                cw = lambda j: V(self.gcw.ap[:, l, which, h, j:j + 1], self.gcw.bufs)
                for (o0, n0, p0) in ((0, SEQ, 0), (SEQ, CTXL, SEQ + 4)):
                    a_ = acc[:, o0:o0 + n0]
                    self.act(a_, rawp[:, p0:p0 + n0], AF.Identity, scale=cw(0))
                    for j in range(1, 5):
                        self.stt(a_, rawp[:, p0 + j:p0 + j + n0], cw(j), a_, ALU.mult, ALU.add)
                self.act(acc, acc, AF.Silu)
                if which == 2:
                    for g4 in range(0, NCH, 4):
                        pt_ = self.PS(4 + (g4 // 4) % 2, 512, 64)
                        for i in range(4):
                            n = g4 + i
                            self.tr(pt_[:, i * 128:(i + 1) * 128], acc[:, n * 64:(n + 1) * 64], self.ident)
                        self.copy(V(v_tok.ap[0:64, g4:g4 + 4, :], v_tok.bufs),
                                  V(pt_.ap.rearrange("p (n c) -> p n c", c=128), pt_.bufs), eng="act")
                    continue
                dst = qT if which == 0 else kT

                def n1(tb):
                    o, nt = TB[tb]
                    sq = sqb[tb % 2]
                    self.act(sq[:, 0:nt], acc[:, o:o + nt], AF.Square)
                    pn = self.PS(2 + tb % 2, nt)
                    self.mm(pn, self.ones_bf, sq[:, 0:nt])

                def n2(tb):
                    o, nt = TB[tb]
                    pn = self.PS(2 + tb % 2, nt)
                    r_ = rr[tb % 2]
                    self.act(r_[:, 0:nt], pn, AF.Ln, bias=self.eps_t, scale=1.0)
                    self.act(r_[:, 0:nt], r_[:, 0:nt], AF.Exp, scale=-0.5)

                def n3(tb, dst=dst):
                    o, nt = TB[tb]
                    r_ = rr[tb % 2]
                    self.tt(dst[:, o:o + nt], acc[:, o:o + nt], r_[:, 0:nt], ALU.mult)

                nb_ = len(TB)
                for i in range(nb_ + 2):
                    if i < nb_:
                        n1(i)
                    if 1 <= i <= nb_:
                        n2(i - 1)
                    if i >= 2:
                        n3(i - 2)
            for (src, dstt) in ((kT, k_tok),):
                for g8 in range(0, NCH, 8):
                    cnt = min(8, NCH - g8)
                    pt_ = V(self.ps[4 + (g8 // 8) % 2][0:64, 0:512].bitcast(BF16), [self.psb[4 + (g8 // 8) % 2]])
                    for i in range(cnt):
                        n = g8 + i
                        self.tr(pt_[:, i * 128:(i + 1) * 128], src[:, n * 64:(n + 1) * 64], self.ident_bf)
                    self.copy(V(dstt.ap[0:64, g8:g8 + cnt, :], dstt.bufs),
                              V(pt_.ap[:, 0:cnt * 128].rearrange("p (n c) -> p n c", c=128), pt_.bufs), eng="act")
            self.P.barrier()
            if STOP <= 1:
                continue
            for d in range(2):
                c = d * 4 + h
                pd = perd[d]
                jobs = []
                for (srcT, scal, dstT) in ((kT, beta, pd["kbT"]), (qT, s_q, pd["qdT"])):
                    for n0 in range(0, NCH, 8):
                        jobs.append((srcT, scal, dstT, n0, min(8, NCH - n0)))
                pendb = []
                for bi in range(len(jobs) + 1):
                    if bi < len(jobs):
                        srcT, scal, dstT, n0, cnt = jobs[bi]
                        dg = diag[bi % 2]
                        self.tt(V(dg.ap[0:64, 0:cnt, :], dg.bufs),
                                V(I64.ap.unsqueeze(1).to_broadcast([64, cnt, 64]), I64.bufs),
                                V(scal.ap[0:64, n0:n0 + cnt, c:c + 1].to_broadcast([64, cnt, 64]), scal.bufs), ALU.mult)
                        pb_ = self.PS(bi % 2, cnt * 64)
                        self.mm(pb_, ONE, V(dg.ap[0:64, 0:cnt, :].rearrange("p n c -> p (n c)"), dg.bufs))
                        pendb.append((srcT, dstT, n0, cnt, pb_))
                    if bi >= 1:
                        srcT, dstT, n0, cnt, pb_ = pendb.pop(0)
                        self.tt(dstT[:, n0 * 64:(n0 + cnt) * 64], srcT[:, n0 * 64:(n0 + cnt) * 64], pb_, ALU.mult)
                if STOP <= 2.1:
                    continue
                NG = NCH // 4
                def g_job(bi):
                    n0 = bi * 8
                    cnt = min(8, NCH - n0)
                    dg = diag[bi % 2]
                    self.tt(V(dg.ap[0:64, 0:cnt, :], dg.bufs),
                            V(I64.ap.unsqueeze(1).to_broadcast([64, cnt, 64]), I64.bufs),
                            V(gcs.ap[0:64, n0:n0 + cnt, c:c + 1].to_broadcast([64, cnt, 64]), gcs.bufs), ALU.mult)
                    gp = self.PS(6 + bi % 2, cnt * 64, 64)
                    self.mm(gp, ONE[:, 0:64], V(dg.ap[0:64, 0:cnt, :].rearrange("p n c -> p (n c)"), dg.bufs))
                    return gp

                gps = {}

                def setup_front(gi):
                    n0 = gi * 4
                    t1, da = GU[gi % 2], Dall[gi % 2]
                    if gi % 2 == 0:
                        gps[gi // 2] = g_job(gi // 2)
                    gp = gps[gi // 2]
                    pk = self.PS(5 if gi % 2 == 0 else 3, 512, 64)
                    pa = self.PS(4 if gi % 2 == 0 else 2, 256, 64)
                    for i in range(4):
                        n = n0 + i
                        self.mm(pk[:, i * 128:i * 128 + 64], pd["kbT"][:, n * 64:(n + 1) * 64], kT[:, n * 64:(n + 1) * 64])
                        self.mm(pk[:, i * 128 + 64:i * 128 + 128], kT[:, n * 64:(n + 1) * 64], pd["kbT"][:, n * 64:(n + 1) * 64])
                        self.mm(pa[:, i * 64:(i + 1) * 64], kT[:, n * 64:(n + 1) * 64], qT[:, n * 64:(n + 1) * 64])
                    gsl = V(gp.ap[:, (gi % 2) * 256:(gi % 2) * 256 + 256].rearrange("p (i c) -> p i c", c=64), gp.bufs)
                    self.tt(t1[0:64], gsl, V(gcs.ap[0:64, n0:n0 + 4, c:c + 1].to_broadcast([64, 4, 64]), gcs.bufs),
                            ALU.subtract)
                    da_a = V(da.ap[0:64, :, 0:64], da.bufs)
                    da_b = V(da.ap[0:64, :, 64:128], da.bufs)
                    self.tt(da_a, t1[0:64], V(MPOS[d].ap.unsqueeze(1).to_broadcast([64, 4, 64]), MPOS[d].bufs), ALU.max)
                    self.tt(da_b, t1[0:64], V(MN[d].ap[:, 64:128].unsqueeze(1).to_broadcast([64, 4, 64]), MN[d].bufs),
                            ALU.min)
                    self.act(da_a, da_a, AF.Exp, scale=-1.0)
                    self.act(da_b, da_b, AF.Exp)
                    self.tt(V(da.ap[0:64, :, 128:192], da.bufs), da_b,
                            V(I64.ap.unsqueeze(1).to_broadcast([64, 4, 64]), I64.bufs), ALU.add)
                    return pk, pa

                def setup_back(gi, pk, pa):
                    n0 = gi * 4
                    da = Dall[gi % 2]
                    pk3 = V(pk.ap.rearrange("p (i c) -> p i c", c=128), pk.bufs)
                    X = V(XYall[0:64, gi, 0], [XYb[gi]])
                    Y = V(XYall[0:64, gi, 1], [XYb[gi]])
                    Q = V(Qall[0:64, gi], [Qgb[gi]])
                    self.stt(X, pk3[:, :, 0:64], -1.0, V(da.ap[0:64, :, 0:64], da.bufs), ALU.mult, ALU.mult)
                    self.stt(Y, pk3[:, :, 64:128], -1.0, V(da.ap[0:64, :, 64:128], da.bufs), ALU.mult, ALU.mult)
                    self.stt(V(pd["AiT"].ap[0:64, n0:n0 + 4, :], pd["AiT"].bufs),
                             V(pa.ap.rearrange("p (i c) -> p i c", c=64), pa.bufs), DK,
                             V(da.ap[0:64, :, 128:192], da.bufs), ALU.mult, ALU.mult)
                    self.tt(Q, Y, V(I64.ap.unsqueeze(1).to_broadcast([64, 4, 64]), I64.bufs), ALU.add, eng="dve")

                prev = None
                for gi in range(NG + 1):
                    cur = None
                    if gi < NG:
                        cur = (gi,) + setup_front(gi)
                    if prev is not None:
                        setup_back(*prev)
                    prev = cur
                if STOP <= 2.2:
                    continue
                nps = 0
                for m in range(1, 6):
                    for gi in range(NG):
                        pxy = self.PS(nps % 3, 512, 64)
                        nps += 1
                        Xi = lambda i: V(XYall[0:64, gi, 0, i, :], [XYb[gi]])
                        Yi = lambda i: V(XYall[0:64, gi, 1, i, :], [XYb[gi]])
                        for i in range(4):
                            self.mm(pxy[:, i * 64:(i + 1) * 64], Yi(i), Xi(i))
                        nc_ = 256
                        if m < 5:
                            nc_ = 512
                            for i in range(4):
                                self.mm(pxy[:, 256 + i * 64:256 + (i + 1) * 64], Xi(i), Yi(i))
                        self.copy(V(XYall[0:64, gi].rearrange("p a i c -> p (a i c)")[:, 0:nc_], [XYb[gi]]), pxy[:, 0:nc_],
                                  eng="act")
                    for gi in range(NG):
                        pq_ = self.PS(3 + gi % 2, 256, 64)
                        for i in range(4):
                            self.mm(pq_[:, i * 64:(i + 1) * 64], V(XYall[0:64, gi, 0, i, :], [XYb[gi]]),
                                    V(Qall[0:64, gi, i, :], [Qgb[gi]]))
                        Qf_ = V(Qall[0:64, gi].rearrange("p i c -> p (i c)"), [Qgb[gi]])
                        self.tt(Qf_, Qf_, pq_, ALU.add)
                if STOP <= 2.3:
                    continue
                for gi in range(NG + 1):
                    if gi < NG:
                        n0 = gi * 4
                        self.copy(V(pd["QT"].ap[0:64, n0:n0 + 4, :], pd["QT"].bufs), V(Qall[0:64, gi], [Qgb[gi]]), eng="act")
                        kb_ = kbg[gi % 2]
                        self.tt(kb_[0:64], V(k_tok.ap[0:64, n0:n0 + 4, :], k_tok.bufs),
                                V(s_kbg.ap[0:64, n0:n0 + 4, c:c + 1].to_broadcast([64, 4, 128]), s_kbg.bufs), ALU.mult,
                                eng="dve")
                    if gi >= 1:
                        g1 = gi - 1
                        n0 = g1 * 4
                        kb_ = kbg[g1 % 2]
                        pw = self.PS(g1 % 2, 256, 128)
                        for i in range(4):
                            n = n0 + i
                            self.mm(pw[:, i * 64:(i + 1) * 64], V(kb_.ap[0:64, i, :], kb_.bufs),
                                    V(pd["QT"].ap[0:64, n, :], pd["QT"].bufs))
                        self.ts(V(pd["nwT"].ap[:, n0:n0 + 4, :].rearrange("p i c -> p (i c)"), pd["nwT"].bufs), pw, -1.0, ALU.mult)
            self.P.barrier()
            if STOP <= 2:
                continue
            self.memset(oT, 0.0, eng="dve")
            for d in range(2):
                self.memset(perd[d]["S"], 0.0)
                self.memset(perd[d]["Sb"], 0.0)
            def prep_step(step):
                out = []
                for d in range(2):
                    n = ORD[d][step]
                    c = d * 4 + h
                    pd = perd[d]
                    sl = (step % 2) * 2 + d
                    vb, kt, vn = vbt[sl], ktt[sl], vnw[sl]
                    self.act(vb[0:64], V(v_tok.ap[0:64, n, :], v_tok.bufs), AF.Identity,
                             scale=V(beta.ap[0:64, n, c:c + 1], beta.bufs))
                    self.act(kt[0:64], V(k_tok.ap[0:64, n, :], k_tok.bufs), AF.Identity,
                             scale=V(etail.ap[0:64, n, c:c + 1], etail.bufs))
                    out.append((d, n, c, pd, vb, kt, vn))
                return out

            nxt = prep_step(0)
            for step in range(NCH):
                ctxs = nxt
                pvs, pos = {}, {}
                for (d, n, c, pd, vb, kt, vn) in ctxs:
                    pv_ = self.PS(d * 3, 128, 64)
                    self.mm(pv_, V(pd["QT"].ap[0:64, n, :], pd["QT"].bufs), vb[0:64], start=True, stop=False)
                    self.mm(pv_, V(pd["nwT"].ap[:, n, :], pd["nwT"].bufs), pd["Sb"], start=False, stop=True)
                    po_ = self.PS(d * 3 + 1 if step % 2 == 0 else 6 + d, 64, 128)
                    self.mm(po_, pd["Sb"], pd["qdT"][:, n * 64:(n + 1) * 64], start=True, stop=False)
                    pvs[d], pos[d] = pv_, po_
                for (d, n, c, pd, vb, kt, vn) in ctxs:
                    self.copy(vn[0:64], pvs[d], eng="act")
                pss = {}
                for (d, n, c, pd, vb, kt, vn) in ctxs:
                    self.mm(pos[d], vn[0:64], V(pd["AiT"].ap[0:64, n, :], pd["AiT"].bufs), start=False, stop=True)
                    ps_ = self.PS(d * 3 + 2, 128, 128)
                    self.mm(ps_, kt[0:64], vn[0:64])
                    pss[d] = ps_
                if step + 1 < NCH:
                    nxt = prep_step(step + 1)
                for (d, n, c, pd, vb, kt, vn) in ctxs:
                    self.stt(pd["Sb"], pd["S"], V(glast.ap[:, n, c:c + 1], glast.bufs), pss[d], ALU.mult, ALU.add)
                for (d, n, c, pd, vb, kt, vn) in ctxs:
                    ov = V(oT.ap[:, n * 64:(n + 1) * 64], [oTb[n]])
                    self.tt(ov, ov, pos[d], ALU.add)
                for (d, n, c, pd, vb, kt, vn) in ctxs:
                    self.stt(pd["S"], pd["S"], V(glast.ap[:, n, c:c + 1], glast.bufs), pss[d], ALU.mult, ALU.add)
            if STOP <= 3:
                continue
            def ovb_(tb):
                o, nt = TB[tb]
                return V(oT.ap[:, o:o + nt], [oTb[n] for n in range(o // 64, (o + nt) // 64)] + oT.bufs)

            def g1(tb):
                o, nt = TB[tb]
                sq = sqb[tb % 2]
                self.act(sq[:, 0:nt], ovb_(tb), AF.Square)
                pn = self.PS(6 + tb % 2, nt)
                self.mm(pn, self.ones_bf, sq[:, 0:nt])

            def g2(tb):
                o, nt = TB[tb]
                pn = self.PS(6 + tb % 2, nt)
                r_ = rr[tb % 2]
                self.act(r_[:, 0:nt], pn, AF.Ln, bias=self.eps_t, scale=1.0 / 128)
                self.act(r_[:, 0:nt], r_[:, 0:nt], AF.Exp, scale=-0.5)

            def g3(tb):
                o, nt = TB[tb]
                r_ = rr[tb % 2]
                gt_ = gts[tb % 2]
                self.stt(gt_[:, 0:nt], ovb_(tb), V(self.ggo.ap[:, l:l + 1], self.ggo.bufs), r_[:, 0:nt], ALU.mult, ALU.mult)
                go = gos[tb % 2]
                self.tt(go[:, 0:nt], gt_[:, 0:nt], szT[:, o:o + nt], ALU.mult)
                self.dma(V(self.mix_d[:, 4 + h, o:o + nt], [self.mixb[4 + h]]), go[:, 0:nt], eng="sp", sembuf=go.bufs[0])

            nb_ = len(TB)
            for i in range(nb_ + 2):
                if i < nb_:
                    g1(i)
                if 1 <= i <= nb_:
                    g2(i - 1)
                if i >= 2:
                    g3(i - 2)
            self.P.barrier()

def host_inputs(inputs, b, mixers=True):
    f = lambda a: np.ascontiguousarray(a, dtype=np.float32)
    m = {}
    m["x"] = f(inputs["x"][b])
    m["ctx"] = f(inputs["ctx"][b])
    cc = np.stack([inputs["c"][b], inputs["c_ctx"]], axis=-1)
    m["cc"] = f(cc.reshape(KC, 128, 2).transpose(1, 0, 2))
    m["w_ada"] = f(inputs["w_ada"])
    m["b_adaT"] = f(inputs["b_ada"].reshape(DEPTH, 48, 128).transpose(2, 0, 1))
    gv = np.stack([inputs["g_mix"], inputs["g_ffn"]], axis=1)
    m["gvecs"] = f(gv.reshape(DEPTH, 2, KC, 128).transpose(3, 0, 1, 2))
    m["g_finalT"] = f(inputs["g_final"].reshape(KC, 128).T)
    m["w_gate"] = f(inputs["w_gate"])
    m["w_up"] = f(inputs["w_up"])
    m["w_down"] = f(inputs["w_down"])
    m["ident"] = np.eye(128, dtype=np.float32)
    if not mixers:
        return m
    w_in = inputs["w_in"]
    m["w_in"] = f(w_in)
    idx = np.arange(32)
    a_, hf, fr = idx // 16, (idx // 8) % 2, idx % 8
    partner = a_ * 16 + (1 - hf) * 8 + fr
    wpe = np.zeros((DEPTH, D, 2, 96), np.float32)
    wpe[:, :, 0, 64:96] = w_in[:, :, 384:416]
    wpe[:, :, 1, 64:96] = w_in[:, :, 384 + partner]
    m["w_pe2"] = wpe
    wq = inputs["mla_w_q_up"]
    wq2 = np.stack([wq, wq], axis=2).astype(np.float32)
    for h in range(4):
        wq2[:, :, 1, h * 96 + 64 + idx] = wq[:, :, h * 96 + 64 + partner]
    m["w_q2"] = f(wq2)
    wkv = inputs["mla_w_kv_up"].reshape(DEPTH, 128, 4, 128)
    m["w_kv_nv"] = f(np.concatenate([wkv[..., :64].reshape(DEPTH, 128, 256), wkv[..., 64:].reshape(DEPTH, 128, 256)], -1))
    gq = inputs["mla_g_q"].reshape(DEPTH, 2, 128)
    mg = np.concatenate([gq, inputs["mla_g_kv"].reshape(DEPTH, 1, 128)], axis=1)
    m["mla_g"] = f(mg.transpose(2, 0, 1))
    m["ropeCS"] = _rope_tables()
    sel = np.zeros((128, 64), np.float32)
    sel[64, :] = 1.0
    m["sel65"] = sel
    m["nab"] = _na_tables(inputs["na_rel_bias"])
    m["w_out"] = f(inputs["w_out"])
    m["gdnc"] = _gdn_consts()
    ab = np.concatenate([inputs["dn_a_log"].reshape(DEPTH, 8), inputs["dn_dt_bias"].reshape(DEPTH, 8)], axis=1)
    m["gdn_ab"] = f(np.broadcast_to(ab[None], (128, DEPTH, 16)))
    cw = inputs["dn_conv_w"].reshape(DEPTH, 5, 3, 4, 128)
    m["gdn_cw"] = f(cw.transpose(4, 0, 2, 3, 1))
    m["gdn_go"] = f(inputs["dn_g_out"].T)
    return m


def _gdn_consts():
    if "gdn" in _CONST:
        return _CONST["gdn"]
    NEG = -30000.0
    t = np.arange(64)[:, None]
    i = np.arange(64)[None, :]
    out = np.zeros((128, 832), np.float32)
    for d in range(2):
        U = (t <= i) if d == 0 else (t >= i)
        SU = (t > i) if d == 0 else (t < i)
        Ma = (i < t) if d == 0 else (i > t)
        Mb = (t < i) if d == 0 else (t > i)
        Mc = (t <= i) if d == 0 else (t >= i)
        base = d * 320
        out[0:64, base:base + 64] = U
        out[0:64, base + 64:base + 128] = SU
        out[0:64, base + 128:base + 192] = np.where(Ma, 0.0, NEG)
        out[0:64, base + 192:base + 256] = np.where(Mb, 0.0, NEG)
        out[0:64, base + 256:base + 320] = np.where(Mc, 0.0, NEG)
    out[0:64, 640:704] = np.eye(64)
    out[0:64, 704:832] = 1.0
    _CONST["gdn"] = out
    return out


_CONST = {}


def _rope_tables():
    if "rope" in _CONST:
        return _CONST["rope"]
    t = np.arange(SEQ)
    pos = np.stack([t // 64, t % 64], axis=-1).astype(np.float32)
    inv = np.power(np.float32(10000.0), -np.arange(8, dtype=np.float32) / np.float32(8)).astype(np.float32)
    ang = (pos[:, :, None] * inv).astype(np.float32)
    cos, sin = np.cos(ang).astype(np.float32), np.sin(ang).astype(np.float32)
    tab = np.zeros((128, 2, SEQ), np.float32)
    for i in range(32):
        a_, hf, fr = i // 16, (i // 8) % 2, i % 8
        tab[64 + i, 0, :] = cos[:, a_, fr]
        tab[64 + i, 1, :] = sin[:, a_, fr] * (-1.0 if hf == 0 else 1.0)
    _CONST["rope"] = tab
    return tab


def _na_tables(rel_bias):
    NEG = np.float32(-30000.0)
    kc = np.arange(64)[:, None]
    qc = np.arange(64)[None, :]
    cs = np.clip(qc - 8, 0, 48)
    ok = (kc >= cs) & (kc < cs + 16)
    dc = np.clip(kc - qc + 15, 0, 30)
    tb = rel_bias[:, :, :, dc]
    tb = np.where(ok[None, None, None], tb, NEG).astype(np.float32)
    mask = np.full((DEPTH, 4, 64, 64), NEG, np.float32)
    tiles = []
    for dra in range(14):
        tiles.append(np.concatenate([tb[:, :, dra], tb[:, :, dra + 1]], axis=2))
    tiles.append(np.concatenate([mask, tb[:, :, 3]], axis=2))
    for dra in (4, 6, 8):
        tiles.append(np.concatenate([tb[:, :, dra], tb[:, :, dra + 1]], axis=2))
    tiles.append(np.concatenate([tb[:, :, 10], mask], axis=2))
    nab = np.stack(tiles, axis=2)
    return np.ascontiguousarray(nab.transpose(0, 3, 1, 2, 4), dtype=np.float32)


def build_nc(n_layers=DEPTH, mixers=True, dbg=None):
    nc = bass.Bass("TRN2", target_bir_lowering=False)
    es = ExitStack()
    b = Builder(nc, es, n_layers=n_layers, mixers=mixers, dbg=dbg)
    with es:
        b.build()
    return nc


def kernel(**inputs):
    nc = build_nc()
    in_maps = [host_inputs(inputs, b) for b in range(8)]
    res = run_bass_kernel_spmd(nc, in_maps, core_ids=list(range(8)))
    out = np.stack([np.asarray(r["out"], dtype=np.float32) for r in res.results], axis=0)
    return out
```

```python
import numpy as np
from contextlib import ExitStack
import concourse.bass as bass
import concourse.mybir as mybir
from concourse.bass_utils import run_bass_kernel_spmd

F32 = mybir.dt.float32
BF16 = mybir.dt.bfloat16
AF = mybir.ActivationFunctionType
ALU = mybir.AluOpType
AX = mybir.AxisListType

D = 1024
SEQ = 2048
CTXL = 256
T = SEQ + CTXL
DEPTH = 4
KC = 8
FFN = 2816
NH = FFN // 128
IN_W = 3248
EPS = 1e-6
TB = [(0, 512), (512, 512), (1024, 512), (1536, 512), (2048, 256)]
ENGS = ("pe", "act", "dve", "pool", "sp")
import os as _os
NOPFIX = int(_os.environ.get("NOPFIX", "0"))


class Buf:
    __slots__ = ("name", "w_eng", "w_dma", "r_eng", "r_dma", "dslot", "excl")

    def __init__(self, name):
        self.name = name
        self.excl = False
        self.w_eng = {}
        self.w_dma = []
        self.r_eng = {}
        self.r_dma = []
        self.dslot = None


class Rec:
    __slots__ = ("eng", "fn", "deps", "is_dma", "sembuf", "dval", "dslot", "need_inc", "ival", "semi")

    def __init__(self):
        self.need_inc = False
        self.ival = 0
        self.semi = 0
        self.dval = 0


class V:
    __slots__ = ("ap", "bufs")

    def __init__(self, ap, bufs):
        self.ap = ap
        self.bufs = bufs

    def __getitem__(self, k):
        return V(self.ap[k], self.bufs)

    def bc(self, shape):
        return V(self.ap.to_broadcast(shape), self.bufs)


def _bufs(vs):
    out = []
    for v in vs:
        if v is None or isinstance(v, (int, float)):
            continue
        out.extend(v.bufs)
    return out


def _ap(v):
    return v.ap if isinstance(v, V) else v


class Prog:
    SEM_EPOCH = 20000

    def __init__(self):
        self.recs = {e: [] for e in ENGS}
        self.dma_all = []
        self.slots = []
        self.free = {0: [], 1: []}
        self.live = []

    def op(self, eng, fn, reads=(), writes=(), pwrites=(), dma_buf=None):
        r = Rec()
        r.eng = eng
        r.fn = fn
        r.is_dma = dma_buf is not None
        r.sembuf = dma_buf
        deps = {}

        def add(d, kind):
            if d is r:
                return
            if (not r.is_dma) and (not d.is_dma) and d.eng == eng:
                if eng == "pe" or kind != "raw":
                    return
            deps[id(d)] = d

        for b in reads:
            for w in b.w_eng.values():
                add(w, "raw")
            for w in b.w_dma:
                add(w, "raw")
            if b.excl:
                for x in b.r_eng.values():
                    if x.eng != eng:
                        add(x, "raw")
        for b in list(writes) + list(pwrites):
            for w in b.w_eng.values():
                add(w, "waw")
            for w in b.w_dma:
                add(w, "waw")
            for x in b.r_eng.values():
                add(x, "war")
            for x in b.r_dma:
                add(x, "war")
        r.deps = list(deps.values())
        for b in reads:
            if r.is_dma:
                b.r_dma.append(r)
            else:
                b.r_eng[eng] = r
        for b in writes:
            b.w_eng = {}
            b.w_dma = []
            b.r_eng = {}
            b.r_dma = []
        for b in list(writes) + list(pwrites):
            if r.is_dma:
                b.w_dma.append(r)
            else:
                b.w_eng[eng] = r
            b.r_eng = {}
            b.r_dma = []
        if r.is_dma:
            kind = 1 if eng == "pool" else 0
            if dma_buf.dslot is None:
                dma_buf.dslot = {}
            if kind not in dma_buf.dslot:
                if self.free[kind]:
                    dma_buf.dslot[kind] = self.free[kind].pop()
                else:
                    self.slots.append(0)
                    dma_buf.dslot[kind] = len(self.slots) - 1
                if dma_buf not in self.live:
                    self.live.append(dma_buf)
            sl = dma_buf.dslot[kind]
            self.slots[sl] += 16
            r.dslot = sl
            r.dval = self.slots[sl]
            self.dma_all.append(r)
        self.recs[eng].append(r)
        return r

    def barrier(self):
        last = []
        for e in ENGS:
            for r in reversed(self.recs[e]):
                if not r.is_dma and r.fn is not None:
                    last.append(r)
                    break
        pend = list(self.dma_all)
        self.dma_all = []
        for b in self.live:
            for kind, sl in b.dslot.items():
                if self.slots[sl] < 40000:
                    self.free[kind].append(sl)
            b.dslot = None
        self.live = []
        for e in ENGS:
            r = Rec()
            r.eng = e
            r.fn = None
            r.is_dma = False
            r.sembuf = None
            r.deps = [d for d in last if d.eng != e] + pend
            self.recs[e].append(r)

    def emit(self, nc, es, final_waits):
        for e in ENGS:
            for r in self.recs[e]:
                for d in r.deps:
                    if not d.is_dma:
                        d.need_inc = True
        nsem = {}
        for e in ENGS:
            c = 0
            si = 0
            for r in self.recs[e]:
                if r.need_inc:
                    c += 1
                    if c > self.SEM_EPOCH:
                        si += 1
                        c = 1
                    r.ival = c
                    r.semi = si
            nsem[e] = si + 1
        sems = {e: [es.enter_context(nc.semaphore("s_%s%d" % (e, i))) for i in range(nsem[e])] for e in ENGS}
        dsems = [es.enter_context(nc.semaphore("d_%d" % i)) for i in range(len(self.slots))]
        recs = self.recs

        def run(e, eng):
            seen = {}
            for r in recs[e]:
                need = {}
                for d in r.deps:
                    if d.is_dma:
                        key = ("d", d.dslot)
                        sem = dsems[d.dslot]
                        val = d.dval
                    else:
                        key = (d.eng, d.semi)
                        sem = sems[d.eng][d.semi]
                        val = d.ival
                    if seen.get(key, 0) >= val:
                        continue
                    if key not in need or need[key][1] < val:
                        need[key] = (sem, val)
                for key, (sem, val) in need.items():
                    eng.wait_ge(sem, val)
                    seen[key] = val
                if NOPFIX and e == "pe" and len(need) >= 2:
                    eng.nop()
                if r.fn is None:
                    continue
                ins = r.fn(eng)
                if r.is_dma:
                    ins.then_inc(dsems[r.dslot], 16)
                elif r.need_inc:
                    ins.then_inc(sems[e][r.semi], 1)
            if e == "sp":
                for d in final_waits:
                    eng.wait_ge(dsems[d.dslot], d.dval)

        block = es.enter_context(nc.Block())

        @block.sync
        def _(eng):
            run("sp", eng)

        @block.tensor
        def _(eng):
            run("pe", eng)

        @block.scalar
        def _(eng):
            run("act", eng)

        @block.vector
        def _(eng):
            run("dve", eng)

        @block.gpsimd
        def _(eng):
            run("pool", eng)


class Builder:
    def __init__(self, nc, es, n_layers=DEPTH, mixers=True, dbg=None):
        self.nc = nc
        self.es = es
        self.P = Prog()
        self.L = n_layers
        self.mixers = mixers
        self.dbg = dbg
        self.nbuf = 0

    def buf(self, name="b"):
        self.nbuf += 1
        return Buf("%s%d" % (name, self.nbuf))

    def sb(self, name, shape, dt):
        t = self.es.enter_context(self.nc.sbuf_tensor("sb_" + name, list(shape), dt))
        return t

    def sbv(self, name, shape, dt):
        t = self.sb(name, shape, dt)
        return V(t[:], [self.buf(name)])

    def dram_in(self, name, shape, dt=F32):
        return self.nc.dram_tensor(name, list(shape), dt, kind="ExternalInput").ap()

    def mm(self, out, lhsT, rhs, start=True, stop=True, extra_reads=()):
        o, l, r = out.ap, lhsT.ap, rhs.ap
        self.P.op("pe", lambda e: e.matmul(o, lhsT=l, rhs=r, start=start, stop=stop),
                  reads=_bufs([lhsT, rhs]) + list(extra_reads), writes=() if not start else (), pwrites=out.bufs)

    def tr(self, out, in_, ident):
        o, i, d = out.ap, in_.ap, ident.ap
        self.P.op("pe", lambda e: e.transpose(o, i, d), reads=_bufs([in_, ident]), pwrites=out.bufs)

    def act(self, out, in_, func, bias=None, scale=None, accum_out=None, eng="act"):
        o, i = out.ap, in_.ap
        kw = {}
        if bias is not None:
            kw["bias"] = _ap(bias)
        if scale is not None:
            kw["scale"] = _ap(scale)
        if accum_out is not None:
            kw["accum_out"] = accum_out.ap
        self.P.op("act", lambda e: e.activation(o, i, func, **kw),
                  reads=_bufs([in_, bias, scale]), pwrites=_bufs([out, accum_out]))

    def tt(self, out, in0, in1, op, eng="dve"):
        o, a, b = out.ap, in0.ap, in1.ap
        self.P.op(eng, lambda e: e.tensor_tensor(o, a, b, op), reads=_bufs([in0, in1]), pwrites=out.bufs)

    def ts(self, out, in0, s1, op0, s2=None, op1=None, eng="dve"):
        o, a = out.ap, in0.ap
        s1a, s2a = _ap(s1), _ap(s2)
        if op1 is None:
            fn = lambda e: e.tensor_scalar(o, a, s1a, None, op0)
        else:
            fn = lambda e: e.tensor_scalar(o, a, s1a, s2a, op0, op1)
        self.P.op(eng, fn, reads=_bufs([in0, s1, s2]), pwrites=out.bufs)

    def stt(self, out, in0, scalar, in1, op0, op1, eng="dve"):
        o, a, b = out.ap, in0.ap, in1.ap
        s = _ap(scalar)
        self.P.op(eng, lambda e: e.scalar_tensor_tensor(o, a, s, b, op0, op1),
                  reads=_bufs([in0, scalar, in1]), pwrites=out.bufs)

    def copy(self, out, in_, eng="dve"):
        o, i = out.ap, in_.ap
        if eng == "act":
            self.P.op("act", lambda e: e.copy(o, i), reads=in_.bufs, pwrites=out.bufs)
        else:
            self.P.op(eng, lambda e: e.tensor_copy(o, i), reads=in_.bufs, pwrites=out.bufs)

    def recip(self, out, in_):
        o, i = out.ap, in_.ap
        self.P.op("dve", lambda e: e.reciprocal(o, i), reads=in_.bufs, pwrites=out.bufs)

    def memset(self, out, val, eng="dve"):
        o = out.ap
        self.P.op(eng, lambda e: e.memset(o, val), reads=(), pwrites=out.bufs)

    def dma(self, out, in_, eng="sp", sembuf=None):
        o, i = _ap(out), _ap(in_)
        rb = in_.bufs if isinstance(in_, V) else []
        wb = out.bufs if isinstance(out, V) else []
        if sembuf is None:
            sembuf = (wb or rb)[0]
        return self.P.op(eng, lambda e: e.dma_start(out=o, in_=i), reads=rb, pwrites=wb, dma_buf=sembuf)

    def build(self):
        nc, P, L = self.nc, self.P, self.L
        x_d = self.dram_in("x", [SEQ, D])
        ctx_d = self.dram_in("ctx", [CTXL, D])
        cc_d = self.dram_in("cc", [128, KC, 2])
        wada_d = self.dram_in("w_ada", [DEPTH, D, 6 * D])
        bada_d = self.dram_in("b_adaT", [128, DEPTH, 48])
        gv_d = self.dram_in("gvecs", [128, DEPTH, 2, KC])
        gfin_d = self.dram_in("g_finalT", [128, KC])
        wg_d = self.dram_in("w_gate", [DEPTH, D, FFN])
        wu_d = self.dram_in("w_up", [DEPTH, D, FFN])
        wd_d = self.dram_in("w_down", [DEPTH, FFN, D])
        ident_d = self.dram_in("ident", [128, 128])
        if self.mixers:
            self.win_d = self.dram_in("w_in", [DEPTH, D, IN_W])
            self.wpe_d = self.dram_in("w_pe2", [DEPTH, D, 2, 96])
            self.wq2_d = self.dram_in("w_q2", [DEPTH, 256, 2, 384])
            self.wkv_d = self.dram_in("w_kv_nv", [DEPTH, 128, 512])
            mlag_d = self.dram_in("mla_g", [128, DEPTH, 3])
            self.rope_d = self.dram_in("ropeCS", [128, 2, SEQ])
            sel_d = self.dram_in("sel65", [128, 64])
            self.nab_d = self.dram_in("nab", [DEPTH, 128, 4, 19, 64])
            self.wout_d = self.dram_in("w_out", [DEPTH, D, D])
            gdnc_d = self.dram_in("gdnc", [128, 832])
            gabc_d = self.dram_in("gdn_ab", [128, DEPTH, 16])
            gcw_d = self.dram_in("gdn_cw", [128, DEPTH, 3, 4, 5])
            ggo_d = self.dram_in("gdn_go", [128, DEPTH])
        out_d = nc.dram_tensor("out", [SEQ, D], F32, kind="ExternalOutput").ap()
        self.out_d = out_d
        if self.dbg:
            self.dbg_d = nc.dram_tensor("dbg", list(self.dbg), F32, kind="ExternalOutput").ap()

        POOLW = 40960
        XW = KC * T
        pool_t = self.sb("pool", [128, POOLW], F32)
        self.pool_t, self.POOLW, self.XW = pool_t, POOLW, XW
        xT_t = pool_t[:, 0:XW].rearrange("p (k t) -> p k t", k=KC)
        hT_t = self.sb("hT", [128, KC, T], BF16)
        self.xT_t, self.hT_t = xT_t, hT_t
        xb = [[self.buf("x") for _ in TB] for _ in range(KC)]
        hb = [[self.buf("h") for _ in TB] for _ in range(KC)]

        def xT(kc, tb):
            o, n = TB[tb]
            return V(xT_t[:, kc, o:o + n], [xb[kc][tb]])

        def hT(kc, tb):
            o, n = TB[tb]
            return V(hT_t[:, kc, o:o + n], [hb[kc][tb]])

        self.xT, self.hT = xT, hT
        modT = self.sbv("modT", [128, DEPTH, 6, KC, 2], F32)
        gsc = self.sbv("gsc", [128, DEPTH, 2, KC, 2], F32)
        bT = self.sbv("bT", [128, DEPTH, 48], F32)
        gv = self.sbv("gv", [128, DEPTH, 2, KC], F32)
        gfin = self.sbv("gfin", [128, KC], F32)
        ident = self.sbv("ident", [128, 128], F32)
        ones_bf = self.sbv("ones_bf", [128, 128], BF16)
        cc = self.sbv("cc", [128, KC, 2], F32)
        scc = self.sbv("scc", [128, KC, 2], F32)
        self.modT, self.gsc, self.ident, self.ones_bf = modT, gsc, ident, ones_bf
        self.xb, self.hb = xb, hb
        if self.mixers:
            self.mlag = self.sbv("mlag", [128, DEPTH, 3], F32)
            self.sel65 = self.sbv("sel65", [128, 64], F32)
            self.dma(self.mlag, mlag_d)
            self.gdnc = self.sbv("gdnc", [128, 832], F32)
            self.gabc = self.sbv("gabc", [128, DEPTH, 16], F32)
            self.gcw = self.sbv("gcw", [128, DEPTH, 3, 4, 5], F32)
            self.ggo = self.sbv("ggo", [128, DEPTH], F32)
            self.one_t = self.sbv("one_t", [128, 1], F32)
            self.ident_bf = self.sbv("ident_bf", [128, 128], BF16)
            self.dma(self.gdnc, gdnc_d)
            self.dma(self.gabc, gabc_d)
            self.dma(self.gcw, gcw_d)
            self.dma(self.ggo, ggo_d)
            self.memset(self.one_t, 1.0)
            self.dma(self.sel65, sel_d)
        self.scr_base = XW
        self.scr_lim = POOLW
        self.ps = [self.es.enter_context(nc.psum_tensor("ps%d" % i, [128, 512], F32)) for i in range(6)]
        self.psD_t = self.es.enter_context(nc.psum_tensor("psD", [128, 1024], F32))
        self.psb = [self.buf("ps") for _ in range(8)]
        for b_ in self.psb:
            b_.excl = True

        def PS(i, n=512, p=128):
            if i >= 6:
                return V(self.psD_t[0:p, (i - 6) * 512:(i - 6) * 512 + n], [self.psb[i]])
            return V(self.ps[i][0:p, 0:n], [self.psb[i]])

        self.PS = PS

        self.dma(ident, ident_d)
        self.dma(cc, cc_d)
        self.dma(bT, bada_d)
        self.dma(gv, gv_d)
        self.dma(gfin, gfin_d)
        self.memset(ones_bf, 1.0)
        self.act(scc, cc, AF.Silu)
        if self.mixers:
            self.copy(self.ident_bf, ident)

        scr_off = [0]

        def carve(n_f32, dt=F32, shape=None, name="c"):
            a = self.scr_base + scr_off[0]
            scr_off[0] += n_f32
            assert a + n_f32 <= self.scr_lim, (name, a + n_f32)
            ap = pool_t[:, a:a + n_f32]
            if dt != F32:
                ap = ap.bitcast(dt)
            if shape is not None:
                names = " ".join("d%d" % i for i in range(len(shape)))
                kw = {"d%d" % i: s for i, s in enumerate(shape[:-1])}
                ap = ap.rearrange("p (%s) -> p %s" % (names, names), **kw)
            return V(ap, [self.buf(name)])

        self.carve = carve
        self.scr_off = scr_off
        wa = [carve(KC * 256, BF16, [KC, 512], "wa") for _ in range(3)]
        scc_bf = self.sbv("scc_bf", [128, KC, 2], BF16)
        self.copy(scc_bf, scc)
        n = 0
        for l in range(L):
            pst = PS(l % 2, 96)
            for cb in range(12):
                w = wa[n % 3]
                n += 1
                self.dma(w, wada_d[l, :, cb * 512:(cb + 1) * 512].rearrange("(k p) c -> p k c", p=128),
                         eng="pool")
                for jj in range(4):
                    j = cb * 4 + jj
                    for k in range(KC):
                        self.mm(pst[:, 2 * j:2 * j + 2], w[:, k, jj * 128:(jj + 1) * 128], scc_bf[:, k, :],
                                start=(k == 0), stop=(k == KC - 1))
            o = modT.ap[:, l].rearrange("p s k w -> p (s k) w")
            i0 = pst.ap.rearrange("p (j w) -> p j w", w=2)
            i1 = bT.ap[:, l, :].unsqueeze(2).to_broadcast([128, 48, 2])
            self.tt(V(o, modT.bufs), V(i0, pst.bufs), V(i1, bT.bufs), ALU.add)
            for which, (si, gi) in enumerate(((1, 0), (4, 1))):
                g_b = gv.ap[:, l, gi, :].unsqueeze(2).to_broadcast([128, KC, 2])
                self.stt(V(gsc.ap[:, l, which], gsc.bufs), V(modT.ap[:, l, si], modT.bufs), 1.0,
                         V(g_b, gv.bufs), ALU.add, ALU.mult)
        P.barrier()
        scr_off[0] = 0

        stg = [carve(4 * D, F32, [4, D], "stg") for _ in range(2)]
        n = 0
        for tb, (o, nt) in enumerate(TB):
            s = stg[tb % 2]
            ntile = nt // 128
            if tb < 4:
                src = x_d[o:o + nt, :]
            else:
                src = ctx_d[:, :]
            self.dma(s[:, 0:ntile, :], src.rearrange("(t p) d -> p t d", p=128), eng="sp")
            for kc in range(KC):
                pb = PS(n % 8, nt)
                n += 1
                for t in range(ntile):
                    self.tr(pb[:, t * 128:(t + 1) * 128], s[:, t, kc * 128:(kc + 1) * 128], ident)
                self.copy(xT(kc, tb), pb, eng="act" if kc % 2 else "dve")
        P.barrier()
        scr_off[0] = 0

        for l in range(L):
            self.norm_mod(l, 0)
            if self.mixers:
                self.mixer_phase(l)
            self.norm_mod(l, 1)
            self.ffn_phase(l, wg_d, wu_d, wd_d)
        self.final_phase(gfin)
        P.emit(nc, self.es, self.final_waits)

    def rstd_block(self, tb, sqs, rs_tmp, rstd):
        o, nt = TB[tb]
        ps = self.PS(tb % 2, nt)
        for kc in range(KC):
            sq = sqs[kc % 2]
            self.act(sq[:, 0:nt], self.xT(kc, tb), AF.Square)
            self.mm(ps, self.ones_bf, sq[:, 0:nt], start=(kc == 0), stop=(kc == KC - 1))
        self.act(rs_tmp[:, 0:nt], ps, AF.Ln, bias=self.eps_t, scale=1.0 / D)
        self.act(rstd[:, 0:nt], rs_tmp[:, 0:nt], AF.Exp, scale=-0.5)

    def norm_mod(self, l, which):
        P = self.P
        self.scr_off[0] = 0
        if not hasattr(self, "eps_t"):
            self.eps_t = self.sbv("eps_t", [128, 1], F32)
            self.memset(self.eps_t, EPS)
        sqs = [self.carve(256, BF16, None, "sq") for _ in range(2)]
        rs_tmp = self.carve(512, F32, None, "rs")
        rstds = [self.carve(512, F32, None, "rstd") for _ in range(2)]
        tmps = [self.carve(512, F32, None, "nt") for _ in range(2)]
        shift_i = 0 if which == 0 else 3
        n = 0
        for tb, (o, nt) in enumerate(TB):
            rstd = rstds[tb % 2]
            self.rstd_block(tb, sqs, rs_tmp, rstd)
            w = 0 if tb < 4 else 1
            for kc in range(KC):
                tmp = tmps[n % 2]
                n += 1
                self.stt(tmp[:, 0:nt], self.xT(kc, tb), V(self.gsc.ap[:, l, which, kc, w:w + 1], self.gsc.bufs),
                         rstd[:, 0:nt], ALU.mult, ALU.mult)
                self.act(self.hT(kc, tb), tmp[:, 0:nt], AF.Identity,
                         bias=V(self.modT.ap[:, l, shift_i, kc, w:w + 1], self.modT.bufs), scale=1.0)
        P.barrier()
        self.scr_off[0] = 0

    def ffn_phase(self, l, wg_d, wu_d, wd_d):
        P = self.P
        self.scr_off[0] = 0
        HC = NH // 2
        hid = self.sb_scr_hid()
        wgs = [self.carve(KC * 64, BF16, [KC, 128], "wg") for _ in range(3)]
        wus = [self.carve(KC * 64, BF16, [KC, 128], "wu") for _ in range(3)]
        wds = [self.carve(HC * 64, BF16, [HC, 128], "wd") for _ in range(2)]
        sil = [self.carve(512, F32, None, "sil") for _ in range(2)]
        n = 0
        nd = 0
        npb = 0
        for half in range(2):
            for jj in range(HC):
                j = half * HC + jj
                wg, wu = wgs[n % 3], wus[n % 3]
                n += 1
                self.dma(wg, wg_d[l, :, j * 128:(j + 1) * 128].rearrange("(k p) c -> p k c", p=128), eng="pool")
                self.dma(wu, wu_d[l, :, j * 128:(j + 1) * 128].rearrange("(k p) c -> p k c", p=128), eng="pool")
                for tb, (o, nt) in enumerate(TB):
                    pg = self.PS(npb % 4, nt)
                    pu = self.PS(4 + npb % 4, nt)
                    npb += 1
                    for kc in range(KC):
                        self.mm(pg, wg[:, kc, :], self.hT(kc, tb), start=(kc == 0), stop=(kc == KC - 1))
                    for kc in range(KC):
                        self.mm(pu, wu[:, kc, :], self.hT(kc, tb), start=(kc == 0), stop=(kc == KC - 1))
                    s = sil[npb % 2]
                    self.act(s[:, 0:nt], pg, AF.Silu)
                    self.tt(V(hid.ap[:, jj, o:o + nt], [self.hidb[jj][tb]]), s[:, 0:nt], pu, ALU.mult)
            for m in range(KC):
                wd = wds[nd % 2]
                nd += 1
                self.dma(wd, wd_d[l, half * HC * 128:(half + 1) * HC * 128, m * 128:(m + 1) * 128]
                         .rearrange("(j p) c -> p j c", p=128), eng="pool")
                for tb, (o, nt) in enumerate(TB):
                    po = self.PS(npb % 8, nt)
                    npb += 1
                    for jj in range(HC):
                        self.mm(po, wd[:, jj, :], V(hid.ap[:, jj, o:o + nt], [self.hidb[jj][tb]]),
                                start=(jj == 0), stop=(jj == HC - 1))
                    w = 0 if tb < 4 else 1
                    self.stt(self.xT(m, tb), po, V(self.modT.ap[:, l, 5, m, w:w + 1], self.modT.bufs),
                             self.xT(m, tb), ALU.mult, ALU.add)
        P.barrier()
        self.scr_off[0] = 0

    def sb_scr_hid(self):
        HC = NH // 2
        hid = self.carve(HC * T // 2, BF16, [HC, T], "hid")
        self.hidb = [[self.buf("hid") for _ in TB] for _ in range(HC)]
        return hid

    def final_phase(self, gfin):
        P = self.P
        self.scr_off[0] = 0
        sqs = [self.carve(256, BF16, None, "sq") for _ in range(2)]
        rs_tmp = self.carve(512, F32, None, "rs")
        rstds = [self.carve(512, F32, None, "rstd") for _ in range(2)]
        ys = [self.carve(512, F32, None, "y") for _ in range(3)]
        outs = [self.carve(4 * D, F32, [4, D], "os") for _ in range(2)]
        self.final_waits = []
        n = 0
        for tb in range(4):
            o, nt = TB[tb]
            rstd = rstds[tb % 2]
            self.rstd_block(tb, sqs, rs_tmp, rstd)
            ost = outs[tb % 2]
            for kc in range(KC):
                y = ys[n % 3]
                self.stt(y, self.xT(kc, tb), gfin[:, kc:kc + 1], rstd, ALU.mult, ALU.mult)
                pb = self.PS(2 + n % 6, 512)
                n += 1
                for t in range(4):
                    self.tr(pb[:, t * 128:(t + 1) * 128], y[:, t * 128:(t + 1) * 128], self.ident)
                dst = V(ost.ap[:, :, kc * 128:(kc + 1) * 128], ost.bufs)
                srcv = V(pb.ap.rearrange("p (t c) -> p t c", c=128), pb.bufs)
                self.copy(dst, srcv, eng="act" if kc % 2 else "dve")
            r = self.dma(self.out_d[o:o + nt, :].rearrange("(t p) d -> p t d", p=128), ost, eng="sp")
            self.final_waits.append(r)

    def load_w(self, dst, src_ap, eng="pool"):
        return self.dma(dst, src_ap, eng=eng)

    def mixer_phase(self, l):
        P = self.P
        nc = self.nc
        if not hasattr(self, "xsp_d"):
            self.xsp_d = nc.dram_tensor("xspill", [128, KC * T], F32).ap()
            self.xsp_b = self.buf("xsp")
        xall = V(self.pool_t[:, 0:self.XW], [b for row in self.xb for b in row])
        self.dma(V(self.xsp_d, [self.xsp_b]), xall, eng="sp", sembuf=self.xsp_b)
        P.barrier()
        self.scr_base = 0
        self.scr_lim = self.POOLW
        self.scr_off[0] = 0
        if not hasattr(self, "mix_d"):
            self.mix_d = nc.dram_tensor("mixspill", [128, KC, T], BF16).ap()
        self.mixb = [self.buf("mix") for _ in range(KC)]
        self.mla(l)
        P.barrier()
        self.scr_off[0] = 0
        self.na(l)
        P.barrier()
        self.scr_off[0] = 0
        self.gdn(l)
        P.barrier()
        self.dma(xall, V(self.xsp_d, [self.xsp_b]), eng="sp", sembuf=self.xsp_b)
        self.scr_base = self.XW
        self.scr_off[0] = 0
        mixs = self.carve(KC * T // 2, BF16, [KC, T], "mixs")
        mix_t = mixs.ap
        self.mix_t = mix_t
        for k in range(KC):
            self.dma(V(mix_t[:, k, :], [mixs.bufs[0]]), V(self.mix_d[:, k, :], [self.mixb[k]]), eng="sp" if k % 2 else "act",
                     sembuf=mixs.bufs[0])
        self.mixb = [mixs.bufs[0]] * KC
        if self.dbg and l == 0:
            self.dump_mix()
        wos = [self.carve(KC * 64, BF16, [KC, 128], "wo") for _ in range(2)]
        npb = 0
        for m in range(KC):
            wo = wos[m % 2]
            self.load_w(wo, self.wout_d[l, :, m * 128:(m + 1) * 128].rearrange("(k p) c -> p k c", p=128))
            for tb, (o, nt) in enumerate(TB):
                po = self.PS(npb % 8, nt)
                npb += 1
                for kk in range(KC):
                    self.mm(po, wo[:, kk, :], V(mix_t[:, kk, o:o + nt], [self.mixb[kk]]),
                            start=(kk == 0), stop=(kk == KC - 1))
                w = 0 if tb < 4 else 1
                self.stt(self.xT(m, tb), po, V(self.modT.ap[:, l, 2, m, w:w + 1], self.modT.bufs),
                         self.xT(m, tb), ALU.mult, ALU.add)
        P.barrier()
        self.scr_off[0] = 0
        self.scr_lim = self.POOLW

    def dump_mix(self):
        st = [self.carve(T, F32, None, "dst") for _ in range(2)]
        for k in range(KC):
            s_ = st[k % 2]
            self.copy(s_, V(self.mix_t[:, k, :], [self.mixb[k]]), eng="act")
            self.dma(self.dbg_d[k * 128:(k + 1) * 128, :], s_, eng="sp")

    def finish_attn(self, O_ps, n, pb, chunk, tok0, tmp):
        osb, rden, obuf = tmp
        self.copy(osb[0:65, 0:n], O_ps[0:65, 0:n], eng="act")
        den = self.PS(5, n, 64)
        self.mm(den, self.sel65[0:65, :], osb[0:65, 0:n])
        self.act(rden[0:64, 0:n], den, AF.Ln)
        self.act(rden[0:64, 0:n], rden[0:64, 0:n], AF.Exp, scale=-1.0)
        self.tt(obuf[0:64, 0:n], osb[0:64, 0:n], rden[0:64, 0:n], ALU.mult)
        dst = V(self.mix_d[pb:pb + 64, chunk, tok0:tok0 + n], [self.mixb[chunk]])
        self.dma(dst, obuf[0:64, 0:n], eng="sp", sembuf=obuf.bufs[0])

    def attn_dense(self, QT, KT_fn, V_fn, key_tiles, n, scale, dst, st):
        pts, fin_tmp = st["pts"], st["fin"]
        O_ps = self.PS(3 + st["nO"] % 2, n, 65)
        st["nO"] += 1
        nk = len(key_tiles)
        LA = 2
        pend = []
        for i in range(nk + LA):
            if i < nk:
                t = key_tiles[i]
                S_ps = self.PS(st["nS"] % 3, n)
                pt = pts[st["nS"] % 3]
                st["nS"] += 1
                self.mm(S_ps, KT_fn(t), QT)
                pend.append((i, t, S_ps, pt))
            if i >= LA:
                j, t, S_ps, pt = pend.pop(0)
                self.act(pt[:, 0:n], S_ps, AF.Exp, scale=scale)
                self.mm(O_ps, V_fn(t), pt[:, 0:n], start=(j == 0), stop=(j == nk - 1))
        self.finish_attn(O_ps, n, dst[0], dst[1], dst[2], fin_tmp[st["nO"] % 2])

    def attn_state(self):
        pts = [self.carve(256, BF16, None, "pt") for _ in range(3)]
        fin = [(self.carve(512, F32, None, "osb"), self.carve(512, F32, None, "rden"),
                self.carve(256, BF16, None, "obuf")) for _ in range(2)]
        return {"pts": pts, "fin": fin, "nO": 0, "nS": 0}

    def mla(self, l):
        hT = self.hT
        SC = 96 ** -0.5
        w_in3 = [self.carve(KC * 64, BF16, [KC, 128], "wmi") for _ in range(3)]
        for c in range(3):
            self.load_w(w_in3[c], self.win_d[l, :, c * 128:(c + 1) * 128].rearrange("(k p) c -> p k c", p=128))
        wpe = self.carve(KC * 96, BF16, [KC, 2, 96], "wpe")
        self.load_w(wpe, self.wpe_d[l].rearrange("(k p) a c -> p k a c", p=128))
        wq2 = self.carve(2 * 384, BF16, [2, 2, 384], "wq2")
        self.load_w(wq2, self.wq2_d[l].rearrange("(k p) a c -> p k a c", p=128))
        wkv = self.carve(256, BF16, None, "wkv")
        self.load_w(wkv, self.wkv_d[l])
        rope = self.carve(2 * SEQ, F32, [2, SEQ], "rope")
        self.dma(rope, self.rope_d, eng="sp")
        mqn = self.carve(T, BF16, [2, T], "mqn")
        mkvn = self.carve(T // 2, BF16, None, "mkvn")
        peR = self.carve(T // 2, BF16, None, "peR")
        vaug = self.carve(18 * 4 * 65 // 2, BF16, [18, 4, 65], "vaug")
        self.memset(V(vaug.ap[:, :, :, 64:65], vaug.bufs), 1.0, eng="pool")
        raw = [self.carve(512, F32, None, "raw") for _ in range(3)]
        sqs = [self.carve(256, BF16, None, "sq") for _ in range(3)]
        rt = [self.carve(512, F32, None, "rt") for _ in range(4)]
        r1 = self.carve(512, F32, None, "r1")
        r2 = self.carve(512, F32, None, "r2")
        for tb, (o, nt) in enumerate(TB):
            pr = [self.PS(6, nt), self.PS(7, nt), self.PS(5, nt)]
            for c in range(3):
                for kc in range(KC):
                    self.mm(pr[c], w_in3[c][:, kc, :], hT(kc, tb), start=(kc == 0), stop=(kc == KC - 1))
                self.copy(raw[c][:, 0:nt], pr[c], eng="act" if c % 2 else "dve")
                self.act(sqs[c][:, 0:nt], raw[c][:, 0:nt], AF.Square)
            ps_q = self.PS(3, nt)
            self.mm(ps_q, self.ones_bf, sqs[0][:, 0:nt], start=True, stop=False)
            self.mm(ps_q, self.ones_bf, sqs[1][:, 0:nt], start=False, stop=True)
            ps_k = self.PS(4, nt)
            self.mm(ps_k, self.ones_bf, sqs[2][:, 0:nt])
            self.act(rt[0][:, 0:nt], ps_q, AF.Ln, bias=self.eps_t, scale=1.0 / 256)
            self.act(rt[1][:, 0:nt], rt[0][:, 0:nt], AF.Exp, scale=-0.5)
            self.act(rt[2][:, 0:nt], ps_k, AF.Ln, bias=self.eps_t, scale=1.0 / 128)
            self.act(rt[3][:, 0:nt], rt[2][:, 0:nt], AF.Exp, scale=-0.5)
            for c in range(2):
                self.stt(V(mqn.ap[:, c, o:o + nt], mqn.bufs), raw[c][:, 0:nt],
                         V(self.mlag.ap[:, l, c:c + 1], self.mlag.bufs), rt[1][:, 0:nt], ALU.mult, ALU.mult)
            self.stt(mkvn[:, o:o + nt], raw[2][:, 0:nt], V(self.mlag.ap[:, l, 2:3], self.mlag.bufs),
                     rt[3][:, 0:nt], ALU.mult, ALU.mult)
            pp = [self.PS(0, nt, 96), self.PS(1, nt, 96)]
            for a in range(2):
                for kc in range(KC):
                    self.mm(pp[a], wpe[:, kc, a, :], hT(kc, tb), start=(kc == 0), stop=(kc == KC - 1))
            if tb < 4:
                self.tt(r1[64:96, 0:nt], pp[0][64:96, :], rope[64:96, 0, o:o + nt], ALU.mult)
                self.tt(r2[64:96, 0:nt], pp[1][64:96, :], rope[64:96, 1, o:o + nt], ALU.mult)
                self.tt(peR[64:96, o:o + nt], r1[64:96, 0:nt], r2[64:96, 0:nt], ALU.add)
            else:
                self.copy(peR[64:96, o:o + nt], pp[0][64:96, :], eng="act")
        for t in range(18):
            pv = self.PS(6 + t % 2, 256)
            self.mm(pv, mkvn[:, t * 128:(t + 1) * 128], wkv[:, 256:512])
            self.copy(V(vaug.ap[:, t, :, 0:64], vaug.bufs), V(pv.ap.rearrange("p (h d) -> p h d", h=4), pv.bufs),
                      eng="act" if t % 2 else "dve")
        QTs = [self.carve(T // 2, BF16, None, "QT") for _ in range(2)]
        KTs = [self.carve(T // 2, BF16, None, "KT") for _ in range(2)]
        st = self.attn_state()
        for h in range(4):
            QT, KT = QTs[h % 2], KTs[h % 2]
            for tb, (o, nt) in enumerate(TB):
                pq = [self.PS(6, nt, 96), self.PS(7, nt, 96)]
                na_ = 2 if tb < 4 else 1
                for a in range(na_):
                    for c in range(2):
                        self.mm(pq[a], wq2[:, c, a, h * 96:(h + 1) * 96], V(mqn.ap[:, c, o:o + nt], mqn.bufs),
                                start=(c == 0), stop=(c == 1))
                if tb < 4:
                    self.copy(QT[0:64, o:o + nt], pq[0][0:64, :], eng="act")
                    self.tt(r1[64:96, 0:nt], pq[0][64:96, :], rope[64:96, 0, o:o + nt], ALU.mult)
                    self.tt(r2[64:96, 0:nt], pq[1][64:96, :], rope[64:96, 1, o:o + nt], ALU.mult)
                    self.tt(QT[64:96, o:o + nt], r1[64:96, 0:nt], r2[64:96, 0:nt], ALU.add)
                else:
                    self.copy(QT[0:96, o:o + nt], pq[0][0:96, :], eng="act")
                pk = self.PS(5, nt, 64)
                self.mm(pk, wkv[:, h * 64:(h + 1) * 64], mkvn[:, o:o + nt])
                self.copy(KT[0:64, o:o + nt], pk, eng="dve")
            self.copy(KT[64:96, :], peR[64:96, :], eng="act")
            KT_fn = lambda t, KT=KT: KT[0:96, t * 128:(t + 1) * 128]
            V_fn = lambda t, h=h: V(vaug.ap[:, t, h, :], vaug.bufs)
            for qb in range(4):
                self.attn_dense(QT[0:96, qb * 512:(qb + 1) * 512], KT_fn, V_fn, list(range(18)), 512, SC,
                                ((h % 2) * 64, h // 2, qb * 512), st)
            self.attn_dense(QT[0:96, SEQ:T], KT_fn, V_fn, [16, 17], 256, SC, ((h % 2) * 64, h // 2, SEQ), st)

    def na(self, l):
        hT = self.hT
        SC = 64 ** -0.5
        nqk = self.carve(2 * T, BF16, [2, 2, T], "nqk")
        vaug = self.carve(18 * 4 * 65 // 2, BF16, [18, 4, 65], "vaugn")
        self.memset(V(vaug.ap[:, :, :, 64:65], vaug.bufs), 1.0, eng="pool")
        ws = [self.carve(KC * 64, BF16, [KC, 128], "wn") for _ in range(2)]
        n = 0
        for qk in range(2):
            for c in range(2):
                w = ws[n % 2]
                n += 1
                col = 416 + qk * 256 + c * 128
                self.load_w(w, self.win_d[l, :, col:col + 128].rearrange("(k p) c -> p k c", p=128))
                for tb, (o, nt) in enumerate(TB):
                    pp = self.PS(6 + tb % 2, nt)
                    for kc in range(KC):
                        self.mm(pp, w[:, kc, :], hT(kc, tb), start=(kc == 0), stop=(kc == KC - 1))
                    self.copy(V(nqk.ap[:, qk, c, o:o + nt], nqk.bufs), pp, eng="act" if tb % 2 else "dve")
        wv = self.carve(KC * 128, BF16, [KC, 256], "wnv")
        self.load_w(wv, self.win_d[l, :, 928:1184].rearrange("(k p) c -> p k c", p=128))
        for t in range(18):
            pv = self.PS(6 + t % 2, 256)
            for kc in range(KC):
                self.mm(pv, V(self.hT_t[:, kc, t * 128:(t + 1) * 128], [self.hb[kc][t // 4]]), wv[:, kc, :],
                        start=(kc == 0), stop=(kc == KC - 1))
            self.copy(V(vaug.ap[:, t, :, 0:64], vaug.bufs), V(pv.ap.rearrange("p (h d) -> p h d", h=4), pv.bufs),
                      eng="act" if t % 2 else "dve")
        nabs = self.carve(4 * 19 * 64, F32, [4, 19, 64], "nabs")
        self.dma(nabs, self.nab_d[l], eng="sp")
        E = self.carve(4 * 19 * 32, BF16, [4, 19, 64], "E")
        self.act(E, nabs, AF.Exp)
        st = self.attn_state()
        tmps = [self.carve(7 * 32, BF16, [7, 64], "ntmp") for _ in range(3)]
        ptl = [self.carve(5 * 32, BF16, [5, 64], "ptl") for _ in range(3)]
        nn = 0
        for h in range(4):
            c, pb = h // 2, (h % 2) * 64
            V_fn = lambda t, h=h: V(vaug.ap[:, t, h, :], vaug.bufs)
            LA = 2
            pend = []
            for rr_ in range(32 + LA):
                if rr_ < 32:
                    r = rr_
                    sr = min(max(r - 4, 0), 24)
                    if sr % 2 == 0:
                        nloc, t0 = 4, sr // 2
                        off = sr - r + 7
                        Esel = V(E.ap[:, h, off:off + 7:2, :], E.bufs)
                    else:
                        nloc, t0 = 5, (sr - 1) // 2
                        Esel = V(E.ap[:, h, 14:19, :], E.bufs)
                    tiles = [t0 + i for i in range(nloc)] + [16, 17]
                    ntl = len(tiles)
                    S_ps = self.PS(nn % 3, ntl * 64)
                    tmp = tmps[nn % 3]
                    pl = ptl[nn % 3]
                    nn += 1
                    q = V(nqk.ap[pb:pb + 64, 0, c, r * 64:(r + 1) * 64], nqk.bufs)
                    for i, t in enumerate(tiles):
                        k = V(nqk.ap[pb:pb + 64, 1, c, t * 128:(t + 1) * 128], nqk.bufs)
                        self.mm(S_ps[:, i * 64:(i + 1) * 64], k, q)
                    pend.append((r, tiles, nloc, Esel, S_ps, tmp, pl))
                if rr_ >= LA:
                    r, tiles, nloc, Esel, S_ps, tmp, pl = pend.pop(0)
                    ntl = len(tiles)
                    if r % 8 == 0:
                        O_ps = self.PS(3 + st["nO"] % 2, 512, 65)
                        st["nO"] += 1
                    self.act(V(tmp.ap[:, 0:ntl, :], tmp.bufs), V(S_ps.ap.rearrange("p (t q) -> p t q", q=64), S_ps.bufs),
                             AF.Exp, scale=SC)
                    self.tt(V(pl.ap[:, 0:nloc, :], pl.bufs), V(tmp.ap[:, 0:nloc, :], tmp.bufs), Esel, ALU.mult)
                    Or = O_ps[0:65, (r % 8) * 64:(r % 8 + 1) * 64]
                    for i, t in enumerate(tiles):
                        if i < nloc:
                            p_ = V(pl.ap[:, i, :], pl.bufs)
                        else:
                            p_ = V(tmp.ap[:, i, :], tmp.bufs)
                        self.mm(Or, V_fn(t), p_, start=(i == 0), stop=(i == ntl - 1))
                    if r % 8 == 7:
                        self.finish_attn(O_ps, 512, pb, 2 + c, (r - 7) * 64, st["fin"][st["nO"] % 2])
            KT_fn = lambda t, c=c, pb=pb: V(nqk.ap[pb:pb + 64, 1, c, t * 128:(t + 1) * 128], nqk.bufs)
            self.attn_dense(V(nqk.ap[pb:pb + 64, 0, c, SEQ:T], nqk.bufs), KT_fn, V_fn, [16, 17], 256, SC,
                            (pb, 2 + c, SEQ), st)

    def gdn(self, l):
        hT = self.hT
        NCH = 36
        DK = 128 ** -0.5
        gc_ = self.gdnc
        U = [gc_[0:64, d * 320:d * 320 + 64] for d in range(2)]
        SU = [gc_[0:64, d * 320 + 64:d * 320 + 128] for d in range(2)]
        MN = [gc_[0:64, d * 320 + 128:d * 320 + 320] for d in range(2)]
        I64 = gc_[0:64, 640:704]
        ONE = gc_[0:64, 704:832]
        ORD = [[32, 33, 34, 35] + list(range(32)), [35, 34, 33, 32] + list(range(31, -1, -1))]
        MPOS = []
        for d_ in range(2):
            mp = self.carve(64, F32, None, "mpos")
            self.ts(mp[0:64], MN[d_][:, 0:64], -1.0, ALU.mult)
            MPOS.append(mp[0:64])
        psD = lambda n0, n1, p=64: V(self.psD_t[0:p, n0:n1], [self.psb[6], self.psb[7]])
        wab = self.carve(KC * 8, BF16, [KC, 16], "wab")
        self.load_w(wab, self.win_d[l, :, 3232:3248].rearrange("(k p) c -> p k c", p=128))
        ab = self.carve(NCH * 16, F32, [NCH, 16], "ab")
        for n in range(NCH):
            for kc in range(KC):
                self.mm(psD(n * 16, n * 16 + 16),
                        V(self.hT_t[:, kc, n * 64:(n + 1) * 64], [self.hb[kc][n // 8]]), wab[:, kc, :],
                        start=(kc == 0), stop=(kc == KC - 1))
        self.copy(ab[0:64], V(psD(0, NCH * 16).ap.rearrange("p (n c) -> p n c", c=16), [self.psb[6], self.psb[7]]))
        abc = self.gabc
        nA = self.carve(8, F32, None, "nA")
        self.act(nA[0:64], V(abc.ap[0:64, l, 0:8], abc.bufs), AF.Exp)
        self.ts(nA[0:64], nA[0:64], -1.0, ALU.mult)
        g = self.carve(NCH * 8, F32, [NCH, 8], "g")
        beta = self.carve(NCH * 8, F32, [NCH, 8], "beta")
        sp = self.carve(NCH * 8, F32, [NCH, 8], "sp")
        dtb = V(abc.ap[0:64, l, 8:16].unsqueeze(1).to_broadcast([64, NCH, 8]), abc.bufs)
        self.tt(sp[0:64], V(ab.ap[0:64, :, 0:8], ab.bufs), dtb, ALU.add)
        self.act(sp[0:64], sp[0:64], AF.Exp)
        self.act(sp[0:64], sp[0:64], AF.Ln, bias=self.one_t[0:64], scale=1.0)
        self.tt(g[0:64], sp[0:64], V(nA.ap[0:64].unsqueeze(1).to_broadcast([64, NCH, 8]), nA.bufs), ALU.mult)
        self.act(beta[0:64], V(ab.ap[0:64, :, 8:16], ab.bufs), AF.Sigmoid)
        gcs = self.carve(NCH * 8, F32, [NCH, 8], "gcs")
        for d in range(2):
            pc = self.PS(d, NCH * 4, 64)
            self.mm(pc, U[d], V(g.ap[0:64, :, d * 4:(d + 1) * 4], g.bufs))
            self.copy(V(gcs.ap[0:64, :, d * 4:(d + 1) * 4], gcs.bufs),
                      V(pc.ap.rearrange("p (n c) -> p n c", c=4), pc.bufs))
        ptot = self.PS(2, NCH * 8, 128)
        self.mm(ptot, ONE, V(g.ap[0:64].rearrange("p n c -> p (n c)"), g.bufs))
        glast = self.carve(NCH * 8, F32, [NCH, 8], "glast")
        self.act(V(glast.ap.rearrange("p n c -> p (n c)"), glast.bufs), ptot, AF.Exp)
        etail = self.carve(NCH * 8, F32, [NCH, 8], "etail")
        self.tt(V(etail.ap[0:64].rearrange("p n c -> p (n c)"), etail.bufs), ptot[0:64, :],
                V(gcs.ap[0:64].rearrange("p n c -> p (n c)"), gcs.bufs), ALU.subtract)
        self.act(etail[0:64], etail[0:64], AF.Exp)
        eg = self.carve(NCH * 8, F32, [NCH, 8], "eg")
        self.act(eg[0:64], gcs[0:64], AF.Exp)
        s_kbg = self.carve(NCH * 8, F32, [NCH, 8], "skbg")
        self.tt(s_kbg[0:64], beta[0:64], eg[0:64], ALU.mult)
        s_q = self.carve(NCH * 8, F32, [NCH, 8], "sq_")
        self.ts(s_q[0:64], eg[0:64], DK, ALU.mult)
        import os
        STOP = float(os.environ.get("GDN_STOP", "99"))
        if STOP <= 0:
            return
        kT = self.carve(T // 2, BF16, None, "kT")
        qT = self.carve(T // 2, BF16, None, "qT")
        szT = self.carve(T // 2, BF16, None, "szT")
        k_tok = self.carve(NCH * 64, BF16, [NCH, 128], "ktok")
        v_tok = self.carve(NCH * 64, BF16, [NCH, 128], "vtok")
        kbT_sh = self.carve(T // 2, BF16, None, "kbT")
        oT = self.carve(T, F32, None, "oT")
        oTb = [self.buf("oT") for _ in range(NCH)]
        off_r = self.scr_base + self.scr_off[0]
        rawp = self.carve(T + 8, F32, None, "rawp")
        acc = self.carve(T, F32, None, "acc")
        XYall = self.pool_t[:, off_r:off_r + 9 * 512].rearrange("p (g a i c) -> p g a i c", g=9, a=2, i=4)
        XYb = [self.buf("XYb") for _ in range(9)]
        Qgb = [self.buf("Qgb") for _ in range(9)]
        perd = []
        for d in range(2):
            perd.append(dict(
                kbT=kbT_sh, qdT=self.carve(T // 2, BF16, None, "qdT"),
                QT=self.carve(NCH * 32, BF16, [NCH, 64], "QTa"), AiT=self.carve(NCH * 32, BF16, [NCH, 64], "AiT"),
                nwT=self.carve(NCH * 32, BF16, [NCH, 64], "nwT"),
                S=self.carve(128, F32, None, "S"), Sb=self.carve(64, BF16, None, "Sb")))
        wq_ = [self.carve(KC * 64, BF16, [KC, 128], "wg_") for _ in range(2)]
        sqb = [self.carve(256, BF16, None, "sqb") for _ in range(2)]
        rr = [self.carve(512, F32, None, "rr") for _ in range(2)]
        diag = [self.carve(512, F32, [8, 64], "diag") for _ in range(2)]
        GU = [self.carve(256, F32, [4, 64], "GU") for _ in range(2)]
        Dall = [self.carve(768, F32, [4, 192], "Dall") for _ in range(2)]
        Qall = self.carve(9 * 256, F32, [9, 4, 64], "Qall").ap
        kbg = [self.carve(256, BF16, [4, 128], "kbg") for _ in range(2)]
        vbt = [self.carve(64, BF16, None, "vb") for _ in range(4)]
        ktt = [self.carve(64, BF16, None, "kt") for _ in range(4)]
        vnw = [self.carve(64, BF16, None, "vn") for _ in range(4)]
        gts = [self.carve(512, F32, None, "gt") for _ in range(2)]
        gos = [self.carve(256, BF16, None, "go") for _ in range(2)]
        self.memset(rawp[:, 0:2], 0.0)
        self.memset(rawp[:, 2 + SEQ:2 + SEQ + 4], 0.0)
        self.memset(rawp[:, T + 6:T + 8], 0.0)
        nw = 0
        for h in range(4):
            for which in (0, 3, 1, 2):
                w = wq_[nw % 2]
                nw += 1
                col = 1184 + which * 512 + h * 128
                self.load_w(w, self.win_d[l, :, col:col + 128].rearrange("(k p) c -> p k c", p=128))
                for tb, (o, nt) in enumerate(TB):
                    pp = self.PS(tb % 2, nt)
                    for kc in range(KC):
                        self.mm(pp, w[:, kc, :], hT(kc, tb), start=(kc == 0), stop=(kc == KC - 1))
                    if which == 3:
                        self.act(szT[:, o:o + nt], pp, AF.Silu)
                    else:
                        oo = 2 + o if tb < 4 else 6 + o
                        self.copy(rawp[:, oo:oo + nt], pp, eng="act")
                if which == 3:
                    continue
                cw = lambda j: V(self.gcw.ap[:, l, which, h, j:j + 1], self.gcw.bufs)
                for (o0, n0, p0) in ((0, SEQ, 0), (SEQ, CTXL, SEQ + 4)):
                    a_ = acc[:, o0:o0 + n0]
                    self.act(a_, rawp[:, p0:p0 + n0], AF.Identity, scale=cw(0))
                    for j in range(1, 5):
                        self.stt(a_, rawp[:, p0 + j:p0 + j + n0], cw(j), a_, ALU.mult, ALU.add)
                self.act(acc, acc, AF.Silu)
                if which == 2:
                    for g4 in range(0, NCH, 4):
                        pt_ = self.PS(4 + (g4 // 4) % 2, 512, 64)
                        for i in range(4):
                            n = g4 + i
                            self.tr(pt_[:, i * 128:(i + 1) * 128], acc[:, n * 64:(n + 1) * 64], self.ident)
                        self.copy(V(v_tok.ap[0:64, g4:g4 + 4, :], v_tok.bufs),
                                  V(pt_.ap.rearrange("p (n c) -> p n c", c=128), pt_.bufs), eng="act")
                    continue
                dst = qT if which == 0 else kT

                def n1(tb):
                    o, nt = TB[tb]
                    sq = sqb[tb % 2]
                    self.act(sq[:, 0:nt], acc[:, o:o + nt], AF.Square)
                    pn = self.PS(2 + tb % 2, nt)
                    self.mm(pn, self.ones_bf, sq[:, 0:nt])

                def n2(tb):
                    o, nt = TB[tb]
                    pn = self.PS(2 + tb % 2, nt)
                    r_ = rr[tb % 2]
                    self.act(r_[:, 0:nt], pn, AF.Ln, bias=self.eps_t, scale=1.0)
                    self.act(r_[:, 0:nt], r_[:, 0:nt], AF.Exp, scale=-0.5)

                def n3(tb, dst=dst):
                    o, nt = TB[tb]
                    r_ = rr[tb % 2]
                    self.tt(dst[:, o:o + nt], acc[:, o:o + nt], r_[:, 0:nt], ALU.mult)

                nb_ = len(TB)
                for i in range(nb_ + 2):
                    if i < nb_:
                        n1(i)
                    if 1 <= i <= nb_:
                        n2(i - 1)
                    if i >= 2:
                        n3(i - 2)
            for (src, dstt) in ((kT, k_tok),):
                for g8 in range(0, NCH, 8):
                    cnt = min(8, NCH - g8)
                    pt_ = V(self.ps[4 + (g8 // 8) % 2][0:64, 0:512].bitcast(BF16), [self.psb[4 + (g8 // 8) % 2]])
                    for i in range(cnt):
                        n = g8 + i
                        self.tr(pt_[:, i * 128:(i + 1) * 128], src[:, n * 64:(n + 1) * 64], self.ident_bf)
                    self.copy(V(dstt.ap[0:64, g8:g8 + cnt, :], dstt.bufs),
                              V(pt_.ap[:, 0:cnt * 128].rearrange("p (n c) -> p n c", c=128), pt_.bufs), eng="act")
            self.P.barrier()
            if STOP <= 1:
                continue
            for d in range(2):
                c = d * 4 + h
                pd = perd[d]
                jobs = []
                for (srcT, scal, dstT) in ((kT, beta, pd["kbT"]), (qT, s_q, pd["qdT"])):
                    for n0 in range(0, NCH, 8):
                        jobs.append((srcT, scal, dstT, n0, min(8, NCH - n0)))
                pendb = []
                for bi in range(len(jobs) + 1):
                    if bi < len(jobs):
                        srcT, scal, dstT, n0, cnt = jobs[bi]
                        dg = diag[bi % 2]
                        self.tt(V(dg.ap[0:64, 0:cnt, :], dg.bufs),
                                V(I64.ap.unsqueeze(1).to_broadcast([64, cnt, 64]), I64.bufs),
                                V(scal.ap[0:64, n0:n0 + cnt, c:c + 1].to_broadcast([64, cnt, 64]), scal.bufs), ALU.mult)
                        pb_ = self.PS(bi % 2, cnt * 64)
                        self.mm(pb_, ONE, V(dg.ap[0:64, 0:cnt, :].rearrange("p n c -> p (n c)"), dg.bufs))
                        pendb.append((srcT, dstT, n0, cnt, pb_))
                    if bi >= 1:
                        srcT, dstT, n0, cnt, pb_ = pendb.pop(0)
                        self.tt(dstT[:, n0 * 64:(n0 + cnt) * 64], srcT[:, n0 * 64:(n0 + cnt) * 64], pb_, ALU.mult)
                if STOP <= 2.1:
                    continue
                NG = NCH // 4
                def g_job(bi):
                    n0 = bi * 8
                    cnt = min(8, NCH - n0)
                    dg = diag[bi % 2]
                    self.tt(V(dg.ap[0:64, 0:cnt, :], dg.bufs),
                            V(I64.ap.unsqueeze(1).to_broadcast([64, cnt, 64]), I64.bufs),
                            V(gcs.ap[0:64, n0:n0 + cnt, c:c + 1].to_broadcast([64, cnt, 64]), gcs.bufs), ALU.mult)
                    gp = self.PS(6 + bi % 2, cnt * 64, 64)
                    self.mm(gp, ONE[:, 0:64], V(dg.ap[0:64, 0:cnt, :].rearrange("p n c -> p (n c)"), dg.bufs))
                    return gp

                gps = {}

                def setup_front(gi):
                    n0 = gi * 4
                    t1, da = GU[gi % 2], Dall[gi % 2]
                    if gi % 2 == 0:
                        gps[gi // 2] = g_job(gi // 2)
                    gp = gps[gi // 2]
                    pk = self.PS(5 if gi % 2 == 0 else 3, 512, 64)
                    pa = self.PS(4 if gi % 2 == 0 else 2, 256, 64)
                    for i in range(4):
                        n = n0 + i
                        self.mm(pk[:, i * 128:i * 128 + 64], pd["kbT"][:, n * 64:(n + 1) * 64], kT[:, n * 64:(n + 1) * 64])
                        self.mm(pk[:, i * 128 + 64:i * 128 + 128], kT[:, n * 64:(n + 1) * 64], pd["kbT"][:, n * 64:(n + 1) * 64])
                        self.mm(pa[:, i * 64:(i + 1) * 64], kT[:, n * 64:(n + 1) * 64], qT[:, n * 64:(n + 1) * 64])
                    gsl = V(gp.ap[:, (gi % 2) * 256:(gi % 2) * 256 + 256].rearrange("p (i c) -> p i c", c=64), gp.bufs)
                    self.tt(t1[0:64], gsl, V(gcs.ap[0:64, n0:n0 + 4, c:c + 1].to_broadcast([64, 4, 64]), gcs.bufs),
                            ALU.subtract)
                    da_a = V(da.ap[0:64, :, 0:64], da.bufs)
                    da_b = V(da.ap[0:64, :, 64:128], da.bufs)
                    self.tt(da_a, t1[0:64], V(MPOS[d].ap.unsqueeze(1).to_broadcast([64, 4, 64]), MPOS[d].bufs), ALU.max)
                    self.tt(da_b, t1[0:64], V(MN[d].ap[:, 64:128].unsqueeze(1).to_broadcast([64, 4, 64]), MN[d].bufs),
                            ALU.min)
                    self.act(da_a, da_a, AF.Exp, scale=-1.0)
                    self.act(da_b, da_b, AF.Exp)
                    self.tt(V(da.ap[0:64, :, 128:192], da.bufs), da_b,
                            V(I64.ap.unsqueeze(1).to_broadcast([64, 4, 64]), I64.bufs), ALU.add)
                    return pk, pa

                def setup_back(gi, pk, pa):
                    n0 = gi * 4
                    da = Dall[gi % 2]
                    pk3 = V(pk.ap.rearrange("p (i c) -> p i c", c=128), pk.bufs)
                    X = V(XYall[0:64, gi, 0], [XYb[gi]])
                    Y = V(XYall[0:64, gi, 1], [XYb[gi]])
                    Q = V(Qall[0:64, gi], [Qgb[gi]])
                    self.stt(X, pk3[:, :, 0:64], -1.0, V(da.ap[0:64, :, 0:64], da.bufs), ALU.mult, ALU.mult)
                    self.stt(Y, pk3[:, :, 64:128], -1.0, V(da.ap[0:64, :, 64:128], da.bufs), ALU.mult, ALU.mult)
                    self.stt(V(pd["AiT"].ap[0:64, n0:n0 + 4, :], pd["AiT"].bufs),
                             V(pa.ap.rearrange("p (i c) -> p i c", c=64), pa.bufs), DK,
                             V(da.ap[0:64, :, 128:192], da.bufs), ALU.mult, ALU.mult)
                    self.tt(Q, Y, V(I64.ap.unsqueeze(1).to_broadcast([64, 4, 64]), I64.bufs), ALU.add, eng="dve")

                prev = None
                for gi in range(NG + 1):
                    cur = None
                    if gi < NG:
                        cur = (gi,) + setup_front(gi)
                    if prev is not None:
                        setup_back(*prev)
                    prev = cur
                if STOP <= 2.2:
                    continue
                nps = 0
                for m in range(1, 6):
                    for gi in range(NG):
                        pxy = self.PS(nps % 3, 512, 64)
                        nps += 1
                        Xi = lambda i: V(XYall[0:64, gi, 0, i, :], [XYb[gi]])
                        Yi = lambda i: V(XYall[0:64, gi, 1, i, :], [XYb[gi]])
                        for i in range(4):
                            self.mm(pxy[:, i * 64:(i + 1) * 64], Yi(i), Xi(i))
                        nc_ = 256
                        if m < 5:
                            nc_ = 512
                            for i in range(4):
                                self.mm(pxy[:, 256 + i * 64:256 + (i + 1) * 64], Xi(i), Yi(i))
                        self.copy(V(XYall[0:64, gi].rearrange("p a i c -> p (a i c)")[:, 0:nc_], [XYb[gi]]), pxy[:, 0:nc_],
                                  eng="act")
                    for gi in range(NG):
                        pq_ = self.PS(3 + gi % 2, 256, 64)
                        for i in range(4):
                            self.mm(pq_[:, i * 64:(i + 1) * 64], V(XYall[0:64, gi, 0, i, :], [XYb[gi]]),
                                    V(Qall[0:64, gi, i, :], [Qgb[gi]]))
                        Qf_ = V(Qall[0:64, gi].rearrange("p i c -> p (i c)"), [Qgb[gi]])
                        self.tt(Qf_, Qf_, pq_, ALU.add)
                if STOP <= 2.3:
                    continue
                for gi in range(NG + 1):
                    if gi < NG:
                        n0 = gi * 4
                        self.copy(V(pd["QT"].ap[0:64, n0:n0 + 4, :], pd["QT"].bufs), V(Qall[0:64, gi], [Qgb[gi]]), eng="act")
                        kb_ = kbg[gi % 2]
                        self.tt(kb_[0:64], V(k_tok.ap[0:64, n0:n0 + 4, :], k_tok.bufs),
                                V(s_kbg.ap[0:64, n0:n0 + 4, c:c + 1].to_broadcast([64, 4, 128]), s_kbg.bufs), ALU.mult,
                                eng="dve")
                    if gi >= 1:
                        g1 = gi - 1
                        n0 = g1 * 4
                        kb_ = kbg[g1 % 2]
                        pw = self.PS(g1 % 2, 256, 128)
                        for i in range(4):
                            n = n0 + i
                            self.mm(pw[:, i * 64:(i + 1) * 64], V(kb_.ap[0:64, i, :], kb_.bufs),
                                    V(pd["QT"].ap[0:64, n, :], pd["QT"].bufs))
                        self.ts(V(pd["nwT"].ap[:, n0:n0 + 4, :].rearrange("p i c -> p (i c)"), pd["nwT"].bufs), pw, -1.0, ALU.mult)
            self.P.barrier()
            if STOP <= 2:
                continue
            self.memset(oT, 0.0, eng="dve")
            for d in range(2):
                self.memset(perd[d]["S"], 0.0)
                self.memset(perd[d]["Sb"], 0.0)
            def prep_step(step):
                out = []
                for d in range(2):
                    n = ORD[d][step]
                    c = d * 4 + h
                    pd = perd[d]
                    sl = (step % 2) * 2 + d
                    vb, kt, vn = vbt[sl], ktt[sl], vnw[sl]
                    self.act(vb[0:64], V(v_tok.ap[0:64, n, :], v_tok.bufs), AF.Identity,
                             scale=V(beta.ap[0:64, n, c:c + 1], beta.bufs))
                    self.act(kt[0:64], V(k_tok.ap[0:64, n, :], k_tok.bufs), AF.Identity,
                             scale=V(etail.ap[0:64, n, c:c + 1], etail.bufs))
                    out.append((d, n, c, pd, vb, kt, vn))
                return out

            nxt = prep_step(0)
            for step in range(NCH):
                ctxs = nxt
                pvs, pos = {}, {}
                for (d, n, c, pd, vb, kt, vn) in ctxs:
                    pv_ = self.PS(d * 3, 128, 64)
                    self.mm(pv_, V(pd["QT"].ap[0:64, n, :], pd["QT"].bufs), vb[0:64], start=True, stop=False)
                    self.mm(pv_, V(pd["nwT"].ap[:, n, :], pd["nwT"].bufs), pd["Sb"], start=False, stop=True)
                    po_ = self.PS(d * 3 + 1 if step % 2 == 0 else 6 + d, 64, 128)
                    self.mm(po_, pd["Sb"], pd["qdT"][:, n * 64:(n + 1) * 64], start=True, stop=False)
                    pvs[d], pos[d] = pv_, po_
                for (d, n, c, pd, vb, kt, vn) in ctxs:
                    self.copy(vn[0:64], pvs[d], eng="act")
                pss = {}
                for (d, n, c, pd, vb, kt, vn) in ctxs:
                    self.mm(pos[d], vn[0:64], V(pd["AiT"].ap[0:64, n, :], pd["AiT"].bufs), start=False, stop=True)
                    ps_ = self.PS(d * 3 + 2, 128, 128)
                    self.mm(ps_, kt[0:64], vn[0:64])
                    pss[d] = ps_
                if step + 1 < NCH:
                    nxt = prep_step(step + 1)
                for (d, n, c, pd, vb, kt, vn) in ctxs:
                    self.stt(pd["Sb"], pd["S"], V(glast.ap[:, n, c:c + 1], glast.bufs), pss[d], ALU.mult, ALU.add)
                for (d, n, c, pd, vb, kt, vn) in ctxs:
                    ov = V(oT.ap[:, n * 64:(n + 1) * 64], [oTb[n]])
                    self.tt(ov, ov, pos[d], ALU.add)
                for (d, n, c, pd, vb, kt, vn) in ctxs:
                    self.stt(pd["S"], pd["S"], V(glast.ap[:, n, c:c + 1], glast.bufs), pss[d], ALU.mult, ALU.add)
            if STOP <= 3:
                continue
            def ovb_(tb):
                o, nt = TB[tb]
                return V(oT.ap[:, o:o + nt], [oTb[n] for n in range(o // 64, (o + nt) // 64)] + oT.bufs)

            def g1(tb):
                o, nt = TB[tb]
                sq = sqb[tb % 2]
                self.act(sq[:, 0:nt], ovb_(tb), AF.Square)
                pn = self.PS(6 + tb % 2, nt)
                self.mm(pn, self.ones_bf, sq[:, 0:nt])

            def g2(tb):
                o, nt = TB[tb]
                pn = self.PS(6 + tb % 2, nt)
                r_ = rr[tb % 2]
                self.act(r_[:, 0:nt], pn, AF.Ln, bias=self.eps_t, scale=1.0 / 128)
                self.act(r_[:, 0:nt], r_[:, 0:nt], AF.Exp, scale=-0.5)

            def g3(tb):
                o, nt = TB[tb]
                r_ = rr[tb % 2]
                gt_ = gts[tb % 2]
                self.stt(gt_[:, 0:nt], ovb_(tb), V(self.ggo.ap[:, l:l + 1], self.ggo.bufs), r_[:, 0:nt], ALU.mult, ALU.mult)
                go = gos[tb % 2]
                self.tt(go[:, 0:nt], gt_[:, 0:nt], szT[:, o:o + nt], ALU.mult)
                self.dma(V(self.mix_d[:, 4 + h, o:o + nt], [self.mixb[4 + h]]), go[:, 0:nt], eng="sp", sembuf=go.bufs[0])

            nb_ = len(TB)
            for i in range(nb_ + 2):
                if i < nb_:
                    g1(i)
                if 1 <= i <= nb_:
                    g2(i - 1)
                if i >= 2:
                    g3(i - 2)
            self.P.barrier()

def host_inputs(inputs, b, mixers=True):
    f = lambda a: np.ascontiguousarray(a, dtype=np.float32)
    m = {}
    m["x"] = f(inputs["x"][b])
    m["ctx"] = f(inputs["ctx"][b])
    cc = np.stack([inputs["c"][b], inputs["c_ctx"]], axis=-1)
    m["cc"] = f(cc.reshape(KC, 128, 2).transpose(1, 0, 2))
    m["w_ada"] = f(inputs["w_ada"])
    m["b_adaT"] = f(inputs["b_ada"].reshape(DEPTH, 48, 128).transpose(2, 0, 1))
    gv = np.stack([inputs["g_mix"], inputs["g_ffn"]], axis=1)
    m["gvecs"] = f(gv.reshape(DEPTH, 2, KC, 128).transpose(3, 0, 1, 2))
    m["g_finalT"] = f(inputs["g_final"].reshape(KC, 128).T)
    m["w_gate"] = f(inputs["w_gate"])
    m["w_up"] = f(inputs["w_up"])
    m["w_down"] = f(inputs["w_down"])
    m["ident"] = np.eye(128, dtype=np.float32)
    if not mixers:
        return m
    w_in = inputs["w_in"]
    m["w_in"] = f(w_in)
    idx = np.arange(32)
    a_, hf, fr = idx // 16, (idx // 8) % 2, idx % 8
    partner = a_ * 16 + (1 - hf) * 8 + fr
    wpe = np.zeros((DEPTH, D, 2, 96), np.float32)
    wpe[:, :, 0, 64:96] = w_in[:, :, 384:416]
    wpe[:, :, 1, 64:96] = w_in[:, :, 384 + partner]
    m["w_pe2"] = wpe
    wq = inputs["mla_w_q_up"]
    wq2 = np.stack([wq, wq], axis=2).astype(np.float32)
    for h in range(4):
        wq2[:, :, 1, h * 96 + 64 + idx] = wq[:, :, h * 96 + 64 + partner]
    m["w_q2"] = f(wq2)
    wkv = inputs["mla_w_kv_up"].reshape(DEPTH, 128, 4, 128)
    m["w_kv_nv"] = f(np.concatenate([wkv[..., :64].reshape(DEPTH, 128, 256), wkv[..., 64:].reshape(DEPTH, 128, 256)], -1))
    gq = inputs["mla_g_q"].reshape(DEPTH, 2, 128)
    mg = np.concatenate([gq, inputs["mla_g_kv"].reshape(DEPTH, 1, 128)], axis=1)
    m["mla_g"] = f(mg.transpose(2, 0, 1))
    m["ropeCS"] = _rope_tables()
    sel = np.zeros((128, 64), np.float32)
    sel[64, :] = 1.0
    m["sel65"] = sel
    m["nab"] = _na_tables(inputs["na_rel_bias"])
    m["w_out"] = f(inputs["w_out"])
    m["gdnc"] = _gdn_consts()
    ab = np.concatenate([inputs["dn_a_log"].reshape(DEPTH, 8), inputs["dn_dt_bias"].reshape(DEPTH, 8)], axis=1)
    m["gdn_ab"] = f(np.broadcast_to(ab[None], (128, DEPTH, 16)))
    cw = inputs["dn_conv_w"].reshape(DEPTH, 5, 3, 4, 128)
    m["gdn_cw"] = f(cw.transpose(4, 0, 2, 3, 1))
    m["gdn_go"] = f(inputs["dn_g_out"].T)
    return m


def _gdn_consts():
    if "gdn" in _CONST:
        return _CONST["gdn"]
    NEG = -30000.0
    t = np.arange(64)[:, None]
    i = np.arange(64)[None, :]
    out = np.zeros((128, 832), np.float32)
    for d in range(2):
        U = (t <= i) if d == 0 else (t >= i)
        SU = (t > i) if d == 0 else (t < i)
        Ma = (i < t) if d == 0 else (i > t)
        Mb = (t < i) if d == 0 else (t > i)
        Mc = (t <= i) if d == 0 else (t >= i)
        base = d * 320
        out[0:64, base:base + 64] = U
        out[0:64, base + 64:base + 128] = SU
        out[0:64, base + 128:base + 192] = np.where(Ma, 0.0, NEG)
        out[0:64, base + 192:base + 256] = np.where(Mb, 0.0, NEG)
        out[0:64, base + 256:base + 320] = np.where(Mc, 0.0, NEG)
    out[0:64, 640:704] = np.eye(64)
    out[0:64, 704:832] = 1.0
    _CONST["gdn"] = out
    return out


_CONST = {}


def _rope_tables():
    if "rope" in _CONST:
        return _CONST["rope"]
    t = np.arange(SEQ)
    pos = np.stack([t // 64, t % 64], axis=-1).astype(np.float32)
    inv = np.power(np.float32(10000.0), -np.arange(8, dtype=np.float32) / np.float32(8)).astype(np.float32)
    ang = (pos[:, :, None] * inv).astype(np.float32)
    cos, sin = np.cos(ang).astype(np.float32), np.sin(ang).astype(np.float32)
    tab = np.zeros((128, 2, SEQ), np.float32)
    for i in range(32):
        a_, hf, fr = i // 16, (i // 8) % 2, i % 8
        tab[64 + i, 0, :] = cos[:, a_, fr]
        tab[64 + i, 1, :] = sin[:, a_, fr] * (-1.0 if hf == 0 else 1.0)
    _CONST["rope"] = tab
    return tab


def _na_tables(rel_bias):
    NEG = np.float32(-30000.0)
    kc = np.arange(64)[:, None]
    qc = np.arange(64)[None, :]
    cs = np.clip(qc - 8, 0, 48)
    ok = (kc >= cs) & (kc < cs + 16)
    dc = np.clip(kc - qc + 15, 0, 30)
    tb = rel_bias[:, :, :, dc]
    tb = np.where(ok[None, None, None], tb, NEG).astype(np.float32)
    mask = np.full((DEPTH, 4, 64, 64), NEG, np.float32)
    tiles = []
    for dra in range(14):
        tiles.append(np.concatenate([tb[:, :, dra], tb[:, :, dra + 1]], axis=2))
    tiles.append(np.concatenate([mask, tb[:, :, 3]], axis=2))
    for dra in (4, 6, 8):
        tiles.append(np.concatenate([tb[:, :, dra], tb[:, :, dra + 1]], axis=2))
    tiles.append(np.concatenate([tb[:, :, 10], mask], axis=2))
    nab = np.stack(tiles, axis=2)
    return np.ascontiguousarray(nab.transpose(0, 3, 1, 2, 4), dtype=np.float32)


def build_nc(n_layers=DEPTH, mixers=True, dbg=None):
    nc = bass.Bass("TRN2", target_bir_lowering=False)
    es = ExitStack()
    b = Builder(nc, es, n_layers=n_layers, mixers=mixers, dbg=dbg)
    with es:
        b.build()
    return nc


def kernel(**inputs):
    nc = build_nc()
    in_maps = [host_inputs(inputs, b) for b in range(8)]
    res = run_bass_kernel_spmd(nc, in_maps, core_ids=list(range(8)))
    out = np.stack([np.asarray(r["out"], dtype=np.float32) for r in res.results], axis=0)
    return out
```
